# Optimizing a Trainium2 kernel written in Bass

```python
import math
import jax, jax.numpy as jnp
from jax import lax
import numpy as np

D_MODEL = 1024
BATCH = 8
SEQ = 8192
DEPTH = 4

GRID_W = 64
CTX_LEN = 256
N_BRANCH = 3
BRANCH_WIDTH = 512
HY_WIDTH = BRANCH_WIDTH
HY_EMB_DIM = 33
HY_BANDS = (HY_EMB_DIM - 1) // 2
HY_FILTER_WIDTH = 64
HY_DECAY_TARGET = 1e-2
HY_FAST_DECAY = 0.3
HY_SLOW_DECAY = 1.5
GQA_HEADS = 8
GQA_KV_HEADS = 2
GQA_GROUP = GQA_HEADS // GQA_KV_HEADS
GQA_HEAD_DIM = BRANCH_WIDTH // GQA_HEADS
GQA_SCALE = GQA_HEAD_DIM ** -0.5
WINDOW = 128
BLOCK = 128
MLA_HEADS = 8
MLA_Q_RANK = 384
MLA_KV_RANK = 256
MLA_NOPE_DIM = 64
MLA_ROPE_DIM = 32
MLA_V_DIM = BRANCH_WIDTH // MLA_HEADS
MLA_QK_DIM = MLA_NOPE_DIM + MLA_ROPE_DIM
MLA_SCALE = MLA_QK_DIM ** -0.5
D_FF = 4 * D_MODEL
ROPE_BASE = 10000.0
EPS = 1e-6
IN_SIZES = (3 * HY_WIDTH, GQA_HEADS * GQA_HEAD_DIM, GQA_KV_HEADS * GQA_HEAD_DIM, GQA_KV_HEADS * GQA_HEAD_DIM, MLA_Q_RANK, MLA_KV_RANK, MLA_ROPE_DIM, N_BRANCH * D_MODEL)
N_IN = 3 * HY_WIDTH + GQA_HEADS * GQA_HEAD_DIM + 2 * GQA_KV_HEADS * GQA_HEAD_DIM + MLA_Q_RANK + MLA_KV_RANK + MLA_ROPE_DIM + N_BRANCH * D_MODEL

kernel_name = 'hybrid_hyena_swa_mla_prefix_dit'


def rms_norm(x, g):
    xf = x.astype(jnp.float32)
    y = xf * lax.rsqrt(jnp.mean(xf * xf, axis=-1, keepdims=True) + EPS)
    return (y * g.astype(jnp.float32)).astype(x.dtype)


def modulate(x, shift, scale):
    return x * (1 + scale) + shift


def split_cols(p, sizes):
    offs, acc = [], 0
    for s in sizes[:-1]:
        acc += s
        offs.append(acc)
    return jnp.split(p, offs, axis=-1)


def axial_rope(rows, dim, dtype):
    n_freq = dim // 4
    inv = ROPE_BASE ** (-jnp.arange(n_freq, dtype=jnp.float32) / n_freq)
    r = jnp.repeat(jnp.arange(rows, dtype=jnp.float32), GRID_W)
    col = jnp.tile(jnp.arange(GRID_W, dtype=jnp.float32), rows)
    ang = jnp.concatenate([r[:, None] * inv, col[:, None] * inv], axis=-1)
    return jnp.cos(ang).astype(dtype), jnp.sin(ang).astype(dtype)


def apply_rope(x, cos, sin):
    x1, x2 = jnp.split(x, 2, axis=-1)
    cs, sn = cos[:, None, :], sin[:, None, :]
    return jnp.concatenate([x1 * cs - x2 * sn, x1 * sn + x2 * cs], axis=-1)


def hyena_filter(length, f1_w, f1_b, f2_w, f2_b, sin_freq, f3_w):
    f32 = jnp.float32
    t = jnp.linspace(0.0, 1.0, length, dtype=f32)[:, None]
    w = 2.0 * math.pi * jnp.arange(length, dtype=f32)[:, None] / length
    f = jnp.linspace(1e-4, HY_BANDS - 1, HY_BANDS, dtype=f32)[None, :]
    z = jnp.concatenate([t, jnp.cos(f * w), -jnp.sin(f * w)], axis=-1)
    h = jnp.sin(sin_freq[0].astype(f32) * (z @ f1_w.astype(f32) + f1_b.astype(f32)))
    h = jnp.sin(sin_freq[1].astype(f32) * (h @ f2_w.astype(f32) + f2_b.astype(f32)))
    h = (h @ f3_w.astype(f32)).reshape(length, 2, HY_WIDTH)
    max_decay = math.log(HY_DECAY_TARGET) / HY_FAST_DECAY
    min_decay = math.log(HY_DECAY_TARGET) / HY_SLOW_DECAY
    deltas = jnp.abs(jnp.linspace(min_decay, max_decay, HY_WIDTH, dtype=f32))
    h = h * jnp.exp(-t * deltas)[:, None, :]
    k = jnp.concatenate([h[:, 0], jnp.zeros((1, HY_WIDTH), f32), h[:0:-1, 1]], axis=0)
    return k * lax.rsqrt(jnp.sum(k * k, axis=0, keepdims=True) + EPS)


def hyena_mix(u, short_w, kernel, skip):
    L = u.shape[1]
    up = jnp.pad(u, ((0, 0), (1, 1), (0, 0)))
    uc = up[:, :-2] * short_w[0] + up[:, 1:-1] * short_w[1] + up[:, 2:] * short_w[2]
    x0, x1, v = jnp.split(uc, 3, axis=-1)
    z = (x1 * v).astype(jnp.float32)
    zf = jnp.fft.rfft(z, n=2 * L, axis=1)
    kf = jnp.fft.rfft(kernel, axis=0)
    y = jnp.fft.irfft(zf * kf[None], n=2 * L, axis=1)[:, :L] + z * skip.astype(jnp.float32)
    return (x0.astype(jnp.float32) * y).astype(u.dtype)


def joint_attention(q, keys, vals, masks, sink, scale):
    logits = []
    for kk, mm in zip(keys, masks):
        s = jnp.einsum('bqhgd,bkhd->bhgqk', q, kk, preferred_element_type=jnp.float32) * scale
        if mm is not None:
            s = jnp.where(mm, s, -1e30)
        logits.append(s)
    if sink is not None:
        logits.append(jnp.broadcast_to(sink.astype(jnp.float32)[None, :, :, None, None], logits[0].shape[:-1] + (1,)))
    p = jax.nn.softmax(jnp.concatenate(logits, axis=-1), axis=-1)
    out, start = 0, 0
    for vv in vals:
        n = vv.shape[1]
        out = out + jnp.einsum('bhgqk,bkhd->bqhgd', p[..., start:start + n].astype(vv.dtype), vv)
        start += n
    return out


def window_gqa_latent(q, k, v, k_ctx, v_ctx, sink):
    B, L = q.shape[0], q.shape[1]
    nb = L // BLOCK
    span = BLOCK + 2 * WINDOW
    kp = jnp.pad(k, ((0, 0), (WINDOW, WINDOW), (0, 0), (0, 0)))
    vp = jnp.pad(v, ((0, 0), (WINDOW, WINDOW), (0, 0), (0, 0)))
    qi = jnp.arange(BLOCK)[:, None]
    kj = jnp.arange(span)[None, :]
    band = (kj >= qi) & (kj <= qi + 2 * WINDOW)

    def one_block(b):
        start = b * BLOCK
        qb = lax.dynamic_slice_in_dim(q, start, BLOCK, axis=1)
        kb = lax.dynamic_slice_in_dim(kp, start, span, axis=1)
        vb = lax.dynamic_slice_in_dim(vp, start, span, axis=1)
        jpos = start - WINDOW + kj
        mask = band & (jpos >= 0) & (jpos < L)
        return joint_attention(qb, [kb, k_ctx], [vb, v_ctx], [mask, None], sink, GQA_SCALE)

    out = lax.map(one_block, jnp.arange(nb))
    return jnp.moveaxis(out, 0, 1).reshape(B, L, GQA_HEADS * GQA_HEAD_DIM)


def mla_latent(q, k, v, k_ctx, v_ctx):
    B, L = q.shape[0], q.shape[1]
    nb = L // BLOCK

    def one_block(b):
        qb = lax.dynamic_slice_in_dim(q, b * BLOCK, BLOCK, axis=1)
        return joint_attention(qb, [k, k_ctx], [v, v_ctx], [None, None], None, MLA_SCALE)

    out = lax.map(one_block, jnp.arange(nb))
    return jnp.moveaxis(out, 0, 1).reshape(B, L, MLA_HEADS * MLA_V_DIM)


def project_stream(h, w_in, gqa_qn, gqa_kn, mla_qa_n, mla_kva_n, w_q_b, w_kv_b, mla_qn, mla_kn, rope_gqa, rope_mla):
    B, L, _ = h.shape
    u, q, k, v, cq, ckv, kr, g = split_cols(h @ w_in, IN_SIZES)
    q = rms_norm(q.reshape(B, L, GQA_HEADS, GQA_HEAD_DIM), gqa_qn)
    k = rms_norm(k.reshape(B, L, GQA_KV_HEADS, GQA_HEAD_DIM), gqa_kn)
    v = v.reshape(B, L, GQA_KV_HEADS, GQA_HEAD_DIM)
    qm = (rms_norm(cq, mla_qa_n) @ w_q_b).reshape(B, L, MLA_HEADS, MLA_QK_DIM)
    kvm = (rms_norm(ckv, mla_kva_n) @ w_kv_b).reshape(B, L, MLA_HEADS, MLA_NOPE_DIM + MLA_V_DIM)
    k_nope, vm = jnp.split(kvm, [MLA_NOPE_DIM], axis=-1)
    km = jnp.concatenate([k_nope, jnp.broadcast_to(kr[:, :, None, :], (B, L, MLA_HEADS, MLA_ROPE_DIM))], axis=-1)
    qm = rms_norm(qm, mla_qn)
    km = rms_norm(km, mla_kn)
    if rope_gqa is not None:
        q = apply_rope(q, *rope_gqa)
        k = apply_rope(k, *rope_gqa)
        qm = jnp.concatenate([qm[..., :MLA_NOPE_DIM], apply_rope(qm[..., MLA_NOPE_DIM:], *rope_mla)], axis=-1)
        km = jnp.concatenate([km[..., :MLA_NOPE_DIM], apply_rope(km[..., MLA_NOPE_DIM:], *rope_mla)], axis=-1)
    gates = jax.nn.sigmoid(g.astype(jnp.float32)).astype(h.dtype).reshape(B, L, N_BRANCH, D_MODEL)
    q = q.reshape(B, L, GQA_KV_HEADS, GQA_GROUP, GQA_HEAD_DIM)
    qm = qm[:, :, :, None, :]
    return u, q, k, v, qm, km, vm, gates


def merge_branches(y_hy, y_gqa, y_mla, gates, w_branch, w_out):
    ys = (y_hy, y_gqa, y_mla)
    merged = sum(gates[:, :, n] * (ys[n] @ w_branch[n]) for n in range(N_BRANCH))
    return merged @ w_out


def squared_relu_mlp(h, w1, w2):
    return jnp.square(jax.nn.relu(h @ w1)) @ w2


def setup_inputs(seed: int = 0) -> dict:
    key = jax.random.key(seed)
    ks = iter(jax.random.split(key, 32))
    D = D_MODEL

    def nrm(shape, scale):
        return jax.random.normal(next(ks), shape, jnp.float32) * scale

    return {
        'x': nrm((BATCH, SEQ, D), 1.0),
        'c': nrm((BATCH, D), 1.0),
        'ctx': nrm((BATCH, CTX_LEN, D), 1.0),
        'c_ctx': nrm((D,), 1.0),
        'w_mod': nrm((DEPTH, D, 6 * D), 0.5 * D ** -0.5),
        'b_mod': nrm((DEPTH, 6 * D), 0.02),
        'norm_mix_g': 1.0 + nrm((DEPTH, D), 0.02),
        'norm_mlp_g': 1.0 + nrm((DEPTH, D), 0.02),
        'w_in': nrm((DEPTH, D, N_IN), D ** -0.5),
        'hy_short_w': nrm((DEPTH, 3, 3 * HY_WIDTH), 3 ** -0.5),
        'hy_f1_w': nrm((DEPTH, HY_EMB_DIM, HY_FILTER_WIDTH), HY_EMB_DIM ** -0.5),
        'hy_f1_b': nrm((DEPTH, HY_FILTER_WIDTH), 0.1),
        'hy_f2_w': nrm((DEPTH, HY_FILTER_WIDTH, HY_FILTER_WIDTH), HY_FILTER_WIDTH ** -0.5),
        'hy_f2_b': nrm((DEPTH, HY_FILTER_WIDTH), 0.1),
        'hy_sin_freq': 1.0 + nrm((DEPTH, 2, HY_FILTER_WIDTH), 0.1),
        'hy_f3_w': nrm((DEPTH, HY_FILTER_WIDTH, 2 * HY_WIDTH), HY_FILTER_WIDTH ** -0.5),
        'hy_skip': nrm((DEPTH, HY_WIDTH), 0.5),
        'gqa_q_norm': 1.0 + nrm((DEPTH, GQA_HEAD_DIM), 0.02),
        'gqa_k_norm': 1.0 + nrm((DEPTH, GQA_HEAD_DIM), 0.02),
        'gqa_sink': nrm((DEPTH, GQA_HEADS), 0.5),
        'mla_q_a_norm': 1.0 + nrm((DEPTH, MLA_Q_RANK), 0.02),
        'mla_kv_a_norm': 1.0 + nrm((DEPTH, MLA_KV_RANK), 0.02),
        'w_q_b': nrm((DEPTH, MLA_Q_RANK, MLA_HEADS * MLA_QK_DIM), MLA_Q_RANK ** -0.5),
        'w_kv_b': nrm((DEPTH, MLA_KV_RANK, MLA_HEADS * (MLA_NOPE_DIM + MLA_V_DIM)), MLA_KV_RANK ** -0.5),
        'mla_q_norm': 1.0 + nrm((DEPTH, MLA_QK_DIM), 0.02),
        'mla_k_norm': 1.0 + nrm((DEPTH, MLA_QK_DIM), 0.02),
        'w_branch': nrm((DEPTH, N_BRANCH, BRANCH_WIDTH, D), BRANCH_WIDTH ** -0.5),
        'w_out': nrm((DEPTH, D, D), D ** -0.5),
        'w_mlp1': nrm((DEPTH, D, D_FF), D ** -0.5),
        'w_mlp2': nrm((DEPTH, D_FF, D), D_FF ** -0.5),
    }


def reference(x, c, ctx, c_ctx, w_mod, b_mod, norm_mix_g, norm_mlp_g, w_in, hy_short_w, hy_f1_w, hy_f1_b, hy_f2_w, hy_f2_b, hy_sin_freq, hy_f3_w, hy_skip, gqa_q_norm, gqa_k_norm, gqa_sink, mla_q_a_norm, mla_kv_a_norm, w_q_b, w_kv_b, mla_q_norm, mla_k_norm, w_branch, w_out, w_mlp1, w_mlp2):
    B, L, _ = x.shape
    Lc = ctx.shape[1]
    rows = L // GRID_W
    rope_gqa = axial_rope(rows, GQA_HEAD_DIM, x.dtype)
    rope_mla = axial_rope(rows, MLA_ROPE_DIM, x.dtype)
    s_lat = jax.nn.silu(c)[:, None, :]
    s_ctx = jax.nn.silu(c_ctx)[None, None, :]
    for l in range(DEPTH):
        mod = jnp.split(s_lat @ w_mod[l] + b_mod[l], 6, axis=-1)
        mod_c = jnp.split(s_ctx @ w_mod[l] + b_mod[l], 6, axis=-1)
        proj_w = (w_in[l], gqa_q_norm[l], gqa_k_norm[l], mla_q_a_norm[l], mla_kv_a_norm[l], w_q_b[l], w_kv_b[l], mla_q_norm[l], mla_k_norm[l])
        filt_w = (hy_f1_w[l], hy_f1_b[l], hy_f2_w[l], hy_f2_b[l], hy_sin_freq[l], hy_f3_w[l])
        sink = gqa_sink[l].reshape(GQA_KV_HEADS, GQA_GROUP)

        hc = modulate(rms_norm(ctx, norm_mix_g[l]), mod_c[0], mod_c[1])
        u_c, q_c, k_c, v_c, qm_c, km_c, vm_c, gt_c = project_stream(hc, *proj_w, None, None)
        h = modulate(rms_norm(x, norm_mix_g[l]), mod[0], mod[1])
        u, q, k, v, qm, km, vm, gt = project_stream(h, *proj_w, rope_gqa, rope_mla)

        y_hy = hyena_mix(u, hy_short_w[l], hyena_filter(L, *filt_w), hy_skip[l])
        y_gqa = window_gqa_latent(q, k, v, k_c, v_c, sink)
        y_mla = mla_latent(qm, km, vm, km_c, vm_c)
        x = x + mod[2] * merge_branches(y_hy, y_gqa, y_mla, gt, w_branch[l], w_out[l])
        x = x + mod[5] * squared_relu_mlp(modulate(rms_norm(x, norm_mlp_g[l]), mod[3], mod[4]), w_mlp1[l], w_mlp2[l])

        if l < DEPTH - 1:
            yc_hy = hyena_mix(u_c, hy_short_w[l], hyena_filter(Lc, *filt_w), hy_skip[l])
            yc_gqa = joint_attention(q_c, [k_c], [v_c], [None], sink, GQA_SCALE).reshape(B, Lc, GQA_HEADS * GQA_HEAD_DIM)
            yc_mla = joint_attention(qm_c, [km_c], [vm_c], [None], None, MLA_SCALE).reshape(B, Lc, MLA_HEADS * MLA_V_DIM)
            ctx = ctx + mod_c[2] * merge_branches(yc_hy, yc_gqa, yc_mla, gt_c, w_branch[l], w_out[l])
            ctx = ctx + mod_c[5] * squared_relu_mlp(modulate(rms_norm(ctx, norm_mlp_g[l]), mod_c[3], mod_c[4]), w_mlp1[l], w_mlp2[l])
    return x
```

```python
import contextlib
import math
import numpy as np
import ml_dtypes
import concourse.bass as bass
import concourse.mybir as mybir
from concourse.bass_utils import run_bass_kernel_spmd

F32 = mybir.dt.float32
BF16 = mybir.dt.bfloat16
AF = mybir.ActivationFunctionType
ALU = mybir.AluOpType
AX = mybir.AxisListType

D = 1024
KD = 8
LC = 256
EPS = 1e-6
NIN = 6048
GQA_SCALE = 64 ** -0.5
MLA_SCALE = 96 ** -0.5
PI = math.pi


class Res:
    __slots__ = ("w", "r", "dsem", "name", "scoped")

    def __init__(self, name):
        self.w = {}
        self.r = {}
        self.dsem = None
        self.name = name
        self.scoped = False


class V:
    def __init__(self, ap, res):
        self.ap = ap
        self.res = res

    def __getitem__(self, idx):
        return V(self.ap[idx], self.res)

    def rr(self, pat, **kw):
        return V(self.ap.rearrange(pat, **kw), self.res)

    def bc(self, shape):
        return V(self.ap.broadcast_to(list(shape)), self.res)

    def us(self, i):
        return V(self.ap.unsqueeze(i), self.res)

    @property
    def shape(self):
        return self.ap.shape


def _merge(d, src):
    for k, v in src.items():
        if d.get(k, 0) < v:
            d[k] = v


class KB:
    def __init__(self, nc):
        self.nc = nc
        self.es = contextlib.ExitStack()
        self.sems = []
        self.semcnt = []
        self.engs = {}
        for name, h in (("pe", nc.tensor), ("act", nc.scalar), ("dve", nc.vector),
                        ("pool", nc.gpsimd), ("sp", nc.sync)):
            si = self._newsem("e_" + name)
            self.engs[name] = dict(h=h, sem=si, waited={})
        self.rings = {}
        for q in ("sp", "pool"):
            self.rings[q] = dict(sems=[self._newsem("r%s%d" % (q, i)) for i in range(24)], nxt=0)
        self.ph = None
        self.ph_res = []
        self.recq = None
        self.uid = 0
        self.ninstr = 0

    def _newsem(self, name):
        h = self.es.enter_context(self.nc.semaphore(name))
        self.sems.append(h)
        self.semcnt.append(0)
        return len(self.sems) - 1

    def dram(self, name, shape, dtype, kind="Internal"):
        t = self.nc.dram_tensor(name, list(shape), dtype, kind=kind)
        return V(t.ap(), Res(name))

    def _stack(self):
        return self.ph if self.ph is not None else self.es

    def sb(self, name, shape, dtype):
        self.uid += 1
        t = self._stack().enter_context(self.nc.sbuf_tensor("%s_%d" % (name, self.uid), list(shape), dtype))
        r = Res(name)
        if self.ph is not None:
            r.scoped = True
            self.ph_res.append(r)
        return V(t.ap(), r)

    def ps(self, name, shape, dtype):
        self.uid += 1
        t = self._stack().enter_context(self.nc.psum_tensor("%s_%d" % (name, self.uid), list(shape), dtype))
        r = Res(name)
        if self.ph is not None:
            r.scoped = True
            self.ph_res.append(r)
        return V(t.ap(), r)

    @contextlib.contextmanager
    def phase(self):
        if self.ph is not None:
            yield
            return
        self.ph = contextlib.ExitStack()
        self.ph_res = []
        try:
            yield
            self.barrier()
        finally:
            st = self.ph
            self.ph = None
            st.close()

    def _wait(self, eng, deps):
        E = self.engs[eng]
        for sem, val in deps.items():
            if eng == "pe" and sem == E["sem"]:
                continue
            if E["waited"].get(sem, 0) >= val:
                continue
            E["h"].wait_ge(self.sems[sem], val)
            E["waited"][sem] = val
            self.ninstr += 1

    def _deps(self, reads, writes, nowaw=False):
        deps = {}
        for v in reads:
            _merge(deps, v.res.w)
        for v in writes:
            if not nowaw:
                _merge(deps, v.res.w)
            _merge(deps, v.res.r)
        return deps

    def _record(self, ev, reads, writes, nowaw=False):
        sem, val = ev
        for v in reads:
            if v.res.r.get(sem, 0) < val:
                v.res.r[sem] = val
        for v in writes:
            if nowaw:
                if v.res.w.get(sem, 0) < val:
                    v.res.w[sem] = val
            else:
                v.res.w = {sem: val}
            v.res.r = {}

    @contextlib.contextmanager
    def record(self):
        prev = self.recq
        lst = []
        self.recq = lst
        try:
            yield lst
        finally:
            self.recq = prev

    def play(self, lst, n=None):
        n = len(lst) if n is None else min(n, len(lst))
        prev = self.recq
        self.recq = None
        for _ in range(n):
            e = lst.pop(0)
            if e[0] == "op":
                self.op(*e[1:])
            else:
                self.dma(*e[1:])
        self.recq = prev

    def play_interleaved(self, lists):
        lists = [l for l in lists if l]
        tot = [len(l) for l in lists]
        done = [0] * len(lists)
        while any(lists):
            bi, bv = -1, 2.0
            for i, l in enumerate(lists):
                if l:
                    v = done[i] / tot[i]
                    if v < bv:
                        bi, bv = i, v
            self.play(lists[bi], 1)
            done[bi] += 1

    def op(self, eng, fn, reads, writes):
        if self.recq is not None:
            self.recq.append(("op", eng, fn, reads, writes))
            return
        self._wait(eng, self._deps(reads, writes))
        ins = fn()
        E = self.engs[eng]
        self.semcnt[E["sem"]] += 1
        ins.then_inc(self.sems[E["sem"]], 1)
        self.ninstr += 1
        self._record((E["sem"], self.semcnt[E["sem"]]), reads, writes)

    def dma(self, out, in_, q="sp", nowaw=False):
        if self.recq is not None:
            self.recq.append(("dma", out, in_, q, nowaw))
            return
        ring = self.rings[q]
        ds = ring["sems"][ring["nxt"] % len(ring["sems"])]
        ring["nxt"] += 1
        deps = self._deps([in_], [out], nowaw)
        if self.semcnt[ds] > 0 and deps.get(ds, 0) < self.semcnt[ds]:
            deps[ds] = self.semcnt[ds]
        self._wait(q, deps)
        with self.nc.allow_non_contiguous_dma("layout"):
            ins = self.engs[q]["h"].dma_start(out=out.ap, in_=in_.ap)
        self.semcnt[ds] += 16
        ins.then_inc(self.sems[ds], 16)
        self.ninstr += 1
        self._record((ds, self.semcnt[ds]), [in_], [out], nowaw)

    def barrier(self):
        allev = {i: c for i, c in enumerate(self.semcnt) if c > 0}
        for eng in self.engs:
            self._wait(eng, allev)

    def mm(self, out, lhsT, rhs, start=True, stop=True):
        self.op("pe", lambda: self.nc.tensor.matmul(out.ap, lhsT.ap, rhs.ap, start=start, stop=stop),
                [lhsT, rhs], [out])

    def tr(self, out, in_, ident):
        self.op("pe", lambda: self.nc.tensor.transpose(out.ap, in_.ap, ident.ap), [in_, ident], [out])

    def act(self, out, in_, func, bias=None, scale=None):
        reads = [in_]
        kw = {}
        if bias is not None:
            if isinstance(bias, V):
                reads.append(bias)
                kw["bias"] = bias.ap
            else:
                kw["bias"] = bias
        if scale is not None:
            if isinstance(scale, V):
                reads.append(scale)
                kw["scale"] = scale.ap
            else:
                kw["scale"] = scale
        self.op("act", lambda: self.nc.scalar.activation(out.ap, in_.ap, func, **kw), reads, [out])

    def tt(self, out, a, b, op, eng="dve"):
        h = self.engs[eng]["h"]
        self.op(eng, lambda: h.tensor_tensor(out.ap, a.ap, b.ap, op), [a, b], [out])

    def ts(self, out, a, s1, s2, op0, op1=None, eng="dve"):
        h = self.engs[eng]["h"]
        reads = [a]
        x1 = s1
        x2 = s2
        if isinstance(s1, V):
            reads.append(s1)
            x1 = s1.ap
        if isinstance(s2, V):
            reads.append(s2)
            x2 = s2.ap
        if op1 is None:
            self.op(eng, lambda: h.tensor_scalar(out.ap, a.ap, x1, x2, op0), reads, [out])
        else:
            self.op(eng, lambda: h.tensor_scalar(out.ap, a.ap, x1, x2, op0, op1), reads, [out])

    def stt(self, out, a, s, b, op0, op1):
        reads = [a, b]
        x = s
        if isinstance(s, V):
            reads.append(s)
            x = s.ap
        self.op("dve", lambda: self.nc.vector.scalar_tensor_tensor(out.ap, a.ap, x, b.ap, op0, op1), reads, [out])

    def cp(self, out, in_, eng="dve"):
        if eng == "act":
            self.op("act", lambda: self.nc.scalar.copy(out.ap, in_.ap), [in_], [out])
        else:
            h = self.engs[eng]["h"]
            self.op(eng, lambda: h.tensor_copy(out.ap, in_.ap), [in_], [out])

    def red(self, out, in_, op=ALU.add):
        self.op("dve", lambda: self.nc.vector.tensor_reduce(out.ap, in_.ap, AX.X, op), [in_], [out])

    def recip(self, out, in_):
        self.op("dve", lambda: self.nc.vector.reciprocal(out.ap, in_.ap), [in_], [out])

    def memset(self, out, val, eng="dve"):
        h = self.engs[eng]["h"]
        self.op(eng, lambda: h.memset(out.ap, val), [], [out])

    def rstd(self, out, ss, scale, tmp):
        self.act(tmp, ss, AF.Sqrt, bias=self.epsb[0:ss.shape[0], :], scale=scale)
        self.recip(out, tmp)


def bf(a):
    return np.ascontiguousarray(a.astype(np.float32)).astype(ml_dtypes.bfloat16)


def make_consts(L):
    Ltot = L + LC
    N = 2 * L
    P = N // 128
    c = {}

    def axial(dim):
        nf = dim // 4
        inv = (10000.0 ** (-np.arange(nf, dtype=np.float32) / nf)).astype(np.float32)
        rows = L // 64
        r = np.repeat(np.arange(rows, dtype=np.float32), 64)
        col = np.tile(np.arange(64, dtype=np.float32), rows)
        ang = np.concatenate([r[:, None] * inv, col[:, None] * inv], -1).astype(np.float32)
        return np.cos(ang).astype(np.float32), np.sin(ang).astype(np.float32)

    rope = np.zeros((Ltot, 192), np.float32)
    cq, sq = axial(64)
    cm, sm = axial(32)
    rope[:L, 0:64] = np.concatenate([cq, cq], -1)
    rope[:L, 64:128] = np.concatenate([-sq, sq], -1)
    rope[:L, 128:160] = np.concatenate([cm, cm], -1)
    rope[:L, 160:192] = np.concatenate([-sm, sm], -1)
    rope[L:, 0:64] = 1.0
    rope[L:, 128:160] = 1.0
    c["rope"] = rope
    j = np.arange(128)[:, None]
    qi = np.arange(128)[None, :]
    lo = (j >= qi).astype(np.float32)
    hi = (j <= qi).astype(np.float32)
    c["masks"] = bf(np.stack([np.tile(lo, (1, 4)), np.tile(hi, (1, 4))], 1))
    c["ident"] = bf(np.eye(128))
    c["identf"] = np.eye(128, dtype=np.float32)
    c["onesf"] = np.ones((128, 128), np.float32)
    sel = np.zeros((65, 64), np.float32)
    sel[64, :] = 1.0
    c["sel"] = sel
    n1 = np.arange(P, dtype=np.float64)
    a1 = 2 * np.pi * np.outer(n1, n1) / P
    c["F1"] = bf(np.concatenate([np.cos(a1), -np.sin(a1)], 1))
    n2 = np.arange(128, dtype=np.float64)
    atw = 2 * np.pi * np.outer(n2, n1) / N
    c["TW"] = np.stack([np.cos(atw), -np.sin(atw)], 1).astype(np.float32)
    a2 = 2 * np.pi * np.outer(n2, n2) / 128
    C2, S2 = np.cos(a2), np.sin(a2)
    c["C2"] = bf(C2)
    c["S2"] = bf(S2)
    c["nS2"] = bf(-S2)
    c["RA1"] = bf(np.concatenate([C2, S2], 1))
    c["RA2"] = bf(np.concatenate([-S2, C2], 1))
    c["TWI"] = np.stack([np.cos(atw.T), np.sin(atw.T)], 1).astype(np.float32)
    c["C1"] = bf(np.cos(a1)[:, :P // 2])
    c["nS1"] = bf(-np.sin(a1)[:, :P // 2])

    def zemb(length, pos):
        t = (pos / (length - 1)).astype(np.float64)
        w = 2 * np.pi * pos / length
        f = np.linspace(1e-4, 15, 16)
        return np.concatenate([t[None, :], np.cos(f[:, None] * w[None, :]), -np.sin(f[:, None] * w[None, :])], 0), t

    posL = np.concatenate([np.arange(L), [0], np.arange(L - 1, 0, -1)]).astype(np.float64)
    zl, tl = zemb(L, posL)
    tl = tl.copy()
    tl[L] = 1e4
    c["zembL"] = zl.astype(np.float32)
    c["tKL"] = tl.astype(np.float32)[None, :]
    posC = np.concatenate([np.arange(LC), [0], np.arange(LC - 1, 0, -1)]).astype(np.float64)
    zc, tc = zemb(LC, posC)
    tc = tc.copy()
    tc[LC] = 1e4
    c["zembC"] = zc.astype(np.float32)
    c["ntKC"] = np.ascontiguousarray((-tc).astype(np.float32).reshape(4, 128).T)
    maxd = math.log(1e-2) / 0.3
    mind = math.log(1e-2) / 1.5
    deltas = np.abs(np.linspace(mind, maxd, 512)).astype(np.float32)
    c["ndelta"] = np.ascontiguousarray((-deltas).reshape(4, 128).T)
    c["deltarow"] = deltas[None, :]
    nn = np.arange(512, dtype=np.float64)
    ac = 2 * np.pi * np.outer(nn, nn) / 512
    c["Cc"] = bf(np.cos(ac).reshape(4, 128, 512).transpose(1, 0, 2))
    c["nSc"] = bf((-np.sin(ac)).reshape(4, 128, 512).transpose(1, 0, 2))
    return c


CONST_DT = {"masks": BF16, "ident": BF16, "F1": BF16, "C2": BF16, "S2": BF16, "nS2": BF16, "RA1": BF16,
            "RA2": BF16, "C1": BF16, "nS1": BF16, "Cc": BF16, "nSc": BF16}

W_SHAPES = {
    "w_mod": (D, 6 * D), "b_mod": (1, 6 * D), "norm_mix_g": (1, D), "norm_mlp_g": (1, D), "w_in": (D, NIN),
    "hy_short_w": (3, 1536), "hy_f1_w": (33, 64), "hy_f1_b": (64, 1), "hy_f2_w": (64, 64), "hy_f2_b": (64, 1),
    "hy_sin_freq": (2, 64), "hy_f3_w": (64, 1024), "hy_skip": (1, 512), "gqa_q_norm": (1, 64),
    "gqa_k_norm": (1, 64), "gqa_sink": (1, 8), "mla_q_a_norm": (1, 384), "mla_kv_a_norm": (1, 256),
    "w_q_b": (384, 768), "w_kv_b": (256, 1024), "mla_q_norm": (1, 96), "mla_k_norm": (1, 96),
    "w_branch": (1536, D), "w_out": (D, D), "w_mlp1": (D, 4 * D), "w_mlp2": (4 * D, D),
}


def build(L, DEPTH, dbg=()):
    Ltot = L + LC
    NT = L // 128
    NTt = NT + 2
    N = 2 * L
    P = N // 128
    PH = P // 2
    nc = bass.Bass("TRN2", target_bir_lowering=False)
    k = KB(nc)
    consts = make_consts(L)

    x_in = k.dram("x", [L, D], F32, "ExternalInput")
    ctx_in = k.dram("ctx", [LC, D], F32, "ExternalInput")
    c_in = k.dram("c", [D, 1], F32, "ExternalInput")
    cc_in = k.dram("c_ctx", [D, 1], F32, "ExternalInput")
    W = {}
    for nm, shp in W_SHAPES.items():
        W[nm] = k.dram(nm, [DEPTH] + list(shp), F32, "ExternalInput")
    C = {}
    for nm, arr in consts.items():
        C[nm] = k.dram("k_" + nm, list(arr.shape), CONST_DT.get(nm, F32), "ExternalInput")
    y_out = k.dram("y", [L, D], F32, "ExternalOutput")

    def scr(name, shape, dt):
        return k.dram(name, shape, dt, "ExternalOutput" if name in dbg else "Internal")

    xsA = scr("xsA", [Ltot, D], F32)
    xsB = scr("xsB", [Ltot, D], F32)
    modv = scr("modv", [2, 2, 6, 128, D], F32)
    zT = scr("zT", [512, Ltot], BF16)
    x0T = scr("x0T", [512, Ltot], F32)
    gts = scr("gts", [Ltot, 3072], BF16)
    qT = scr("qT", [512, Ltot], BF16)
    kT = scr("kT", [128, Ltot], BF16)
    vA = scr("vA", [Ltot, 130], BF16)
    qmT = scr("qmT", [768, Ltot], BF16)
    kmT = scr("kmT", [768, Ltot], BF16)
    vmA = scr("vmA", [Ltot, 520], BF16)
    yconv = scr("yconv", [NTt, 512, 128], F32)
    ygT = scr("ygT", [512, Ltot], BF16)
    ymT = scr("ymT", [512, Ltot], BF16)
    kfil = scr("kfil", [512, N], BF16)
    KfL = scr("KfL", [2, 128, 512 * P], F32)
    KfC = scr("KfC", [2, 512, 512], F32)

    ident = k.sb("ident", [128, 128], BF16)
    identf = k.sb("identf", [128, 128], F32)
    onesf = k.sb("onesf", [128, 128], F32)
    k.epsb = k.sb("epsb", [128, 1], F32)
    sLat = k.sb("sLat", [128, KD * 128], BF16)
    sCtx = k.sb("sCtx", [128, KD * 128], BF16)
    rnlk = k.sb("rnlk", [128, 512], F32)
    k.dma(ident, C["ident"])
    k.dma(identf, C["identf"])
    k.dma(onesf, C["onesf"])
    k.memset(k.epsb, EPS)

    k.dma(xsA[0:L, :], x_in, nowaw=True)
    k.dma(xsA[L:Ltot, :], ctx_in, nowaw=True)
    with k.phase():
        ccol = k.sb("ccol", [128, 2 * KD], F32)
        scol = k.sb("scol", [128, 2 * KD], F32)
        onesb = k.sb("onesb", [128, 128], BF16)
        with nc.allow_non_contiguous_dma("tiny column load"):
            k.dma(ccol[:, 0:KD], c_in.rr("(k p) o -> p (k o)", p=128), nowaw=True)
            k.dma(ccol[:, KD:2 * KD], cc_in.rr("(k p) o -> p (k o)", p=128), nowaw=True)
        k.act(scol, ccol, AF.Silu)
        k.memset(onesb, 1.0)
        for kk in range(KD):
            k.ts(sLat[:, kk * 128:(kk + 1) * 128], onesb, scol[:, kk:kk + 1], None, ALU.mult)
            k.ts(sCtx[:, kk * 128:(kk + 1) * 128], onesb, scol[:, KD + kk:KD + kk + 1], None, ALU.mult)

    def bcast_row(dst, src_row):
        k.dma(dst, src_row.bc([128, src_row.shape[1]]))

    def seq_of(i):
        return 0 if i < NT else 1

    def is_first(i):
        return i == 0 or i == NT

    def is_last(i):
        return i == NT - 1 or i == NTt - 1

    def phase_mod(l):
        with k.phase():
            gm = k.sb("gm", [128, D], F32)
            gl = k.sb("gl", [128, D], F32)
            bcast_row(gm, W["norm_mix_g"][l])
            bcast_row(gl, W["norm_mlp_g"][l])
            wt = [k.sb("wmod%d" % i, [128, KD * 512], BF16) for i in range(2)]
            bm = [k.sb("bm%d" % i, [128, 512], F32) for i in range(2)]
            ps = [k.ps("psmod%d" % i, [128, 512], F32) for i in range(2)]
            ot = [k.sb("omod%d" % i, [128, 512], F32) for i in range(2)]
            tmp = k.sb("tmod", [128, 512], F32)
            wm = W["w_mod"][l].rr("(k p) n -> p k n", p=128)
            n = 0
            for j in range(12):
                w_ = wt[j % 2]
                k.dma(w_.rr("p (k n) -> p k n", k=KD), wm[:, :, j * 512:(j + 1) * 512], q="pool")
                b_ = bm[j % 2]
                bcast_row(b_, W["b_mod"][l][:, j * 512:(j + 1) * 512])
                vi, half = j // 2, j % 2
                hs = slice(half * 512, (half + 1) * 512)
                for st, sv in ((0, sLat), (1, sCtx)):
                    p_ = ps[n % 2]
                    o_ = ot[n % 2]
                    n += 1
                    for kk in range(KD):
                        k.mm(p_, sv[:, kk * 128:(kk + 1) * 128], w_[:, kk * 512:(kk + 1) * 512],
                             start=(kk == 0), stop=(kk == KD - 1))
                    if vi in (1, 4):
                        g_ = gm if vi == 1 else gl
                        k.stt(tmp, p_, 1.0, b_, ALU.add, ALU.add)
                        k.tt(o_, tmp, g_[:, hs], ALU.mult)
                    else:
                        k.tt(o_, p_, b_, ALU.add)
                    k.dma(modv[l % 2, st, vi, :, hs], o_, q="pool", nowaw=True)

    def norm_mod_T(xt, G, SH, hb, psT, hT_dst, junk, ss, t1, rs):
        k.act(junk, xt, AF.Square)
        k.red(ss, junk)
        k.rstd(rs, ss, 1.0 / D, t1)
        k.stt(junk, xt, rs[:, 0:1], G, ALU.mult, ALU.mult)
        k.tt(hb, junk, SH, ALU.add)
        for kk in range(KD):
            k.tr(psT[:, kk, :], hb[:, kk * 128:(kk + 1) * 128], ident)
        k.cp(hT_dst, psT, eng="act")

    def phase_proj(l, xs):
        with k.phase():
            win = k.sb("win", [128, KD * NIN], BF16)
            winv = win.rr("p (k n) -> p k n", k=KD)
            for kk in range(KD):
                k.dma(winv[:, kk, :], W["w_in"][l][kk * 128:(kk + 1) * 128, :], q="pool", nowaw=True)
            wqb = k.sb("wqb", [128, 3 * 768], BF16)
            k.dma(wqb.rr("p (k n) -> p k n", k=3), W["w_q_b"][l].rr("(k p) n -> p k n", p=128), q="pool")
            wkvb = k.sb("wkvb", [128, 2 * 1024], BF16)
            k.dma(wkvb.rr("p (k n) -> p k n", k=2), W["w_kv_b"][l].rr("(k p) n -> p k n", p=128), q="pool")
            G1 = k.sb("G1", [128, D], F32)
            SH1 = k.sb("SH1", [128, D], F32)
            gqk = k.sb("gqk", [128, 10 * 64], F32)
            gqkv = gqk.rr("p (h d) -> p h d", d=64)
            g64 = k.sb("g64", [128, 128], F32)
            bcast_row(g64[:, 0:64], W["gqa_q_norm"][l])
            bcast_row(g64[:, 64:128], W["gqa_k_norm"][l])
            for h in range(8):
                k.ts(gqkv[:, h, :], g64[:, 0:64], GQA_SCALE, None, ALU.mult)
            for h in range(2):
                k.cp(gqkv[:, 8 + h, :], g64[:, 64:128])
            gqa = k.sb("gqa", [128, 384], F32)
            gkva = k.sb("gkva", [128, 256], F32)
            mqn = k.sb("mqn", [128, 96], F32)
            mkn = k.sb("mkn", [128, 96], F32)
            bcast_row(gqa, W["mla_q_a_norm"][l])
            bcast_row(gkva, W["mla_kv_a_norm"][l])
            bcast_row(mqn, W["mla_q_norm"][l])
            bcast_row(mkn, W["mla_k_norm"][l])
            k.ts(mqn, mqn, MLA_SCALE, None, ALU.mult)
            sw = k.sb("sw", [128, 36], F32)
            with nc.allow_non_contiguous_dma("tiny column load"):
                k.dma(sw.rr("p (t c) -> p t c", t=3), W["hy_short_w"][l].rr("t (c p) -> p t c", p=128))
            hT = [k.sb("hT%d" % i, [128, KD * 130], BF16) for i in range(3)]
            hTv = [h_.rr("p (k t) -> p k t", k=KD) for h_ in hT]
            xt = [k.sb("xt%d" % i, [128, D], F32) for i in range(2)]
            rp = [k.sb("rp%d" % i, [128, 192], F32) for i in range(2)]
            junk = k.sb("junk", [128, D], F32)
            hb = k.sb("hb", [128, D], BF16)
            ss = k.sb("ss", [128, 16], F32)
            t1 = k.sb("t1", [128, 16], F32)
            rs = k.sb("rs", [128, 16], F32)
            psT = k.ps("psT", [128, KD, 128], BF16)
            psT2 = k.ps("psT2", [128, KD, 128], BF16)
            psT3 = k.ps("psT3", [128, KD, 128], BF16)
            NPA = 4
            psC2 = k.ps("psC2", [128, 512], F32)
            psA = [k.ps("psA%d" % i, [128, 512], F32) for i in range(NPA)]
            ssA = k.sb("ssA", [128, 1], F32)
            t1A = k.sb("t1A", [128, 1], F32)
            rsA = k.sb("rsA", [128, 1], F32)
            ss2 = k.sb("ss2", [128, 16], F32)
            t12 = k.sb("t12", [128, 16], F32)
            rs2 = k.sb("rs2", [128, 16], F32)
            sq2 = k.sb("sq2", [128, 768], F32)
            ucT = k.sb("ucT", [128, 12 * 128], F32)
            ucv = ucT.rr("p (c t) -> p c t", c=12)
            tmpu = k.sb("tmpu", [128, 128], F32)
            zt = k.sb("zt", [128, 4 * 128], BF16)
            pBs = [k.sb("pB%d" % i, [128, 1440], F32) for i in range(2)]
            gsb = [k.sb("gsb%d" % i, [128, 1024], BF16) for i in range(2)]
            sq = k.sb("sq", [128, 640], F32)
            qkn = k.sb("qkn", [128, 640], F32)
            qkA = sq
            qkB = k.sb("qkB", [128, 640], F32)
            qkb = k.sb("qkb", [128, 640], BF16)
            qkT = k.sb("qkT", [128, 5 * 128], BF16)
            vaug = k.sb("vaug", [128, 130], BF16)
            k.memset(vaug, 1.0)
            cqb = k.sb("cqb", [128, 640], BF16)
            cT = k.sb("cT", [128, 5 * 128], BF16)
            qmS = k.sb("qmS", [128, 768], F32)
            kvS = k.sb("kvS", [128, 1024], F32)
            qmn = k.sb("qmn", [128, 768], F32)
            qmb = k.sb("qmb", [128, 768], BF16)
            kmb = k.sb("kmb", [128, 768], BF16)
            kmn = qmn[:, 0:512]
            rA = k.sb("rA", [128, 256], F32)
            rB = k.sb("rB", [128, 256], F32)
            krg = k.sb("krg", [128, 32], F32)
            krr = k.sb("krr", [128, 32], F32)
            vmaug = k.sb("vmaug", [128, 520], BF16)
            k.memset(vmaug, 1.0)
            mT = k.sb("mT", [128, 16 * 128], BF16)
            npa = [0]

            def nextps():
                p_ = psA[npa[0] % NPA]
                npa[0] += 1
                return p_

            def stageA(i):
                s = seq_of(i)
                if is_first(i):
                    k.dma(G1, modv[l % 2, s, 1])
                    k.dma(SH1, modv[l % 2, s, 0])
                slot = i % 3
                x_ = xt[i % 2]
                k.dma(x_, xs[i * 128:(i + 1) * 128, :])
                norm_mod_T(x_, G1, SH1, hb, psT, hTv[slot][:, :, 1:129], junk, ssA, t1A, rsA)
                if is_first(i):
                    k.memset(hTv[slot][:, :, 0:1], 0.0)
                else:
                    k.cp(hTv[(i - 1) % 3][:, :, 129:130], hTv[slot][:, :, 1:2])
                if is_last(i):
                    k.memset(hTv[slot][:, :, 129:130], 0.0)
                else:
                    k.cp(hTv[(i + 1) % 3][:, :, 0:1], hTv[slot][:, :, 128:129])

            def stageB(i):
                pB = pBs[i % 2]
                slot = i % 3
                hv = hTv[slot]
                cols = slice(i * 128, (i + 1) * 128)
                r_ = rp[i % 2]
                k.dma(r_, C["rope"][i * 128:(i + 1) * 128, :])
                for b4 in range(4):
                    p_ = nextps()
                    for r3 in range(3):
                        cc = b4 * 3 + r3
                        for kk in range(KD):
                            k.mm(p_[:, r3 * 130:(r3 + 1) * 130], winv[:, kk, cc * 128:(cc + 1) * 128],
                                 hv[:, kk, 0:130], start=(kk == 0), stop=(kk == KD - 1))
                    for r3 in range(3):
                        cc = b4 * 3 + r3
                        u_ = p_[:, r3 * 130:(r3 + 1) * 130]
                        k.ts(tmpu, u_[:, 0:128], sw[:, cc:cc + 1], None, ALU.mult)
                        k.stt(tmpu, u_[:, 1:129], sw[:, 12 + cc:13 + cc], tmpu, ALU.mult, ALU.add)
                        k.stt(ucv[:, cc, :], u_[:, 2:130], sw[:, 24 + cc:25 + cc], tmpu, ALU.mult, ALU.add)
                k.tt(zt, ucT[:, 512:1024], ucT[:, 1024:1536], ALU.mult)
                k.dma(zT.rr("(k p) n -> p k n", p=128)[:, :, cols], zt.rr("p (k t) -> p k t", k=4), q="pool",
                      nowaw=True)
                k.dma(x0T.rr("(k p) n -> p k n", p=128)[:, :, cols], ucT[:, 0:512].rr("p (k t) -> p k t", k=4),
                      q="pool", nowaw=True)
                for (o0, wd) in ((0, 512), (512, 512), (1024, 416)):
                    p_ = nextps()
                    for kk in range(KD):
                        k.mm(p_[:, 0:wd], hv[:, kk, 1:129], winv[:, kk, 1536 + o0:1536 + o0 + wd],
                             start=(kk == 0), stop=(kk == KD - 1))
                    k.cp(pB[:, o0:o0 + wd], p_[:, 0:wd], eng="act")
                for gc in range(6):
                    g_ = gsb[(gc // 2) % 2]
                    p_ = nextps()
                    for kk in range(KD):
                        k.mm(p_, hv[:, kk, 1:129], winv[:, kk, 2976 + gc * 512:2976 + (gc + 1) * 512],
                             start=(kk == 0), stop=(kk == KD - 1))
                    k.act(g_[:, (gc % 2) * 512:(gc % 2 + 1) * 512], p_, AF.Sigmoid)
                    if gc % 2 == 1:
                        k.dma(gts[i * 128:(i + 1) * 128, (gc // 2) * 1024:(gc // 2 + 1) * 1024], g_, q="pool",
                              nowaw=True)
            def chain1(i):
                pB = pBs[i % 2]
                cols = slice(i * 128, (i + 1) * 128)
                r_ = rp[i % 2]
                psT = psT2
                k.act(sq[:, 0:640], pB[:, 0:640], AF.Square)
                k.red(ss[:, 0:10], sq[:, 0:640].rr("p (h d) -> p h d", d=64))
                k.rstd(rs[:, 0:10], ss[:, 0:10], 1.0 / 64, t1[:, 0:10])
                qv = qkn.rr("p (h d) -> p h d", d=64)
                k.tt(qv, pB[:, 0:640].rr("p (h d) -> p h d", d=64), rs[:, 0:10].us(2).bc([128, 10, 64]), ALU.mult)
                k.tt(qkn, qkn, gqk, ALU.mult)
                Av = qkA.rr("p (h d) -> p h d", d=64)
                Bv = qkB.rr("p (h d) -> p h d", d=64)
                k.tt(Av, qv, r_[:, 0:64].us(1).bc([128, 10, 64]), ALU.mult)
                k.tt(Bv[:, :, 0:32], qv[:, :, 32:64], r_[:, 64:96].us(1).bc([128, 10, 32]), ALU.mult)
                k.tt(Bv[:, :, 32:64], qv[:, :, 0:32], r_[:, 96:128].us(1).bc([128, 10, 32]), ALU.mult)
                k.tt(qkb, qkA, qkB, ALU.add)
                for t in range(5):
                    k.tr(psT[:, t, :], qkb[:, t * 128:(t + 1) * 128], ident)
                k.cp(qkT.rr("p (k t) -> p k t", k=5), psT[:, 0:5, :], eng="act")
                k.dma(qT.rr("(k p) n -> p k n", p=128)[:, :, cols], qkT[:, 0:512].rr("p (k t) -> p k t", k=4),
                      q="pool", nowaw=True)
                k.dma(kT[:, cols], qkT[:, 512:640], q="pool", nowaw=True)
                k.cp(vaug.rr("p (g f) -> p g f", g=2)[:, :, 0:64], pB[:, 640:768].rr("p (g f) -> p g f", g=2))
                k.dma(vA[i * 128:(i + 1) * 128, :], vaug, q="pool", nowaw=True)
            def chain2(i):
                pB = pBs[i % 2]
                cols = slice(i * 128, (i + 1) * 128)
                r_ = rp[i % 2]
                psT = psT3
                sq, ss, rs, t1 = sq2, ss2, rs2, t12
                k.act(sq[:, 0:640], pB[:, 768:1408], AF.Square)
                k.red(ss[:, 10:11], sq[:, 0:384])
                k.red(ss[:, 11:12], sq[:, 384:640])
                k.rstd(rs[:, 10:11], ss[:, 10:11], 1.0 / 384, t1[:, 10:11])
                k.rstd(rs[:, 11:12], ss[:, 11:12], 1.0 / 256, t1[:, 11:12])
                k.stt(cqb[:, 0:384], pB[:, 768:1152], rs[:, 10:11], gqa, ALU.mult, ALU.mult)
                k.stt(cqb[:, 384:640], pB[:, 1152:1408], rs[:, 11:12], gkva, ALU.mult, ALU.mult)
                for t in range(5):
                    k.tr(psT[:, t, :], cqb[:, t * 128:(t + 1) * 128], ident)
                k.cp(cT.rr("p (k t) -> p k t", k=5), psT[:, 0:5, :], eng="act")
                for (o0, wd) in ((0, 512), (512, 256)):
                    p_ = psC2
                    for kk in range(3):
                        k.mm(p_[:, 0:wd], cT[:, kk * 128:(kk + 1) * 128], wqb[:, kk * 768 + o0:kk * 768 + o0 + wd],
                             start=(kk == 0), stop=(kk == 2))
                    k.cp(qmS[:, o0:o0 + wd], p_[:, 0:wd], eng="act")
                for o0 in (0, 512):
                    p_ = psC2
                    for kk in range(2):
                        k.mm(p_, cT[:, (3 + kk) * 128:(4 + kk) * 128], wkvb[:, kk * 1024 + o0:kk * 1024 + o0 + 512],
                             start=(kk == 0), stop=(kk == 1))
                    k.cp(kvS[:, o0:o0 + 512], p_, eng="act")
                k.act(sq[:, 0:768], qmS, AF.Square)
                k.red(ss[:, 0:8], sq[:, 0:768].rr("p (h d) -> p h d", d=96))
                k.rstd(rs[:, 0:8], ss[:, 0:8], 1.0 / 96, t1[:, 0:8])
                qmv = qmn.rr("p (h d) -> p h d", d=96)
                qbv = qmb.rr("p (h d) -> p h d", d=96)
                k.tt(qmv, qmS.rr("p (h d) -> p h d", d=96), rs[:, 0:8].us(2).bc([128, 8, 96]), ALU.mult)
                k.tt(qmv, qmv, mqn.us(1).bc([128, 8, 96]), ALU.mult)
                k.cp(qbv[:, :, 0:64], qmv[:, :, 0:64])
                rAv = rA.rr("p (h d) -> p h d", d=32)
                rBv = rB.rr("p (h d) -> p h d", d=32)
                k.tt(rAv, qmv[:, :, 64:96], r_[:, 128:160].us(1).bc([128, 8, 32]), ALU.mult)
                k.tt(rBv[:, :, 0:16], qmv[:, :, 80:96], r_[:, 160:176].us(1).bc([128, 8, 16]), ALU.mult)
                k.tt(rBv[:, :, 16:32], qmv[:, :, 64:80], r_[:, 176:192].us(1).bc([128, 8, 16]), ALU.mult)
                k.tt(qbv[:, :, 64:96], rAv, rBv, ALU.add)
                kvv = kvS.rr("p (h d) -> p h d", d=128)
                k.act(sq[:, 0:512].rr("p (h d) -> p h d", d=64), kvv[:, :, 0:64], AF.Square)
                k.red(ss[:, 0:8], sq[:, 0:512].rr("p (h d) -> p h d", d=64))
                k.act(sq[:, 512:544], pB[:, 1408:1440], AF.Square)
                k.red(ss[:, 8:9], sq[:, 512:544])
                k.ts(ss[:, 0:8], ss[:, 0:8], ss[:, 8:9], None, ALU.add)
                k.rstd(rs[:, 0:8], ss[:, 0:8], 1.0 / 96, t1[:, 0:8])
                knv = kmn.rr("p (h d) -> p h d", d=64)
                kbv = kmb.rr("p (h d) -> p h d", d=96)
                k.tt(knv, kvv[:, :, 0:64], rs[:, 0:8].us(2).bc([128, 8, 64]), ALU.mult)
                k.tt(kbv[:, :, 0:64], knv, mkn[:, 0:64].us(1).bc([128, 8, 64]), ALU.mult)
                k.tt(krg, pB[:, 1408:1440], mkn[:, 64:96], ALU.mult)
                k.tt(rA[:, 0:32], krg, r_[:, 128:160], ALU.mult)
                k.tt(rB[:, 0:16], krg[:, 16:32], r_[:, 160:176], ALU.mult)
                k.tt(rB[:, 16:32], krg[:, 0:16], r_[:, 176:192], ALU.mult)
                k.tt(krr, rA[:, 0:32], rB[:, 0:32], ALU.add)
                k.tt(kbv[:, :, 64:96], krr.us(1).bc([128, 8, 32]), rs[:, 0:8].us(2).bc([128, 8, 32]), ALU.mult)
                k.cp(vmaug.rr("p (h f) -> p h f", h=8)[:, :, 0:64], kvv[:, :, 64:128])
                k.dma(vmA[i * 128:(i + 1) * 128, :], vmaug, q="pool", nowaw=True)
                mTv = mT.rr("p (k t) -> p k t", k=16)
                for h in range(8):
                    k.tr(psT[0:96, h, :], qmb[:, h * 96:(h + 1) * 96], ident)
                k.cp(mTv[0:96, 0:8, :], psT[0:96, :, :], eng="act")
                for h in range(8):
                    k.tr(psT[0:96, h, :], kmb[:, h * 96:(h + 1) * 96], ident)
                k.cp(mTv[0:96, 8:16, :], psT[0:96, :, :], eng="act")
                k.dma(qmT.rr("(h d) n -> d h n", d=96)[:, :, cols], mTv[0:96, 0:8, :], q="pool", nowaw=True)
                k.dma(kmT.rr("(h d) n -> d h n", d=96)[:, :, cols], mTv[0:96, 8:16, :], q="pool", nowaw=True)

            stageA(0)
            prev = []
            for i in range(1, NTt + 1):
                if i < NTt:
                    stageA(i)
                with k.record() as r1:
                    stageB(i - 1)
                k.play_interleaved([r1] + prev)
                with k.record() as c1:
                    chain1(i - 1)
                with k.record() as c2:
                    chain2(i - 1)
                prev = [c1, c2]
            k.play_interleaved(prev)

    def sin_wrapped(dst, ps_in, bcol, fcol, a, m, npart, n):
        k.ts(a[0:npart, 0:n], ps_in, bcol, fcol, ALU.add, ALU.mult)
        k.ts(m[0:npart, 0:n], a[0:npart, 0:n], PI, None, ALU.is_gt)
        k.stt(dst, m[0:npart, 0:n], -2 * PI, a[0:npart, 0:n], ALU.mult, ALU.add)
        k.ts(m[0:npart, 0:n], a[0:npart, 0:n], -PI, None, ALU.is_lt)
        k.stt(dst, m[0:npart, 0:n], 2 * PI, dst, ALU.mult, ALU.add)
        k.act(dst, dst, AF.Sin)

    def fft_fwd(F1s, TWs, C2s, S2s, nS2s, src, nrows, ncb, consume, psS1=None, psX=None, consume_part=None):
        QW = 4 * P
        xin = [k.sb("xin%d" % i, [128, ncb * 128], BF16) for i in range(2)]
        if psS1 is None:
            psS1 = [k.ps("psS1_%d" % i, [128, 2 * 2 * P], F32) for i in range(2)]
        if psX is None:
            psX = [k.ps("psX%d" % i, [128, QW], F32) for i in range(2)]
        m1s = [k.sb("m1_%d" % i, [128, 2 * 2 * P], F32) for i in range(2)]
        m2s = [k.sb("m2_%d" % i, [128, 2 * 2 * P], F32) for i in range(2)]
        Bri = [k.sb("Bri%d" % i, [128, 2 * QW], BF16) for i in range(2)]
        nq = 0
        for cb in range(512 // ncb):
            xi = xin[cb % 2]
            xv = xi.rr("p (c t) -> p c t", c=ncb)
            with nc.allow_non_contiguous_dma("fft gather"):
                k.dma(xv[0:nrows], src[cb * ncb:(cb + 1) * ncb, 0:nrows * 128].rr("c (a t) -> a c t", t=128))
            for q4 in range(ncb // 4):
                B = Bri[nq % 2]
                Bv = B.rr("p (r c k) -> p r c k", r=2, c=4)
                for half in range(2):
                    p_ = psS1[half % len(psS1)][:, 0:2 * 2 * P]
                    pv = p_.rr("p (c r k) -> p c r k", c=2, r=2)
                    for ci in range(2):
                        cl = q4 * 4 + half * 2 + ci
                        k.mm(p_[:, ci * 2 * P:(ci + 1) * 2 * P], xv[0:nrows, cl, :], F1s[0:nrows, :])
                    m1v = m1s[half].rr("p (c r k) -> p c r k", c=2, r=2)
                    m2v = m2s[half].rr("p (c r k) -> p c r k", c=2, r=2)
                    k.tt(m1v, pv, TWs[:, 0, :].us(1).us(1).bc([128, 2, 2, P]), ALU.mult)
                    k.tt(m2v, pv, TWs[:, 1, :].us(1).us(1).bc([128, 2, 2, P]), ALU.mult)
                    k.tt(Bv[:, 0, half * 2:half * 2 + 2, :], m1v[:, :, 0, :], m2v[:, :, 1, :], ALU.subtract,
                         eng="pool")
                    k.tt(Bv[:, 1, half * 2:half * 2 + 2, :], m2v[:, :, 0, :], m1v[:, :, 1, :], ALU.add, eng="pool")
                Br = B[:, 0:QW]
                Bi = B[:, QW:2 * QW]
                if len(psX) == 2:
                    k.mm(psX[0][:, 0:QW], C2s, Br, start=True, stop=False)
                    k.mm(psX[0][:, 0:QW], S2s, Bi, start=False, stop=True)
                    k.mm(psX[1][:, 0:QW], C2s, Bi, start=True, stop=False)
                    k.mm(psX[1][:, 0:QW], nS2s, Br, start=False, stop=True)
                    consume(cb * (ncb // 4) + q4, psX[0][:, 0:QW], psX[1][:, 0:QW])
                else:
                    k.mm(psX[0][:, 0:QW], C2s, Br, start=True, stop=False)
                    k.mm(psX[0][:, 0:QW], S2s, Bi, start=False, stop=True)
                    consume_part(cb * (ncb // 4) + q4, 0, psX[0][:, 0:QW])
                    k.mm(psX[0][:, 0:QW], C2s, Bi, start=True, stop=False)
                    k.mm(psX[0][:, 0:QW], nS2s, Br, start=False, stop=True)
                    consume_part(cb * (ncb // 4) + q4, 1, psX[0][:, 0:QW])
                nq += 1

    def phase_filt(l, bg=None):
        with k.phase():
            f1w = k.sb("f1w", [33, 64], F32)
            f2w = k.sb("f2w", [64, 64], F32)
            f3w = k.sb("f3w", [64, 1024], F32)
            b1 = k.sb("b1", [64, 1], F32)
            b2 = k.sb("b2", [64, 1], F32)
            fq = k.sb("fq", [64, 2], F32)
            k.dma(f1w, W["hy_f1_w"][l])
            k.dma(f2w, W["hy_f2_w"][l])
            k.dma(f3w, W["hy_f3_w"][l])
            k.dma(b1, W["hy_f1_b"][l])
            k.dma(b2, W["hy_f2_b"][l])
            with nc.allow_non_contiguous_dma("tiny column load"):
                k.dma(fq, W["hy_sin_freq"][l].rr("t j -> j t"))
            ndl = k.sb("ndl", [128, 4], F32)
            k.dma(ndl, C["ndelta"])
            ze = [k.sb("ze%d" % i, [33, 512], F32) for i in range(2)]
            tr_ = [k.sb("trow%d" % i, [128, 512], F32) for i in range(2)]
            a_ = k.sb("fa", [128, 512], F32)
            m_ = k.sb("fm", [128, 512], F32)
            h1 = k.sb("h1", [64, 512], F32)
            h2 = k.sb("h2", [64, 512], F32)
            dec = k.sb("dec", [128, 512], F32)
            kf32 = k.sb("kf32", [128, 512], F32)
            kb16 = [k.sb("kb16_%d" % i, [128, 4 * 512], BF16) for i in range(2)]
            sqj = k.sb("sqj", [128, 512], F32)
            NCH = N // 512
            ssall = k.sb("ssall", [128, 4 * NCH], F32)
            ssv = ssall.rr("p (c n) -> p c n", c=4)
            if bg is None:
                psf = [k.ps("psf%d" % i, [128, 512], F32) for i in range(4)]
            else:
                psf = [bg[0], bg[0], bg[1], bg[1]]
            for ch in range(NCH):
                z_ = ze[ch % 2]
                t_ = tr_[ch % 2]
                k.dma(z_, C["zembL"][:, ch * 512:(ch + 1) * 512])
                bcast_row(t_, C["tKL"][:, ch * 512:(ch + 1) * 512])
                k.mm(psf[0][0:64, :], f1w, z_)
                sin_wrapped(h1, psf[0][0:64, :], b1[:, 0:1], fq[:, 0:1], a_, m_, 64, 512)
                k.mm(psf[1][0:64, :], f2w, h1)
                sin_wrapped(h2, psf[1][0:64, :], b2[:, 0:1], fq[:, 1:2], a_, m_, 64, 512)
                dr = 0 if ch * 512 < L else 1
                kb_ = kb16[ch % 2]
                for cc in range(4):
                    p_ = psf[2 + cc % 2]
                    k.mm(p_, f3w[:, dr * 512 + cc * 128:dr * 512 + (cc + 1) * 128], h2)
                    k.act(dec, t_, AF.Exp, scale=ndl[:, cc:cc + 1])
                    k.tt(kf32, p_, dec, ALU.mult)
                    k.act(sqj, kf32, AF.Square)
                    k.red(ssv[:, cc, ch:ch + 1], sqj)
                    k.cp(kb_[:, cc * 512:(cc + 1) * 512], kf32)
                k.dma(kfil.rr("(k p) n -> p k n", p=128)[:, :, ch * 512:(ch + 1) * 512],
                      kb_.rr("p (k n) -> p k n", k=4), q="pool", nowaw=True)
            sscol = k.sb("sscol", [128, 4], F32)
            k.red(sscol, ssv)
            dg = k.sb("dg", [128, 128], F32)
            rnl = k.sb("rnl", [128, 512], F32)
            rt = k.sb("rt", [128, 512], F32)
            for cc in range(4):
                k.ts(dg, identf, sscol[:, cc:cc + 1], None, ALU.mult)
                k.mm(psf[0][:, cc * 128:(cc + 1) * 128], onesf, dg)
            k.act(rt, psf[0], AF.Sqrt, bias=k.epsb, scale=1.0)
            k.recip(rnl, rt)
            k.ts(rnl, rnl, 1.0 / N, None, ALU.mult)
            k.cp(rnlk, rnl)
        with k.phase():
            F1s = k.sb("F1s", [P, 2 * P], BF16)
            TWs = k.sb("TWs", [128, 2, P], F32)
            C2s = k.sb("C2s", [128, 128], BF16)
            S2s = k.sb("S2s", [128, 128], BF16)
            nS2s = k.sb("nS2s", [128, 128], BF16)
            k.dma(F1s, C["F1"])
            k.dma(TWs, C["TW"])
            k.dma(C2s, C["C2"])
            k.dma(S2s, C["S2"])
            k.dma(nS2s, C["nS2"])
            ko = [k.sb("ko%d" % i, [128, 2 * 4 * P], F32) for i in range(2)]

            def consume_part(q, r, X):
                o = ko[q % 2]
                ov = o.rr("p (r c k) -> p r c k", r=2, c=4)
                rb = rnlk[:, q * 4:(q + 1) * 4].us(2).bc([128, 4, P])
                k.tt(ov[:, r], X.rr("p (c k) -> p c k", c=4), rb, ALU.mult)
                if r == 1:
                    k.dma(KfL.rr("r p m -> p r m")[:, :, q * 4 * P:(q + 1) * 4 * P], o.rr("p (r m) -> p r m", r=2),
                          q="pool", nowaw=True)

            def consume(q, Xr, Xi):
                consume_part(q, 0, Xr)
                consume_part(q, 1, Xi)

            if bg is None:
                fft_fwd(F1s, TWs, C2s, S2s, nS2s, kfil, P, 64, consume)
            else:
                fft_fwd(F1s, TWs, C2s, S2s, nS2s, kfil, P, 32, consume, psS1=[bg[0]], psX=[bg[1]],
                        consume_part=consume_part)
        with k.phase():
            f1w = k.sb("f1w", [33, 64], F32)
            f2w = k.sb("f2w", [64, 64], F32)
            f3w = k.sb("f3w", [64, 1024], F32)
            b1 = k.sb("b1", [64, 1], F32)
            b2 = k.sb("b2", [64, 1], F32)
            fq = k.sb("fq", [64, 2], F32)
            k.dma(f1w, W["hy_f1_w"][l])
            k.dma(f2w, W["hy_f2_w"][l])
            k.dma(f3w, W["hy_f3_w"][l])
            k.dma(b1, W["hy_f1_b"][l])
            k.dma(b2, W["hy_f2_b"][l])
            with nc.allow_non_contiguous_dma("tiny column load"):
                k.dma(fq, W["hy_sin_freq"][l].rr("t j -> j t"))
            ntk = k.sb("ntk", [128, 4], F32)
            k.dma(ntk, C["ntKC"])
            drow = k.sb("drow", [128, 512], F32)
            bcast_row(drow, C["deltarow"])
            Ccs = k.sb("Ccs", [128, 4 * 512], BF16)
            nScs = k.sb("nScs", [128, 4 * 512], BF16)
            k.dma(Ccs.rr("p (k n) -> p k n", k=4), C["Cc"])
            k.dma(nScs.rr("p (k n) -> p k n", k=4), C["nSc"])
            ze = k.sb("ze", [33, 512], F32)
            k.dma(ze, C["zembC"])
            a_ = k.sb("fa", [128, 512], F32)
            m_ = k.sb("fm", [128, 512], F32)
            h1 = k.sb("h1", [64, 512], F32)
            h2 = k.sb("h2", [64, 512], F32)
            if bg is None:
                psf = [k.ps("psf%d" % i, [128, 512], F32) for i in range(4)]
            else:
                psf = [bg[0], bg[0], bg[1], bg[1]]
            k.mm(psf[0][0:64, :], f1w, ze)
            sin_wrapped(h1, psf[0][0:64, :], b1[:, 0:1], fq[:, 0:1], a_, m_, 64, 512)
            k.mm(psf[1][0:64, :], f2w, h1)
            sin_wrapped(h2, psf[1][0:64, :], b2[:, 0:1], fq[:, 1:2], a_, m_, 64, 512)
            dec = k.sb("dec", [128, 512], F32)
            kf32 = k.sb("kf32", [128, 4 * 512], F32)
            ksq = k.sb("ksq", [128, 4 * 512], F32)
            kb16 = k.sb("kb16", [128, 4 * 512], BF16)
            for nch in range(4):
                dr = 0 if nch < 2 else 1
                p_ = psf[2 + nch % 2]
                k.mm(p_, h2[:, nch * 128:(nch + 1) * 128], f3w[:, dr * 512:(dr + 1) * 512])
                k.act(dec, drow, AF.Exp, scale=ntk[:, nch:nch + 1])
                k.tt(kf32[:, nch * 512:(nch + 1) * 512], p_, dec, ALU.mult)
            k.act(ksq, kf32, AF.Square)
            k.cp(kb16, kf32)
            for nch in range(4):
                k.mm(psf[0], onesf, ksq[:, nch * 512:(nch + 1) * 512], start=(nch == 0), stop=(nch == 3))
            rt = k.sb("rt", [128, 512], F32)
            rnc = k.sb("rnc", [128, 512], F32)
            k.act(rt, psf[0], AF.Sqrt, bias=k.epsb, scale=1.0)
            k.recip(rnc, rt)
            k.ts(rnc, rnc, 1.0 / 512, None, ALU.mult)
            ko = [k.sb("ko%d" % i, [128, 512], F32) for i in range(2)]
            n = 0
            for kc in range(4):
                for r, tab in ((0, Ccs), (1, nScs)):
                    p_ = psf[2 + n % 2]
                    o = ko[n % 2]
                    n += 1
                    for nch in range(4):
                        k.mm(p_, tab[:, nch * 512 + kc * 128:nch * 512 + (kc + 1) * 128],
                             kb16[:, nch * 512:(nch + 1) * 512], start=(nch == 0), stop=(nch == 3))
                    k.tt(o, p_, rnc, ALU.mult)
                    k.dma(KfC[r, kc * 128:(kc + 1) * 128, :], o, q="pool", nowaw=True)

    def phase_hyena(l):
        with k.phase():
            F1s = k.sb("F1s", [P, 2 * P], BF16)
            TWs = k.sb("TWs", [128, 2, P], F32)
            C2s = k.sb("C2s", [128, 128], BF16)
            S2s = k.sb("S2s", [128, 128], BF16)
            nS2s = k.sb("nS2s", [128, 128], BF16)
            RA1 = k.sb("RA1", [128, 256], BF16)
            RA2 = k.sb("RA2", [128, 256], BF16)
            TWI = k.sb("TWI", [P, 2, 128], F32)
            C1s = k.sb("C1s", [P, PH], BF16)
            nS1s = k.sb("nS1s", [P, PH], BF16)
            for t_, nm in ((F1s, "F1"), (TWs, "TW"), (C2s, "C2"), (S2s, "S2"), (nS2s, "nS2"), (RA1, "RA1"),
                           (RA2, "RA2"), (TWI, "TWI"), (C1s, "C1"), (nS1s, "nS1")):
                k.dma(t_, C[nm])
            QW = 4 * P
            kfq = [k.sb("kfq%d" % i, [128, 2 * QW], F32) for i in range(2)]
            ta = k.sb("ta", [128, QW], F32)
            tb = k.sb("tb", [128, QW], F32)
            ta2 = k.sb("ta2", [128, QW], F32)
            tb2 = k.sb("tb2", [128, QW], F32)
            Yri = [k.sb("Yri%d" % i, [128, 2 * QW], BF16) for i in range(2)]
            psC = [k.ps("psC%d" % i, [128, 512], F32) for i in range(2)]
            psY = k.ps("psY", [128, 512], F32)
            n1s = [k.sb("n1_%d" % i, [128, 512], F32) for i in range(2)]
            n2s = [k.sb("n2_%d" % i, [128, 512], F32) for i in range(2)]
            Dri = [k.sb("Dri%d" % i, [128, 2 * 512], BF16) for i in range(2)]
            yo = [k.sb("yo%d" % i, [128, 512], F32) for i in range(2)]

            def consume(q, Xr, Xi):
                kq = kfq[q % 2]
                k.dma(kq.rr("p (r m) -> p r m", r=2), KfL.rr("r p m -> p r m")[:, :, q * QW:(q + 1) * QW])
                Kr = kq[:, 0:QW]
                Ki = kq[:, QW:2 * QW]
                Y = Yri[q % 2]
                k.tt(ta, Xr, Kr, ALU.mult)
                k.tt(tb, Xi, Ki, ALU.mult)
                k.tt(Y[:, 0:QW], ta, tb, ALU.subtract, eng="pool")
                k.tt(ta2, Xr, Ki, ALU.mult)
                k.tt(tb2, Xi, Kr, ALU.mult)
                k.tt(Y[:, QW:2 * QW], ta2, tb2, ALU.add, eng="pool")
                Dt = Dri[q % 2]
                Dv = Dt.rr("p (r c t) -> p r c t", r=2, c=4)
                for half in range(2):
                    p_ = psC[half]
                    for ci in range(2):
                        cl = half * 2 + ci
                        k.mm(p_[0:P, ci * 256:(ci + 1) * 256], Y[:, cl * P:(cl + 1) * P], RA1, start=True, stop=False)
                        k.mm(p_[0:P, ci * 256:(ci + 1) * 256], Y[:, QW + cl * P:QW + (cl + 1) * P], RA2,
                             start=False, stop=True)
                    pv = p_[0:P, :].rr("p (c r t) -> p c r t", c=2, r=2)
                    n1v = n1s[half][0:P, :].rr("p (c r t) -> p c r t", c=2, r=2)
                    n2v = n2s[half][0:P, :].rr("p (c r t) -> p c r t", c=2, r=2)
                    k.tt(n1v, pv, TWI[:, 0, :].us(1).us(1).bc([P, 2, 2, 128]), ALU.mult)
                    k.tt(n2v, pv, TWI[:, 1, :].us(1).us(1).bc([P, 2, 2, 128]), ALU.mult)
                    k.tt(Dv[0:P, 0, half * 2:half * 2 + 2, :], n1v[:, :, 0, :], n2v[:, :, 1, :], ALU.subtract,
                         eng="pool")
                    k.tt(Dv[0:P, 1, half * 2:half * 2 + 2, :], n2v[:, :, 0, :], n1v[:, :, 1, :], ALU.add,
                         eng="pool")
                k.mm(psY[0:PH, :], C1s, Dt[0:P, 0:512], start=True, stop=False)
                k.mm(psY[0:PH, :], nS1s, Dt[0:P, 512:1024], start=False, stop=True)
                o = yo[q % 2]
                k.cp(o[0:PH, :], psY[0:PH, :], eng="act")
                k.dma(yconv[0:PH, q * 4:(q + 1) * 4, :], o[0:PH, :].rr("p (c t) -> p c t", c=4), q="sp", nowaw=True)

            fft_fwd(F1s, TWs, C2s, S2s, nS2s, zT, PH, 64, consume)
        with k.phase():
            Ccs = k.sb("Ccs", [128, 4 * 512], BF16)
            nScs = k.sb("nScs", [128, 4 * 512], BF16)
            k.dma(Ccs.rr("p (k n) -> p k n", k=4), C["Cc"])
            k.dma(nScs.rr("p (k n) -> p k n", k=4), C["nSc"])
            kfc = k.sb("kfc", [128, 2 * 4 * 512], F32)
            k.dma(kfc.rr("p (r k c) -> p r k c", r=2, k=4), KfC.rr("r (k p) c -> p r k c", p=128))
            zc = k.sb("zc", [128, 4 * 256], BF16)
            k.dma(zc.rr("p (k t) -> p k t", k=4), zT.rr("(k p) n -> p k n", p=128)[:, :, L:Ltot])
            psT = k.ps("psTc", [128, 8, 128], BF16)
            for t in range(2):
                for cc in range(4):
                    k.tr(psT[:, t * 4 + cc, :], zc[:, cc * 256 + t * 128:cc * 256 + (t + 1) * 128], ident)
            ztm = k.sb("ztm", [128, 2 * 512], BF16)
            k.cp(ztm.rr("p (a t) -> p a t", a=8), psT, eng="act")
            psZ = [k.ps("psZ%d" % i, [128, 512], F32) for i in range(2)]
            ta = k.sb("ta", [128, 512], F32)
            tb = k.sb("tb", [128, 512], F32)
            Yr = k.sb("Yr", [128, 4 * 512], BF16)
            Yi = k.sb("Yi", [128, 4 * 512], BF16)
            for kc in range(4):
                for r, tab in ((0, Ccs), (1, nScs)):
                    for t in range(2):
                        k.mm(psZ[r], tab[:, t * 512 + kc * 128:t * 512 + (kc + 1) * 128], ztm[:, t * 512:(t + 1) * 512],
                             start=(t == 0), stop=(t == 1))
                Kr = kfc[:, kc * 512:(kc + 1) * 512]
                Ki = kfc[:, 2048 + kc * 512:2048 + (kc + 1) * 512]
                k.tt(ta, psZ[0], Kr, ALU.mult)
                k.tt(tb, psZ[1], Ki, ALU.mult)
                k.tt(Yr[:, kc * 512:(kc + 1) * 512], ta, tb, ALU.subtract)
                k.tt(ta, psZ[0], Ki, ALU.mult)
                k.tt(tb, psZ[1], Kr, ALU.mult)
                k.tt(Yi[:, kc * 512:(kc + 1) * 512], ta, tb, ALU.add)
            yo = k.sb("yoc", [128, 4 * 256], F32)
            for cc in range(4):
                p_ = psZ[cc % 2]
                for kc in range(4):
                    k.mm(p_[:, 0:256], Yr[:, kc * 512 + cc * 128:kc * 512 + (cc + 1) * 128],
                         Ccs[:, kc * 512:kc * 512 + 256], start=(kc == 0), stop=False)
                    k.mm(p_[:, 0:256], Yi[:, kc * 512 + cc * 128:kc * 512 + (cc + 1) * 128],
                         nScs[:, kc * 512:kc * 512 + 256], start=False, stop=(kc == 3))
                k.cp(yo[:, cc * 256:(cc + 1) * 256], p_[:, 0:256], eng="act")
            for t in range(2):
                k.dma(yconv[NT + t].rr("(k p) n -> p k n", p=128),
                      yo.rr("p (k t n) -> p k t n", k=4, t=2)[:, :, t, :], q="pool", nowaw=True)

    def attn_finish(o_ps, npart_q, oS, sel_s, ps_bc, rec, yb, esk=None):
        nq = npart_q
        k.cp(oS[:, 0:nq], o_ps[:, 0:nq], eng="act")
        k.mm(ps_bc[0:64, 0:nq], sel_s, oS[:, 0:nq])
        if esk is not None:
            k.tt(rec[:, 0:nq].rr("p (h t) -> p h t", h=4), ps_bc[0:64, 0:nq].rr("p (h t) -> p h t", h=4),
                 esk.us(2).bc([64, 4, nq // 4]), ALU.add)
            k.recip(rec[:, 0:nq], rec[:, 0:nq])
        else:
            k.recip(rec[:, 0:nq], ps_bc[0:64, 0:nq])
        k.tt(yb[:, 0:nq], oS[0:64, 0:nq], rec[:, 0:nq], ALU.mult)

    def phase_gqa(l, bgjob=None):
        with k.phase():
            kTs = k.sb("kTs", [128, Ltot], BF16)
            k.dma(kTs, kT)
            vAs = k.sb("vAs", [128, NTt * 130], BF16)
            vAv = vAs.rr("p (t f) -> p t f", f=130)
            k.dma(vAv, vA.rr("(t p) f -> p t f", p=128))
            msk = k.sb("msk", [128, 2, 512], BF16)
            k.dma(msk, C["masks"])
            sel_s = k.sb("sel", [65, 64], F32)
            k.dma(sel_s, C["sel"])
            sk = k.sb("sk", [64, 8], F32)
            bcast_row_n(sk, W["gqa_sink"][l], 64)
            esk = k.sb("esk", [64, 8], F32)
            k.act(esk, sk, AF.Exp)
            qb = [k.sb("qb%d" % i, [128, 512], BF16) for i in range(2)]
            psS = [k.ps("psS%d" % i, [128, 512], F32) for i in range(3)]
            psO = [k.ps("psO%d" % i, [128, 512], F32) for i in range(2)]
            psB = k.ps("psB", [128, 512], F32)
            pT = [k.sb("pT%d" % i, [128, 512], BF16) for i in range(4)]
            oS = k.sb("oS", [65, 512], F32)
            rec = k.sb("rec", [64, 512], F32)
            yb = [k.sb("yb%d" % i, [64, 512], BF16) for i in range(2)]
            bgl = []
            if bgjob is not None:
                with k.record() as bgl:
                    bgjob()
            seqs = []
            for i in range(NTt):
                if i < NT:
                    keys = [(kt, (0 if kt == i - 1 else (1 if kt == i + 1 else None)))
                            for kt in (i - 1, i, i + 1) if 0 <= kt < NT]
                    keys += [(NT, None), (NT + 1, None)]
                else:
                    keys = [(NT, None), (NT + 1, None)]
                for g in range(2):
                    seqs.append((i, g, keys))
            items = [(si, j) for si, sq_ in enumerate(seqs) for j in range(len(sq_[2]))]
            loaded = set()

            def ensure_q(i):
                if i in loaded:
                    return
                loaded.add(i)
                cols = slice(i * 128, (i + 1) * 128)
                for g in range(2):
                    k.dma(qb[i % 2][g * 64:(g + 1) * 64, :].rr("d (h t) -> d h t", h=4),
                          qT[g * 256:(g + 1) * 256, :].rr("(h d) n -> d h n", d=64)[:, :, cols], nowaw=True)

            def S(t):
                si, j = items[t]
                i, g, keys = seqs[si]
                ensure_q(i)
                kt, mk = keys[j]
                pr = slice(g * 64, (g + 1) * 64)
                k.mm(psS[t % 3], kTs[pr, kt * 128:(kt + 1) * 128], qb[i % 2][pr, :])

            for t0 in range(min(3, len(items))):
                S(t0)
            for t, (si, j) in enumerate(items):
                i, g, keys = seqs[si]
                kt, mk = keys[j]
                p_ = pT[t % 4]
                k.act(p_, psS[t % 3], AF.Exp)
                if mk is not None:
                    k.tt(p_, p_, msk[:, mk, :], ALU.mult)
                if t + 3 < len(items):
                    S(t + 3)
                o_ = psO[si % 2]
                k.mm(o_[0:65, :], vAv[:, kt, g * 65:(g + 1) * 65], p_, start=(j == 0), stop=(j == len(keys) - 1))
                if bgl:
                    k.play(bgl, 1)
                if j == len(keys) - 1:
                    y_ = yb[si % 2]
                    cols = slice(i * 128, (i + 1) * 128)
                    attn_finish(o_[0:65, :], 512, oS, sel_s, psB, rec, y_, esk=esk[:, g * 4:(g + 1) * 4])
                    k.dma(ygT[g * 256:(g + 1) * 256, :].rr("(h d) n -> d h n", d=64)[:, :, cols],
                          y_.rr("d (h t) -> d h t", h=4), q="pool", nowaw=True)
            if bgl:
                k.play(bgl)

    def bcast_row_n(dst, src_row, n):
        k.dma(dst, src_row.bc([n, src_row.shape[1]]))

    def phase_mla(l, bgjob=None):
        with k.phase():
            sel_s = k.sb("sel", [65, 64], F32)
            k.dma(sel_s, C["sel"])
            vms = [k.sb("vms%d" % i, [128, NTt * 65], BF16) for i in range(2)]
            kms = [k.sb("kms0", [96, Ltot], BF16)]
            qms = [k.sb("qms0", [96, Ltot], BF16)]
            psS = [k.ps("psS%d" % i, [128, 512], F32) for i in range(3)]
            psO = [k.ps("psO%d" % i, [128, 512], F32) for i in range(2)]
            psB = k.ps("psB", [128, 512], F32)
            pT = [k.sb("pT%d" % i, [128, 512], BF16) for i in range(4)]
            oS = k.sb("oS", [65, 512], F32)
            rec = k.sb("rec", [64, 512], F32)
            yb = [k.sb("yb%d" % i, [64, 512], BF16) for i in range(2)]
            bgl = []
            if bgjob is not None:
                bgA = k.ps("bgA", [128, 512], F32)
                bgB = k.ps("bgB", [128, 512], F32)
                with k.record() as bgl:
                    bgjob((bgA, bgB))
            n = 0
            groups = [(g * 512, 512, list(range(NTt))) for g in range(L // 512)] + [(L, LC, [NT, NT + 1])]
            for h in range(8):
                km_ = kms[0]
                qm_ = qms[0]
                vm_ = vms[h % 2]
                vmv = vm_.rr("p (t f) -> p t f", f=65)
                k.dma(km_, kmT[h * 96:(h + 1) * 96, :])
                k.dma(qm_, qmT[h * 96:(h + 1) * 96, :])
                with nc.allow_non_contiguous_dma("per-head V gather"):
                    k.dma(vmv, vmA.rr("(t p) f -> p t f", p=128)[:, :, h * 65:(h + 1) * 65])
                for (q0, nq, keys) in groups:
                    o_ = psO[n % 2]
                    y_ = yb[n % 2]
                    n += 1
                    nk = len(keys)

                    def S(j):
                        kt = keys[j]
                        k.mm(psS[j % 3][:, 0:nq], km_[:, kt * 128:(kt + 1) * 128], qm_[:, q0:q0 + nq])

                    for j0 in range(min(2, nk)):
                        S(j0)
                    for j in range(nk):
                        kt = keys[j]
                        k.act(pT[j % 4][:, 0:nq], psS[j % 3][:, 0:nq], AF.Exp)
                        if j + 2 < nk:
                            S(j + 2)
                        k.mm(o_[0:65, 0:nq], vmv[:, kt, :], pT[j % 4][:, 0:nq], start=(j == 0),
                             stop=(j == nk - 1))
                        if bgl:
                            k.play(bgl, 1)
                    attn_finish(o_[0:65, :], nq, oS, sel_s, psB, rec, y_)
                    k.dma(ymT[h * 64:(h + 1) * 64, q0:q0 + nq], y_[:, 0:nq], q="pool", nowaw=True)
            if bgl:
                k.play(bgl)

    def phase_merge(l, xs, xd, only_latent):
        with k.phase():
            wbr = k.sb("wbr", [128, 12 * D], BF16)
            k.dma(wbr.rr("p (k n) -> p k n", k=12), W["w_branch"][l].rr("(k p) n -> p k n", p=128), q="pool")
            wo = k.sb("wo", [128, KD * D], BF16)
            k.dma(wo.rr("p (k n) -> p k n", k=KD), W["w_out"][l].rr("(k p) n -> p k n", p=128), q="pool")
            GT = [k.sb("GT%d" % s, [128, D], F32) for s in range(2)]
            for s in range(2):
                k.dma(GT[s], modv[l % 2, s, 2])
            skp = k.sb("skp", [128, 4], F32)
            with nc.allow_non_contiguous_dma("tiny column load"):
                k.dma(skp, W["hy_skip"][l].rr("o (k p) -> p (o k)", p=128))
            xt = [k.sb("xt%d" % i, [128, D], F32) for i in range(2)]
            yc = [k.sb("yc%d" % i, [128, 512], F32) for i in range(2)]
            zt = [k.sb("zt%d" % i, [128, 512], BF16) for i in range(2)]
            x0 = [k.sb("x0%d" % i, [128, 512], F32) for i in range(2)]
            yT3 = [k.sb("yT3_%d" % i, [128, 12 * 128], BF16) for i in range(2)]
            gt = [k.sb("gt%d" % i, [128, 3072], BF16) for i in range(2)]
            tmp = k.sb("tmpm", [128, 512], F32)
            mg = k.sb("mg", [128, D], F32)
            mg2 = k.sb("mg2", [128, D], F32)
            mgb = k.sb("mgb", [128, D], BF16)
            mTs = k.sb("mTs", [128, KD * 128], BF16)
            psT = k.ps("psT", [128, KD, 128], BF16)
            psM = [k.ps("psM%d" % i, [128, 512], F32) for i in range(6)]
            xo = [k.sb("xo%d" % i, [128, D], F32) for i in range(2)]
            npm = 0
            for i in range(NT if only_latent else NTt):
                s = seq_of(i)
                cols = slice(i * 128, (i + 1) * 128)
                b2 = i % 2
                k.dma(xt[b2], xs[i * 128:(i + 1) * 128, :])
                k.dma(yc[b2].rr("p (k t) -> p k t", k=4), yconv[i].rr("(k p) t -> p k t", p=128))
                k.dma(zt[b2].rr("p (k t) -> p k t", k=4), zT.rr("(k p) n -> p k n", p=128)[:, :, cols])
                k.dma(x0[b2].rr("p (k t) -> p k t", k=4), x0T.rr("(k p) n -> p k n", p=128)[:, :, cols])
                y3 = yT3[b2]
                y3v = y3.rr("p (k t) -> p k t", k=12)
                k.dma(y3v[:, 4:8, :], ygT.rr("(k p) n -> p k n", p=128)[:, :, cols], nowaw=True)
                k.dma(y3v[:, 8:12, :], ymT.rr("(k p) n -> p k n", p=128)[:, :, cols], nowaw=True)
                k.dma(gt[b2], gts[i * 128:(i + 1) * 128, :])
                for cc in range(4):
                    cs = slice(cc * 128, (cc + 1) * 128)
                    k.stt(tmp[:, cs], zt[b2][:, cs], skp[:, cc:cc + 1], yc[b2][:, cs], ALU.mult, ALU.add)
                k.tt(y3[:, 0:512], tmp, x0[b2], ALU.mult)
                for br in range(3):
                    for half in range(2):
                        p_ = psM[npm % 6]
                        npm += 1
                        hs = slice(half * 512, (half + 1) * 512)
                        for kk in range(4):
                            k.mm(p_, y3[:, (br * 4 + kk) * 128:(br * 4 + kk + 1) * 128],
                                 wbr[:, (br * 4 + kk) * D + half * 512:(br * 4 + kk) * D + (half + 1) * 512],
                                 start=(kk == 0), stop=(kk == 3))
                        gsl = gt[b2][:, br * D + half * 512:br * D + (half + 1) * 512]
                        if br == 0:
                            k.tt(mg[:, hs], p_, gsl, ALU.mult)
                        else:
                            k.tt(mg2[:, hs], p_, gsl, ALU.mult)
                            k.tt(mg[:, hs], mg[:, hs], mg2[:, hs], ALU.add)
                k.cp(mgb, mg, eng="act")
                for kk in range(KD):
                    k.tr(psT[:, kk, :], mgb[:, kk * 128:(kk + 1) * 128], ident)
                k.cp(mTs.rr("p (k t) -> p k t", k=KD), psT, eng="act")
                for half in range(2):
                    p_ = psM[npm % 6]
                    npm += 1
                    hs = slice(half * 512, (half + 1) * 512)
                    for kk in range(KD):
                        k.mm(p_, mTs[:, kk * 128:(kk + 1) * 128], wo[:, kk * D + half * 512:kk * D + (half + 1) * 512],
                             start=(kk == 0), stop=(kk == KD - 1))
                    k.tt(mg2[:, hs], p_, GT[s][:, hs], ALU.mult)
                    k.tt(xo[b2][:, hs], mg2[:, hs], xt[b2][:, hs], ALU.add)
                k.dma(xd[i * 128:(i + 1) * 128, :], xo[b2], q="pool", nowaw=True)

    def phase_mlp(l, xs, xd, only_latent):
        with k.phase():
            w1 = k.sb("w1", [128, KD * 4096], BF16)
            w1v = w1.rr("p (k n) -> p k n", k=KD)
            for kk in range(KD):
                k.dma(w1v[:, kk, :], W["w_mlp1"][l][kk * 128:(kk + 1) * 128, :], q="pool", nowaw=True)
            w2 = k.sb("w2", [128, 32 * D], BF16)
            w2v = w2.rr("p (k n) -> p k n", k=32)
            for q4 in range(4):
                k.dma(w2v[:, q4 * 8:(q4 + 1) * 8, :],
                      W["w_mlp2"][l][q4 * 1024:(q4 + 1) * 1024, :].rr("(k p) n -> p k n", p=128), q="pool", nowaw=True)
            G2 = k.sb("G2", [128, D], F32)
            SH2 = k.sb("SH2", [128, D], F32)
            GT2 = k.sb("GT2", [128, D], F32)
            xt = [k.sb("xt%d" % i, [128, D], F32) for i in range(4)]
            junk = k.sb("junk", [128, D], F32)
            hb = k.sb("hb", [128, D], BF16)
            ss = k.sb("ss", [128, 1], F32)
            t1 = k.sb("t1", [128, 1], F32)
            rs = k.sb("rs", [128, 1], F32)
            psT = k.ps("psT", [128, KD, 128], BF16)
            hT = k.sb("hTm", [128, KD * 256], BF16)
            hTv = hT.rr("p (k t) -> p k t", k=KD)
            aT = k.sb("aT", [128, 32 * 256], BF16)
            r1 = [k.sb("r1_%d" % i, [128, 512], F32) for i in range(2)]
            psH = [k.ps("psH%d" % i, [128, 512], F32) for i in range(4)]
            psO = [k.ps("psO%d" % i, [128, 512], F32) for i in range(2)]
            cur = -1
            ntl = NT if only_latent else NTt
            for i0 in range(0, ntl, 2):
                s = seq_of(i0)
                if s != cur:
                    cur = s
                    k.dma(G2, modv[l % 2, s, 4])
                    k.dma(SH2, modv[l % 2, s, 3])
                    k.dma(GT2, modv[l % 2, s, 5])
                for t in range(2):
                    i = i0 + t
                    k.dma(xt[i % 4], xs[i * 128:(i + 1) * 128, :])
                    norm_mod_T(xt[i % 4], G2, SH2, hb, psT, hTv[:, :, t * 128:(t + 1) * 128], junk, ss, t1, rs)
                for fb in range(16):
                    p_ = psH[fb % 4]
                    for r2 in range(2):
                        fc = fb * 2 + r2
                        for kk in range(KD):
                            k.mm(p_[:, r2 * 256:(r2 + 1) * 256], w1v[:, kk, fc * 128:(fc + 1) * 128],
                                 hTv[:, kk, :], start=(kk == 0), stop=(kk == KD - 1))
                    r_ = r1[fb % 2]
                    k.act(r_, p_, AF.Relu)
                    k.tt(aT[:, fb * 512:(fb + 1) * 512], r_, r_, ALU.mult)
                for t in range(2):
                    i = i0 + t
                    x_ = xt[i % 4]
                    for half in range(2):
                        p_ = psO[half]
                        hs = slice(half * 512, (half + 1) * 512)
                        for fc in range(32):
                            k.mm(p_, aT[:, fc * 256 + t * 128:fc * 256 + (t + 1) * 128], w2v[:, fc, hs],
                                 start=(fc == 0), stop=(fc == 31))
                        k.tt(junk[:, hs], p_, GT2[:, hs], ALU.mult)
                        k.tt(x_[:, hs], junk[:, hs], x_[:, hs], ALU.add)
                    k.dma(xd[i * 128:(i + 1) * 128, :], x_, q="pool", nowaw=True)

    for l in range(DEPTH):
        last = (l == DEPTH - 1)
        if l == 0:
            phase_mod(l)
        phase_proj(l, xsA)
        if l == 0:
            phase_filt(l)
        phase_hyena(l)
        if l + 1 < DEPTH:
            phase_gqa(l, bgjob=(lambda ll=l + 1: phase_mod(ll)))
        else:
            phase_gqa(l)
        if l + 1 < DEPTH:
            phase_mla(l, bgjob=(lambda bg, ll=l + 1: phase_filt(ll, bg)))
        else:
            phase_mla(l)
        phase_merge(l, xsA, xsB, last)
        phase_mlp(l, xsB, xsA, last)
    k.dma(y_out, xsA[0:L, :])
    k.barrier()
    k.es.close()
    return nc, consts, k


def make_in_maps(inputs, consts, L, DEPTH, B):
    maps = []
    shared = {}
    for nm, shp in W_SHAPES.items():
        a = np.asarray(inputs[nm], dtype=np.float32)
        shared[nm] = np.ascontiguousarray(a.reshape([DEPTH] + list(shp)))
    for nm, arr in consts.items():
        shared["k_" + nm] = np.ascontiguousarray(arr)
    x = np.asarray(inputs["x"], dtype=np.float32)
    c = np.asarray(inputs["c"], dtype=np.float32)
    ctx = np.asarray(inputs["ctx"], dtype=np.float32)
    cc = np.asarray(inputs["c_ctx"], dtype=np.float32)
    for b in range(B):
        m = dict(shared)
        m["x"] = np.ascontiguousarray(x[b])
        m["ctx"] = np.ascontiguousarray(ctx[b])
        m["c"] = np.ascontiguousarray(c[b].reshape(D, 1))
        m["c_ctx"] = np.ascontiguousarray(cc.reshape(D, 1))
        maps.append(m)
    return maps


def run(inputs, L, DEPTH, dbg=()):
    B = inputs["x"].shape[0]
    nc, consts, kb = build(L, DEPTH, dbg)
    maps = make_in_maps(inputs, consts, L, DEPTH, B)
    res = run_bass_kernel_spmd(nc, maps, core_ids=list(range(B)))
    return res


def kernel(**inputs):
    x = inputs["x"]
    B, L, _ = x.shape
    DEPTH = inputs["w_mod"].shape[0]
    res = run(inputs, L, DEPTH)
    return np.stack([np.asarray(r["y"], dtype=np.float32) for r in res.results], 0)
```

```python
import contextlib
import math
import numpy as np
import ml_dtypes
import concourse.bass as bass
import concourse.mybir as mybir
from concourse.bass_utils import run_bass_kernel_spmd

F32 = mybir.dt.float32
BF16 = mybir.dt.bfloat16
AF = mybir.ActivationFunctionType
ALU = mybir.AluOpType
AX = mybir.AxisListType

D = 1024
KD = 8
LC = 256
EPS = 1e-6
NIN = 6048
GQA_SCALE = 64 ** -0.5
MLA_SCALE = 96 ** -0.5
PI = math.pi


class Res:
    __slots__ = ("w", "r", "dsem", "name", "scoped")

    def __init__(self, name):
        self.w = {}
        self.r = {}
        self.dsem = None
        self.name = name
        self.scoped = False


class V:
    def __init__(self, ap, res):
        self.ap = ap
        self.res = res

    def __getitem__(self, idx):
        return V(self.ap[idx], self.res)

    def rr(self, pat, **kw):
        return V(self.ap.rearrange(pat, **kw), self.res)

    def bc(self, shape):
        return V(self.ap.broadcast_to(list(shape)), self.res)

    def us(self, i):
        return V(self.ap.unsqueeze(i), self.res)

    @property
    def shape(self):
        return self.ap.shape


def _merge(d, src):
    for k, v in src.items():
        if d.get(k, 0) < v:
            d[k] = v


class KB:
    def __init__(self, nc):
        self.nc = nc
        self.es = contextlib.ExitStack()
        self.sems = []
        self.semcnt = []
        self.engs = {}
        for name, h in (("pe", nc.tensor), ("act", nc.scalar), ("dve", nc.vector),
                        ("pool", nc.gpsimd), ("sp", nc.sync)):
            si = self._newsem("e_" + name)
            self.engs[name] = dict(h=h, sem=si, waited={})
        self.rings = {}
        for q in ("sp", "pool"):
            self.rings[q] = dict(sems=[self._newsem("r%s%d" % (q, i)) for i in range(24)], nxt=0)
        self.ph = None
        self.ph_res = []
        self.recq = None
        self.uid = 0
        self.ninstr = 0

    def _newsem(self, name):
        h = self.es.enter_context(self.nc.semaphore(name))
        self.sems.append(h)
        self.semcnt.append(0)
        return len(self.sems) - 1

    def dram(self, name, shape, dtype, kind="Internal"):
        t = self.nc.dram_tensor(name, list(shape), dtype, kind=kind)
        return V(t.ap(), Res(name))

    def _stack(self):
        return self.ph if self.ph is not None else self.es

    def sb(self, name, shape, dtype):
        self.uid += 1
        t = self._stack().enter_context(self.nc.sbuf_tensor("%s_%d" % (name, self.uid), list(shape), dtype))
        r = Res(name)
        if self.ph is not None:
            r.scoped = True
            self.ph_res.append(r)
        return V(t.ap(), r)

    def ps(self, name, shape, dtype):
        self.uid += 1
        t = self._stack().enter_context(self.nc.psum_tensor("%s_%d" % (name, self.uid), list(shape), dtype))
        r = Res(name)
        if self.ph is not None:
            r.scoped = True
            self.ph_res.append(r)
        return V(t.ap(), r)

    @contextlib.contextmanager
    def phase(self):
        if self.ph is not None:
            yield
            return
        self.ph = contextlib.ExitStack()
        self.ph_res = []
        try:
            yield
            self.barrier()
        finally:
            st = self.ph
            self.ph = None
            st.close()

    def _wait(self, eng, deps):
        E = self.engs[eng]
        for sem, val in deps.items():
            if eng == "pe" and sem == E["sem"]:
                continue
            if E["waited"].get(sem, 0) >= val:
                continue
            E["h"].wait_ge(self.sems[sem], val)
            E["waited"][sem] = val
            self.ninstr += 1

    def _deps(self, reads, writes, nowaw=False):
        deps = {}
        for v in reads:
            _merge(deps, v.res.w)
        for v in writes:
            if not nowaw:
                _merge(deps, v.res.w)
            _merge(deps, v.res.r)
        return deps

    def _record(self, ev, reads, writes, nowaw=False):
        sem, val = ev
        for v in reads:
            if v.res.r.get(sem, 0) < val:
                v.res.r[sem] = val
        for v in writes:
            if nowaw:
                if v.res.w.get(sem, 0) < val:
                    v.res.w[sem] = val
            else:
                v.res.w = {sem: val}
            v.res.r = {}

    @contextlib.contextmanager
    def record(self):
        prev = self.recq
        lst = []
        self.recq = lst
        try:
            yield lst
        finally:
            self.recq = prev

    def play(self, lst, n=None):
        n = len(lst) if n is None else min(n, len(lst))
        prev = self.recq
        if prev is not None:
            for _ in range(n):
                prev.append(lst.pop(0))
            return
        self.recq = None
        for _ in range(n):
            e = lst.pop(0)
            if e[0] == "op":
                self.op(*e[1:])
            else:
                self.dma(*e[1:])
        self.recq = prev

    def play_interleaved(self, lists):
        lists = [l for l in lists if l]
        tot = [len(l) for l in lists]
        done = [0] * len(lists)
        while any(lists):
            bi, bv = -1, 2.0
            for i, l in enumerate(lists):
                if l:
                    v = done[i] / tot[i]
                    if v < bv:
                        bi, bv = i, v
            self.play(lists[bi], 1)
            done[bi] += 1

    def op(self, eng, fn, reads, writes):
        if self.recq is not None:
            self.recq.append(("op", eng, fn, reads, writes))
            return
        self._wait(eng, self._deps(reads, writes))
        ins = fn()
        E = self.engs[eng]
        self.semcnt[E["sem"]] += 1
        ins.then_inc(self.sems[E["sem"]], 1)
        self.ninstr += 1
        self._record((E["sem"], self.semcnt[E["sem"]]), reads, writes)

    def dma(self, out, in_, q="sp", nowaw=False):
        if self.recq is not None:
            self.recq.append(("dma", out, in_, q, nowaw))
            return
        ring = self.rings[q]
        ds = ring["sems"][ring["nxt"] % len(ring["sems"])]
        ring["nxt"] += 1
        deps = self._deps([in_], [out], nowaw)
        if self.semcnt[ds] > 0 and deps.get(ds, 0) < self.semcnt[ds]:
            deps[ds] = self.semcnt[ds]
        self._wait(q, deps)
        with self.nc.allow_non_contiguous_dma("layout"):
            ins = self.engs[q]["h"].dma_start(out=out.ap, in_=in_.ap)
        self.semcnt[ds] += 16
        ins.then_inc(self.sems[ds], 16)
        self.ninstr += 1
        self._record((ds, self.semcnt[ds]), [in_], [out], nowaw)

    def barrier(self):
        allev = {i: c for i, c in enumerate(self.semcnt) if c > 0}
        for eng in self.engs:
            self._wait(eng, allev)

    def mm(self, out, lhsT, rhs, start=True, stop=True):
        self.op("pe", lambda: self.nc.tensor.matmul(out.ap, lhsT.ap, rhs.ap, start=start, stop=stop),
                [lhsT, rhs], [out])

    def tr(self, out, in_, ident):
        self.op("pe", lambda: self.nc.tensor.transpose(out.ap, in_.ap, ident.ap), [in_, ident], [out])

    def act(self, out, in_, func, bias=None, scale=None):
        reads = [in_]
        kw = {}
        if bias is not None:
            if isinstance(bias, V):
                reads.append(bias)
                kw["bias"] = bias.ap
            else:
                kw["bias"] = bias
        if scale is not None:
            if isinstance(scale, V):
                reads.append(scale)
                kw["scale"] = scale.ap
            else:
                kw["scale"] = scale
        self.op("act", lambda: self.nc.scalar.activation(out.ap, in_.ap, func, **kw), reads, [out])

    def tt(self, out, a, b, op, eng="dve"):
        h = self.engs[eng]["h"]
        self.op(eng, lambda: h.tensor_tensor(out.ap, a.ap, b.ap, op), [a, b], [out])

    def ts(self, out, a, s1, s2, op0, op1=None, eng="dve"):
        h = self.engs[eng]["h"]
        reads = [a]
        x1 = s1
        x2 = s2
        if isinstance(s1, V):
            reads.append(s1)
            x1 = s1.ap
        if isinstance(s2, V):
            reads.append(s2)
            x2 = s2.ap
        if op1 is None:
            self.op(eng, lambda: h.tensor_scalar(out.ap, a.ap, x1, x2, op0), reads, [out])
        else:
            self.op(eng, lambda: h.tensor_scalar(out.ap, a.ap, x1, x2, op0, op1), reads, [out])

    def stt(self, out, a, s, b, op0, op1):
        reads = [a, b]
        x = s
        if isinstance(s, V):
            reads.append(s)
            x = s.ap
        self.op("dve", lambda: self.nc.vector.scalar_tensor_tensor(out.ap, a.ap, x, b.ap, op0, op1), reads, [out])

    def cp(self, out, in_, eng="dve"):
        if eng == "act":
            self.op("act", lambda: self.nc.scalar.copy(out.ap, in_.ap), [in_], [out])
        else:
            h = self.engs[eng]["h"]
            self.op(eng, lambda: h.tensor_copy(out.ap, in_.ap), [in_], [out])

    def red(self, out, in_, op=ALU.add):
        self.op("dve", lambda: self.nc.vector.tensor_reduce(out.ap, in_.ap, AX.X, op), [in_], [out])

    def recip(self, out, in_):
        self.op("dve", lambda: self.nc.vector.reciprocal(out.ap, in_.ap), [in_], [out])

    def memset(self, out, val, eng="dve"):
        h = self.engs[eng]["h"]
        self.op(eng, lambda: h.memset(out.ap, val), [], [out])

    def rstd(self, out, ss, scale, tmp):
        self.act(tmp, ss, AF.Sqrt, bias=self.epsb[0:ss.shape[0], :], scale=scale)
        self.recip(out, tmp)


def bf(a):
    return np.ascontiguousarray(a.astype(np.float32)).astype(ml_dtypes.bfloat16)


def make_consts(L):
    Ltot = L + LC
    N = 2 * L
    P = N // 128
    c = {}

    def axial(dim):
        nf = dim // 4
        inv = (10000.0 ** (-np.arange(nf, dtype=np.float32) / nf)).astype(np.float32)
        rows = L // 64
        r = np.repeat(np.arange(rows, dtype=np.float32), 64)
        col = np.tile(np.arange(64, dtype=np.float32), rows)
        ang = np.concatenate([r[:, None] * inv, col[:, None] * inv], -1).astype(np.float32)
        return np.cos(ang).astype(np.float32), np.sin(ang).astype(np.float32)

    rope = np.zeros((Ltot, 192), np.float32)
    cq, sq = axial(64)
    cm, sm = axial(32)
    rope[:L, 0:64] = np.concatenate([cq, cq], -1)
    rope[:L, 64:128] = np.concatenate([-sq, sq], -1)
    rope[:L, 128:160] = np.concatenate([cm, cm], -1)
    rope[:L, 160:192] = np.concatenate([-sm, sm], -1)
    rope[L:, 0:64] = 1.0
    rope[L:, 128:160] = 1.0
    c["rope"] = rope
    j = np.arange(128)[:, None]
    qi = np.arange(128)[None, :]
    lo = (j >= qi).astype(np.float32)
    hi = (j <= qi).astype(np.float32)
    c["masks"] = bf(np.stack([np.tile(lo, (1, 4)), np.tile(hi, (1, 4))], 1))
    c["ident"] = bf(np.eye(128))
    c["identf"] = np.eye(128, dtype=np.float32)
    c["onesf"] = np.ones((128, 128), np.float32)
    sel = np.zeros((65, 64), np.float32)
    sel[64, :] = 1.0
    c["sel"] = sel
    n1 = np.arange(P, dtype=np.float64)
    a1 = 2 * np.pi * np.outer(n1, n1) / P
    c["F1"] = bf(np.concatenate([np.cos(a1), -np.sin(a1)], 1))
    n2 = np.arange(128, dtype=np.float64)
    atw = 2 * np.pi * np.outer(n2, n1) / N
    c["TW"] = np.stack([np.cos(atw), -np.sin(atw)], 1).astype(np.float32)
    a2 = 2 * np.pi * np.outer(n2, n2) / 128
    C2, S2 = np.cos(a2), np.sin(a2)
    c["C2"] = bf(C2)
    c["S2"] = bf(S2)
    c["nS2"] = bf(-S2)
    c["RA1"] = bf(np.concatenate([C2, S2], 1))
    c["RA2"] = bf(np.concatenate([-S2, C2], 1))
    c["TWI"] = np.stack([np.cos(atw.T), np.sin(atw.T)], 1).astype(np.float32)
    c["C1"] = bf(np.cos(a1)[:, :P // 2])
    c["nS1"] = bf(-np.sin(a1)[:, :P // 2])

    def zemb(length, pos):
        t = (pos / (length - 1)).astype(np.float64)
        w = 2 * np.pi * pos / length
        f = np.linspace(1e-4, 15, 16)
        return np.concatenate([t[None, :], np.cos(f[:, None] * w[None, :]), -np.sin(f[:, None] * w[None, :])], 0), t

    posL = np.concatenate([np.arange(L), [0], np.arange(L - 1, 0, -1)]).astype(np.float64)
    zl, tl = zemb(L, posL)
    tl = tl.copy()
    tl[L] = 1e4
    c["zembL"] = zl.astype(np.float32)
    c["tKL"] = tl.astype(np.float32)[None, :]
    posC = np.concatenate([np.arange(LC), [0], np.arange(LC - 1, 0, -1)]).astype(np.float64)
    zc, tc = zemb(LC, posC)
    tc = tc.copy()
    tc[LC] = 1e4
    c["zembC"] = zc.astype(np.float32)
    c["ntKC"] = np.ascontiguousarray((-tc).astype(np.float32).reshape(4, 128).T)
    maxd = math.log(1e-2) / 0.3
    mind = math.log(1e-2) / 1.5
    deltas = np.abs(np.linspace(mind, maxd, 512)).astype(np.float32)
    c["ndelta"] = np.ascontiguousarray((-deltas).reshape(4, 128).T)
    c["deltarow"] = deltas[None, :]
    nn = np.arange(512, dtype=np.float64)
    ac = 2 * np.pi * np.outer(nn, nn) / 512
    c["Cc"] = bf(np.cos(ac).reshape(4, 128, 512).transpose(1, 0, 2))
    c["nSc"] = bf((-np.sin(ac)).reshape(4, 128, 512).transpose(1, 0, 2))
    return c


CONST_DT = {"masks": BF16, "ident": BF16, "F1": BF16, "C2": BF16, "S2": BF16, "nS2": BF16, "RA1": BF16,
            "RA2": BF16, "C1": BF16, "nS1": BF16, "Cc": BF16, "nSc": BF16}

W_SHAPES = {
    "w_mod": (D, 6 * D), "b_mod": (1, 6 * D), "norm_mix_g": (1, D), "norm_mlp_g": (1, D), "w_in": (D, NIN),
    "hy_short_w": (3, 1536), "hy_f1_w": (33, 64), "hy_f1_b": (64, 1), "hy_f2_w": (64, 64), "hy_f2_b": (64, 1),
    "hy_sin_freq": (2, 64), "hy_f3_w": (64, 1024), "hy_skip": (1, 512), "gqa_q_norm": (1, 64),
    "gqa_k_norm": (1, 64), "gqa_sink": (1, 8), "mla_q_a_norm": (1, 384), "mla_kv_a_norm": (1, 256),
    "w_q_b": (384, 768), "w_kv_b": (256, 1024), "mla_q_norm": (1, 96), "mla_k_norm": (1, 96),
    "w_branch": (1536, D), "w_out": (D, D), "w_mlp1": (D, 4 * D), "w_mlp2": (4 * D, D),
}


def build(L, DEPTH, dbg=()):
    Ltot = L + LC
    NT = L // 128
    NTt = NT + 2
    N = 2 * L
    P = N // 128
    PH = P // 2
    nc = bass.Bass("TRN2", target_bir_lowering=False)
    k = KB(nc)
    consts = make_consts(L)

    x_in = k.dram("x", [L, D], F32, "ExternalInput")
    ctx_in = k.dram("ctx", [LC, D], F32, "ExternalInput")
    c_in = k.dram("c", [D, 1], F32, "ExternalInput")
    cc_in = k.dram("c_ctx", [D, 1], F32, "ExternalInput")
    W = {}
    for nm, shp in W_SHAPES.items():
        W[nm] = k.dram(nm, [DEPTH] + list(shp), F32, "ExternalInput")
    C = {}
    for nm, arr in consts.items():
        C[nm] = k.dram("k_" + nm, list(arr.shape), CONST_DT.get(nm, F32), "ExternalInput")
    y_out = k.dram("y", [L, D], F32, "ExternalOutput")

    def scr(name, shape, dt):
        return k.dram(name, shape, dt, "ExternalOutput" if name in dbg else "Internal")

    xsA = scr("xsA", [Ltot, D], F32)
    xsB = scr("xsB", [Ltot, D], F32)
    modv = scr("modv", [2, 2, 6, 128, D], F32)
    zT = scr("zT", [512, Ltot], BF16)
    x0T = scr("x0T", [512, Ltot], F32)
    gts = scr("gts", [Ltot, 3072], BF16)
    qT = scr("qT", [512, Ltot], BF16)
    kT = scr("kT", [128, Ltot], BF16)
    vA = scr("vA", [Ltot, 130], BF16)
    qmT = scr("qmT", [768, Ltot], BF16)
    kmT = scr("kmT", [768, Ltot], BF16)
    vmA = scr("vmA", [Ltot, 520], BF16)
    yconv = scr("yconv", [NTt, 512, 128], F32)
    ygT = scr("ygT", [512, Ltot], BF16)
    ymT = scr("ymT", [512, Ltot], BF16)
    kfil = scr("kfil", [512, N], BF16)
    KfL = scr("KfL", [2, 128, 512 * P], F32)
    KfC = scr("KfC", [2, 512, 512], F32)

    ident = k.sb("ident", [128, 128], BF16)
    identf = k.sb("identf", [128, 128], F32)
    onesf = k.sb("onesf", [128, 128], F32)
    k.epsb = k.sb("epsb", [128, 1], F32)
    sLat = k.sb("sLat", [128, KD * 128], BF16)
    sCtx = k.sb("sCtx", [128, KD * 128], BF16)
    rnlk = k.sb("rnlk", [128, 512], F32)
    k.dma(ident, C["ident"])
    k.dma(identf, C["identf"])
    k.dma(onesf, C["onesf"])
    k.memset(k.epsb, EPS)

    k.dma(xsA[0:L, :], x_in, nowaw=True)
    k.dma(xsA[L:Ltot, :], ctx_in, nowaw=True)
    with k.phase():
        ccol = k.sb("ccol", [128, 2 * KD], F32)
        scol = k.sb("scol", [128, 2 * KD], F32)
        onesb = k.sb("onesb", [128, 128], BF16)
        with nc.allow_non_contiguous_dma("tiny column load"):
            k.dma(ccol[:, 0:KD], c_in.rr("(k p) o -> p (k o)", p=128), nowaw=True)
            k.dma(ccol[:, KD:2 * KD], cc_in.rr("(k p) o -> p (k o)", p=128), nowaw=True)
        k.act(scol, ccol, AF.Silu)
        k.memset(onesb, 1.0)
        for kk in range(KD):
            k.ts(sLat[:, kk * 128:(kk + 1) * 128], onesb, scol[:, kk:kk + 1], None, ALU.mult)
            k.ts(sCtx[:, kk * 128:(kk + 1) * 128], onesb, scol[:, KD + kk:KD + kk + 1], None, ALU.mult)

    def bcast_row(dst, src_row):
        k.dma(dst, src_row.bc([128, src_row.shape[1]]))

    def seq_of(i):
        return 0 if i < NT else 1

    def is_first(i):
        return i == 0 or i == NT

    def is_last(i):
        return i == NT - 1 or i == NTt - 1

    def phase_mod(l):
        with k.phase():
            gm = k.sb("gm", [128, D], F32)
            gl = k.sb("gl", [128, D], F32)
            bcast_row(gm, W["norm_mix_g"][l])
            bcast_row(gl, W["norm_mlp_g"][l])
            wt = [k.sb("wmod%d" % i, [128, KD * 512], BF16) for i in range(2)]
            bm = [k.sb("bm%d" % i, [128, 512], F32) for i in range(2)]
            ps = [k.ps("psmod%d" % i, [128, 512], F32) for i in range(2)]
            ot = [k.sb("omod%d" % i, [128, 512], F32) for i in range(2)]
            tmp = k.sb("tmod", [128, 512], F32)
            wm = W["w_mod"][l].rr("(k p) n -> p k n", p=128)
            n = 0
            for j in range(12):
                w_ = wt[j % 2]
                k.dma(w_.rr("p (k n) -> p k n", k=KD), wm[:, :, j * 512:(j + 1) * 512], q="pool")
                b_ = bm[j % 2]
                bcast_row(b_, W["b_mod"][l][:, j * 512:(j + 1) * 512])
                vi, half = j // 2, j % 2
                hs = slice(half * 512, (half + 1) * 512)
                for st, sv in ((0, sLat), (1, sCtx)):
                    p_ = ps[n % 2]
                    o_ = ot[n % 2]
                    n += 1
                    for kk in range(KD):
                        k.mm(p_, sv[:, kk * 128:(kk + 1) * 128], w_[:, kk * 512:(kk + 1) * 512],
                             start=(kk == 0), stop=(kk == KD - 1))
                    if vi in (1, 4):
                        g_ = gm if vi == 1 else gl
                        k.stt(tmp, p_, 1.0, b_, ALU.add, ALU.add)
                        k.tt(o_, tmp, g_[:, hs], ALU.mult)
                    else:
                        k.tt(o_, p_, b_, ALU.add)
                    k.dma(modv[l % 2, st, vi, :, hs], o_, q="pool", nowaw=True)

    def norm_mod_T(xt, G, SH, hb, psT, hT_dst, junk, ss, t1, rs):
        k.act(junk, xt, AF.Square)
        k.red(ss, junk)
        k.rstd(rs, ss, 1.0 / D, t1)
        k.stt(junk, xt, rs[:, 0:1], G, ALU.mult, ALU.mult)
        k.tt(hb, junk, SH, ALU.add)
        for kk in range(KD):
            k.tr(psT[:, kk, :], hb[:, kk * 128:(kk + 1) * 128], ident)
        k.cp(hT_dst, psT, eng="act")

    def phase_proj(l, xs):
        with k.phase():
            win = k.sb("win", [128, KD * NIN], BF16)
            winv = win.rr("p (k n) -> p k n", k=KD)
            for kk in range(KD):
                k.dma(winv[:, kk, :], W["w_in"][l][kk * 128:(kk + 1) * 128, :], q="pool", nowaw=True)
            wqb = k.sb("wqb", [128, 3 * 768], BF16)
            k.dma(wqb.rr("p (k n) -> p k n", k=3), W["w_q_b"][l].rr("(k p) n -> p k n", p=128), q="pool")
            wkvb = k.sb("wkvb", [128, 2 * 1024], BF16)
            k.dma(wkvb.rr("p (k n) -> p k n", k=2), W["w_kv_b"][l].rr("(k p) n -> p k n", p=128), q="pool")
            G1 = k.sb("G1", [128, D], F32)
            SH1 = k.sb("SH1", [128, D], F32)
            gqk = k.sb("gqk", [128, 10 * 64], F32)
            gqkv = gqk.rr("p (h d) -> p h d", d=64)
            g64 = k.sb("g64", [128, 128], F32)
            bcast_row(g64[:, 0:64], W["gqa_q_norm"][l])
            bcast_row(g64[:, 64:128], W["gqa_k_norm"][l])
            for h in range(8):
                k.ts(gqkv[:, h, :], g64[:, 0:64], GQA_SCALE, None, ALU.mult)
            for h in range(2):
                k.cp(gqkv[:, 8 + h, :], g64[:, 64:128])
            gqa = k.sb("gqa", [128, 384], F32)
            gkva = k.sb("gkva", [128, 256], F32)
            mqn = k.sb("mqn", [128, 96], F32)
            mkn = k.sb("mkn", [128, 96], F32)
            bcast_row(gqa, W["mla_q_a_norm"][l])
            bcast_row(gkva, W["mla_kv_a_norm"][l])
            bcast_row(mqn, W["mla_q_norm"][l])
            bcast_row(mkn, W["mla_k_norm"][l])
            k.ts(mqn, mqn, MLA_SCALE, None, ALU.mult)
            sw = k.sb("sw", [128, 36], F32)
            with nc.allow_non_contiguous_dma("tiny column load"):
                k.dma(sw.rr("p (t c) -> p t c", t=3), W["hy_short_w"][l].rr("t (c p) -> p t c", p=128))
            hT = [k.sb("hT%d" % i, [128, KD * 130], BF16) for i in range(3)]
            hTv = [h_.rr("p (k t) -> p k t", k=KD) for h_ in hT]
            xt = [k.sb("xt%d" % i, [128, D], F32) for i in range(2)]
            rp = [k.sb("rp%d" % i, [128, 192], F32) for i in range(2)]
            junk = k.sb("junk", [128, D], F32)
            hb = k.sb("hb", [128, D], BF16)
            ss = k.sb("ss", [128, 16], F32)
            t1 = k.sb("t1", [128, 16], F32)
            rs = k.sb("rs", [128, 16], F32)
            psT = k.ps("psT", [128, KD, 128], BF16)
            psT2 = k.ps("psT2", [128, KD, 128], BF16)
            psT3 = k.ps("psT3", [128, KD, 128], BF16)
            NPA = 4
            psC2 = k.ps("psC2", [128, 512], F32)
            psA = [k.ps("psA%d" % i, [128, 512], F32) for i in range(NPA)]
            ssA = k.sb("ssA", [128, 1], F32)
            t1A = k.sb("t1A", [128, 1], F32)
            rsA = k.sb("rsA", [128, 1], F32)
            ss2 = k.sb("ss2", [128, 16], F32)
            t12 = k.sb("t12", [128, 16], F32)
            rs2 = k.sb("rs2", [128, 16], F32)
            sq2 = k.sb("sq2", [128, 768], F32)
            ucT = k.sb("ucT", [128, 12 * 128], F32)
            ucv = ucT.rr("p (c t) -> p c t", c=12)
            tmpu = k.sb("tmpu", [128, 128], F32)
            zt = k.sb("zt", [128, 4 * 128], BF16)
            pBs = [k.sb("pB%d" % i, [128, 1440], F32) for i in range(2)]
            gsb = [k.sb("gsb%d" % i, [128, 1024], BF16) for i in range(2)]
            sq = k.sb("sq", [128, 640], F32)
            qkn = k.sb("qkn", [128, 640], F32)
            qkA = sq
            qkB = k.sb("qkB", [128, 640], F32)
            qkb = k.sb("qkb", [128, 640], BF16)
            qkT = k.sb("qkT", [128, 5 * 128], BF16)
            vaug = k.sb("vaug", [128, 130], BF16)
            k.memset(vaug, 1.0)
            cqb = k.sb("cqb", [128, 640], BF16)
            cT = k.sb("cT", [128, 5 * 128], BF16)
            qmS = k.sb("qmS", [128, 768], F32)
            kvS = k.sb("kvS", [128, 1024], F32)
            qmn = k.sb("qmn", [128, 768], F32)
            qmb = k.sb("qmb", [128, 768], BF16)
            kmb = k.sb("kmb", [128, 768], BF16)
            kmn = qmn[:, 0:512]
            rA = k.sb("rA", [128, 256], F32)
            rB = k.sb("rB", [128, 256], F32)
            krg = k.sb("krg", [128, 32], F32)
            krr = k.sb("krr", [128, 32], F32)
            vmaug = k.sb("vmaug", [128, 520], BF16)
            k.memset(vmaug, 1.0)
            mT = k.sb("mT", [128, 16 * 128], BF16)
            npa = [0]

            def nextps():
                p_ = psA[npa[0] % NPA]
                npa[0] += 1
                return p_

            def stageA(i):
                s = seq_of(i)
                if is_first(i):
                    k.dma(G1, modv[l % 2, s, 1])
                    k.dma(SH1, modv[l % 2, s, 0])
                slot = i % 3
                x_ = xt[i % 2]
                k.dma(x_, xs[i * 128:(i + 1) * 128, :])
                norm_mod_T(x_, G1, SH1, hb, psT, hTv[slot][:, :, 1:129], junk, ssA, t1A, rsA)
                if is_first(i):
                    k.memset(hTv[slot][:, :, 0:1], 0.0)
                else:
                    k.cp(hTv[(i - 1) % 3][:, :, 129:130], hTv[slot][:, :, 1:2])
                if is_last(i):
                    k.memset(hTv[slot][:, :, 129:130], 0.0)
                else:
                    k.cp(hTv[(i + 1) % 3][:, :, 0:1], hTv[slot][:, :, 128:129])

            def stageB(i):
                pB = pBs[i % 2]
                slot = i % 3
                hv = hTv[slot]
                cols = slice(i * 128, (i + 1) * 128)
                r_ = rp[i % 2]
                k.dma(r_, C["rope"][i * 128:(i + 1) * 128, :])
                for b4 in range(4):
                    p_ = nextps()
                    for r3 in range(3):
                        cc = b4 * 3 + r3
                        for kk in range(KD):
                            k.mm(p_[:, r3 * 130:(r3 + 1) * 130], winv[:, kk, cc * 128:(cc + 1) * 128],
                                 hv[:, kk, 0:130], start=(kk == 0), stop=(kk == KD - 1))
                    for r3 in range(3):
                        cc = b4 * 3 + r3
                        u_ = p_[:, r3 * 130:(r3 + 1) * 130]
                        k.ts(tmpu, u_[:, 0:128], sw[:, cc:cc + 1], None, ALU.mult)
                        k.stt(tmpu, u_[:, 1:129], sw[:, 12 + cc:13 + cc], tmpu, ALU.mult, ALU.add)
                        k.stt(ucv[:, cc, :], u_[:, 2:130], sw[:, 24 + cc:25 + cc], tmpu, ALU.mult, ALU.add)
                k.tt(zt, ucT[:, 512:1024], ucT[:, 1024:1536], ALU.mult)
                k.dma(zT.rr("(k p) n -> p k n", p=128)[:, :, cols], zt.rr("p (k t) -> p k t", k=4), q="pool",
                      nowaw=True)
                k.dma(x0T.rr("(k p) n -> p k n", p=128)[:, :, cols], ucT[:, 0:512].rr("p (k t) -> p k t", k=4),
                      q="pool", nowaw=True)
                for (o0, wd) in ((0, 512), (512, 512), (1024, 416)):
                    p_ = nextps()
                    for kk in range(KD):
                        k.mm(p_[:, 0:wd], hv[:, kk, 1:129], winv[:, kk, 1536 + o0:1536 + o0 + wd],
                             start=(kk == 0), stop=(kk == KD - 1))
                    k.cp(pB[:, o0:o0 + wd], p_[:, 0:wd], eng="act")
                for gc in range(6):
                    g_ = gsb[(gc // 2) % 2]
                    p_ = nextps()
                    for kk in range(KD):
                        k.mm(p_, hv[:, kk, 1:129], winv[:, kk, 2976 + gc * 512:2976 + (gc + 1) * 512],
                             start=(kk == 0), stop=(kk == KD - 1))
                    k.act(g_[:, (gc % 2) * 512:(gc % 2 + 1) * 512], p_, AF.Sigmoid)
                    if gc % 2 == 1:
                        k.dma(gts[i * 128:(i + 1) * 128, (gc // 2) * 1024:(gc // 2 + 1) * 1024], g_, q="pool",
                              nowaw=True)
            def chain1(i):
                pB = pBs[i % 2]
                cols = slice(i * 128, (i + 1) * 128)
                r_ = rp[i % 2]
                psT = psT2
                k.act(sq[:, 0:640], pB[:, 0:640], AF.Square)
                k.red(ss[:, 0:10], sq[:, 0:640].rr("p (h d) -> p h d", d=64))
                k.rstd(rs[:, 0:10], ss[:, 0:10], 1.0 / 64, t1[:, 0:10])
                qv = qkn.rr("p (h d) -> p h d", d=64)
                k.tt(qv, pB[:, 0:640].rr("p (h d) -> p h d", d=64), rs[:, 0:10].us(2).bc([128, 10, 64]), ALU.mult)
                k.tt(qkn, qkn, gqk, ALU.mult)
                Av = qkA.rr("p (h d) -> p h d", d=64)
                Bv = qkB.rr("p (h d) -> p h d", d=64)
                k.tt(Av, qv, r_[:, 0:64].us(1).bc([128, 10, 64]), ALU.mult)
                k.tt(Bv[:, :, 0:32], qv[:, :, 32:64], r_[:, 64:96].us(1).bc([128, 10, 32]), ALU.mult)
                k.tt(Bv[:, :, 32:64], qv[:, :, 0:32], r_[:, 96:128].us(1).bc([128, 10, 32]), ALU.mult)
                k.tt(qkb, qkA, qkB, ALU.add)
                for t in range(5):
                    k.tr(psT[:, t, :], qkb[:, t * 128:(t + 1) * 128], ident)
                k.cp(qkT.rr("p (k t) -> p k t", k=5), psT[:, 0:5, :], eng="act")
                k.dma(qT.rr("(k p) n -> p k n", p=128)[:, :, cols], qkT[:, 0:512].rr("p (k t) -> p k t", k=4),
                      q="pool", nowaw=True)
                k.dma(kT[:, cols], qkT[:, 512:640], q="pool", nowaw=True)
                k.cp(vaug.rr("p (g f) -> p g f", g=2)[:, :, 0:64], pB[:, 640:768].rr("p (g f) -> p g f", g=2))
                k.dma(vA[i * 128:(i + 1) * 128, :], vaug, q="pool", nowaw=True)
            def chain2(i):
                pB = pBs[i % 2]
                cols = slice(i * 128, (i + 1) * 128)
                r_ = rp[i % 2]
                psT = psT3
                sq, ss, rs, t1 = sq2, ss2, rs2, t12
                k.act(sq[:, 0:640], pB[:, 768:1408], AF.Square)
                k.red(ss[:, 10:11], sq[:, 0:384])
                k.red(ss[:, 11:12], sq[:, 384:640])
                k.rstd(rs[:, 10:11], ss[:, 10:11], 1.0 / 384, t1[:, 10:11])
                k.rstd(rs[:, 11:12], ss[:, 11:12], 1.0 / 256, t1[:, 11:12])
                k.stt(cqb[:, 0:384], pB[:, 768:1152], rs[:, 10:11], gqa, ALU.mult, ALU.mult)
                k.stt(cqb[:, 384:640], pB[:, 1152:1408], rs[:, 11:12], gkva, ALU.mult, ALU.mult)
                for t in range(5):
                    k.tr(psT[:, t, :], cqb[:, t * 128:(t + 1) * 128], ident)
                k.cp(cT.rr("p (k t) -> p k t", k=5), psT[:, 0:5, :], eng="act")
                for (o0, wd) in ((0, 512), (512, 256)):
                    p_ = psC2
                    for kk in range(3):
                        k.mm(p_[:, 0:wd], cT[:, kk * 128:(kk + 1) * 128], wqb[:, kk * 768 + o0:kk * 768 + o0 + wd],
                             start=(kk == 0), stop=(kk == 2))
                    k.cp(qmS[:, o0:o0 + wd], p_[:, 0:wd], eng="act")
                for o0 in (0, 512):
                    p_ = psC2
                    for kk in range(2):
                        k.mm(p_, cT[:, (3 + kk) * 128:(4 + kk) * 128], wkvb[:, kk * 1024 + o0:kk * 1024 + o0 + 512],
                             start=(kk == 0), stop=(kk == 1))
                    k.cp(kvS[:, o0:o0 + 512], p_, eng="act")
                k.act(sq[:, 0:768], qmS, AF.Square)
                k.red(ss[:, 0:8], sq[:, 0:768].rr("p (h d) -> p h d", d=96))
                k.rstd(rs[:, 0:8], ss[:, 0:8], 1.0 / 96, t1[:, 0:8])
                qmv = qmn.rr("p (h d) -> p h d", d=96)
                qbv = qmb.rr("p (h d) -> p h d", d=96)
                k.tt(qmv, qmS.rr("p (h d) -> p h d", d=96), rs[:, 0:8].us(2).bc([128, 8, 96]), ALU.mult)
                k.tt(qmv, qmv, mqn.us(1).bc([128, 8, 96]), ALU.mult)
                k.cp(qbv[:, :, 0:64], qmv[:, :, 0:64])
                rAv = rA.rr("p (h d) -> p h d", d=32)
                rBv = rB.rr("p (h d) -> p h d", d=32)
                k.tt(rAv, qmv[:, :, 64:96], r_[:, 128:160].us(1).bc([128, 8, 32]), ALU.mult)
                k.tt(rBv[:, :, 0:16], qmv[:, :, 80:96], r_[:, 160:176].us(1).bc([128, 8, 16]), ALU.mult)
                k.tt(rBv[:, :, 16:32], qmv[:, :, 64:80], r_[:, 176:192].us(1).bc([128, 8, 16]), ALU.mult)
                k.tt(qbv[:, :, 64:96], rAv, rBv, ALU.add)
                kvv = kvS.rr("p (h d) -> p h d", d=128)
                k.act(sq[:, 0:512].rr("p (h d) -> p h d", d=64), kvv[:, :, 0:64], AF.Square)
                k.red(ss[:, 0:8], sq[:, 0:512].rr("p (h d) -> p h d", d=64))
                k.act(sq[:, 512:544], pB[:, 1408:1440], AF.Square)
                k.red(ss[:, 8:9], sq[:, 512:544])
                k.ts(ss[:, 0:8], ss[:, 0:8], ss[:, 8:9], None, ALU.add)
                k.rstd(rs[:, 0:8], ss[:, 0:8], 1.0 / 96, t1[:, 0:8])
                knv = kmn.rr("p (h d) -> p h d", d=64)
                kbv = kmb.rr("p (h d) -> p h d", d=96)
                k.tt(knv, kvv[:, :, 0:64], rs[:, 0:8].us(2).bc([128, 8, 64]), ALU.mult)
                k.tt(kbv[:, :, 0:64], knv, mkn[:, 0:64].us(1).bc([128, 8, 64]), ALU.mult)
                k.tt(krg, pB[:, 1408:1440], mkn[:, 64:96], ALU.mult)
                k.tt(rA[:, 0:32], krg, r_[:, 128:160], ALU.mult)
                k.tt(rB[:, 0:16], krg[:, 16:32], r_[:, 160:176], ALU.mult)
                k.tt(rB[:, 16:32], krg[:, 0:16], r_[:, 176:192], ALU.mult)
                k.tt(krr, rA[:, 0:32], rB[:, 0:32], ALU.add)
                k.tt(kbv[:, :, 64:96], krr.us(1).bc([128, 8, 32]), rs[:, 0:8].us(2).bc([128, 8, 32]), ALU.mult)
                k.cp(vmaug.rr("p (h f) -> p h f", h=8)[:, :, 0:64], kvv[:, :, 64:128])
                k.dma(vmA[i * 128:(i + 1) * 128, :], vmaug, q="pool", nowaw=True)
                mTv = mT.rr("p (k t) -> p k t", k=16)
                for h in range(8):
                    k.tr(psT[0:96, h, :], qmb[:, h * 96:(h + 1) * 96], ident)
                k.cp(mTv[0:96, 0:8, :], psT[0:96, :, :], eng="act")
                for h in range(8):
                    k.tr(psT[0:96, h, :], kmb[:, h * 96:(h + 1) * 96], ident)
                k.cp(mTv[0:96, 8:16, :], psT[0:96, :, :], eng="act")
                k.dma(qmT.rr("(h d) n -> d h n", d=96)[:, :, cols], mTv[0:96, 0:8, :], q="pool", nowaw=True)
                k.dma(kmT.rr("(h d) n -> d h n", d=96)[:, :, cols], mTv[0:96, 8:16, :], q="pool", nowaw=True)

            stageA(0)
            prev = []
            for i in range(1, NTt + 1):
                if i < NTt:
                    stageA(i)
                with k.record() as r1:
                    stageB(i - 1)
                k.play_interleaved([r1] + prev)
                with k.record() as c1:
                    chain1(i - 1)
                with k.record() as c2:
                    chain2(i - 1)
                prev = [c1, c2]
            k.play_interleaved(prev)

    def sin_wrapped(dst, ps_in, bcol, fcol, a, m, npart, n):
        k.ts(a[0:npart, 0:n], ps_in, bcol, fcol, ALU.add, ALU.mult)
        k.ts(m[0:npart, 0:n], a[0:npart, 0:n], PI, None, ALU.is_gt)
        k.stt(dst, m[0:npart, 0:n], -2 * PI, a[0:npart, 0:n], ALU.mult, ALU.add)
        k.ts(m[0:npart, 0:n], a[0:npart, 0:n], -PI, None, ALU.is_lt)
        k.stt(dst, m[0:npart, 0:n], 2 * PI, dst, ALU.mult, ALU.add)
        k.act(dst, dst, AF.Sin)

    def fft_fwd(F1s, TWs, C2s, S2s, nS2s, src, nrows, ncb, consume, psS1=None, psX=None, consume_part=None,
                staged=None):
        QW = 4 * P
        xin = [k.sb("xin%d" % i, [128, ncb * 128], BF16) for i in range(2)]
        if psS1 is None:
            psS1 = [k.ps("psS1_%d" % i, [128, 2 * 2 * P], F32) for i in range(2)]
        if psX is None:
            psX = [k.ps("psX%d" % i, [128, QW], F32) for i in range(2)]
        m1 = k.sb("m1", [128, 2 * 2 * P], F32)
        m2 = k.sb("m2", [128, 2 * 2 * P], F32)
        Bri = [k.sb("Bri%d" % i, [128, 2 * QW], BF16) for i in range(2)]
        nq = 0
        pend = []

        def flush_iter(st1):
            lists = [st1] if st1 else []
            for sidx in range(3):
                qi = len(pend) - 1 - (sidx + 1) + (1 if st1 is None else 0)
                if 0 <= qi < len(pend) and pend[qi][sidx]:
                    lists.append(pend[qi][sidx])
            k.play_interleaved(lists)

        for cb in range(512 // ncb):
            xi = xin[cb % 2]
            xv = xi.rr("p (c t) -> p c t", c=ncb)
            with nc.allow_non_contiguous_dma("fft gather"):
                k.dma(xv[0:nrows], src[cb * ncb:(cb + 1) * ncb, 0:nrows * 128].rr("c (a t) -> a c t", t=128))
            for q4 in range(ncb // 4):
                B = Bri[nq % 2]
                Bv = B.rr("p (r c k) -> p r c k", r=2, c=4)
                if staged is not None:
                    k_rec = k.record()
                    st1 = k_rec.__enter__()
                for half in range(2):
                    p_ = psS1[half % len(psS1)][:, 0:2 * 2 * P]
                    pv = p_.rr("p (c r k) -> p c r k", c=2, r=2)
                    for ci in range(2):
                        cl = q4 * 4 + half * 2 + ci
                        k.mm(p_[:, ci * 2 * P:(ci + 1) * 2 * P], xv[0:nrows, cl, :], F1s[0:nrows, :])
                    m1v = m1.rr("p (c r k) -> p c r k", c=2, r=2)
                    m2v = m2.rr("p (c r k) -> p c r k", c=2, r=2)
                    k.tt(m1v, pv, TWs[:, 0, :].us(1).us(1).bc([128, 2, 2, P]), ALU.mult)
                    k.tt(m2v, pv, TWs[:, 1, :].us(1).us(1).bc([128, 2, 2, P]), ALU.mult)
                    k.tt(Bv[:, 0, half * 2:half * 2 + 2, :], m1v[:, :, 0, :], m2v[:, :, 1, :], ALU.subtract)
                    k.tt(Bv[:, 1, half * 2:half * 2 + 2, :], m2v[:, :, 0, :], m1v[:, :, 1, :], ALU.add)
                Br = B[:, 0:QW]
                Bi = B[:, QW:2 * QW]
                if staged is not None:
                    k_rec.__exit__(None, None, None)
                    with k.record() as st2:
                        k.mm(psX[0][:, 0:QW], C2s, Br, start=True, stop=False)
                        k.mm(psX[0][:, 0:QW], S2s, Bi, start=False, stop=True)
                        k.mm(psX[1][:, 0:QW], C2s, Bi, start=True, stop=False)
                        k.mm(psX[1][:, 0:QW], nS2s, Br, start=False, stop=True)
                    parts = staged(cb * (ncb // 4) + q4, psX[0][:, 0:QW], psX[1][:, 0:QW])
                    st2.extend(parts[0])
                    pend.append([st2, parts[1], parts[2]])
                    flush_iter(st1)
                    nq += 1
                    continue
                if len(psX) == 2:
                    k.mm(psX[0][:, 0:QW], C2s, Br, start=True, stop=False)
                    k.mm(psX[0][:, 0:QW], S2s, Bi, start=False, stop=True)
                    k.mm(psX[1][:, 0:QW], C2s, Bi, start=True, stop=False)
                    k.mm(psX[1][:, 0:QW], nS2s, Br, start=False, stop=True)
                    consume(cb * (ncb // 4) + q4, psX[0][:, 0:QW], psX[1][:, 0:QW])
                else:
                    k.mm(psX[0][:, 0:QW], C2s, Br, start=True, stop=False)
                    k.mm(psX[0][:, 0:QW], S2s, Bi, start=False, stop=True)
                    consume_part(cb * (ncb // 4) + q4, 0, psX[0][:, 0:QW])
                    k.mm(psX[0][:, 0:QW], C2s, Bi, start=True, stop=False)
                    k.mm(psX[0][:, 0:QW], nS2s, Br, start=False, stop=True)
                    consume_part(cb * (ncb // 4) + q4, 1, psX[0][:, 0:QW])
                nq += 1
        if staged is not None:
            while any(l for p_ in pend for l in p_):
                pend.append([[], [], []])
                flush_iter([])

    def phase_filt(l, bg=None):
        with k.phase():
            f1w = k.sb("f1w", [33, 64], F32)
            f2w = k.sb("f2w", [64, 64], F32)
            f3w = k.sb("f3w", [64, 1024], F32)
            b1 = k.sb("b1", [64, 1], F32)
            b2 = k.sb("b2", [64, 1], F32)
            fq = k.sb("fq", [64, 2], F32)
            k.dma(f1w, W["hy_f1_w"][l])
            k.dma(f2w, W["hy_f2_w"][l])
            k.dma(f3w, W["hy_f3_w"][l])
            k.dma(b1, W["hy_f1_b"][l])
            k.dma(b2, W["hy_f2_b"][l])
            with nc.allow_non_contiguous_dma("tiny column load"):
                k.dma(fq, W["hy_sin_freq"][l].rr("t j -> j t"))
            ndl = k.sb("ndl", [128, 4], F32)
            k.dma(ndl, C["ndelta"])
            ze = [k.sb("ze%d" % i, [33, 512], F32) for i in range(2)]
            tr_ = [k.sb("trow%d" % i, [128, 512], F32) for i in range(2)]
            a_ = k.sb("fa", [128, 512], F32)
            m_ = k.sb("fm", [128, 512], F32)
            h1 = k.sb("h1", [64, 512], F32)
            h2 = k.sb("h2", [64, 512], F32)
            dec = k.sb("dec", [128, 512], F32)
            kf32 = k.sb("kf32", [128, 512], F32)
            kb16 = [k.sb("kb16_%d" % i, [128, 4 * 512], BF16) for i in range(2)]
            sqj = k.sb("sqj", [128, 512], F32)
            NCH = N // 512
            ssall = k.sb("ssall", [128, 4 * NCH], F32)
            ssv = ssall.rr("p (c n) -> p c n", c=4)
            if bg is None:
                psf = [k.ps("psf%d" % i, [128, 512], F32) for i in range(4)]
            else:
                psf = [bg[0], bg[0], bg[1], bg[1]]
            for ch in range(NCH):
                z_ = ze[ch % 2]
                t_ = tr_[ch % 2]
                k.dma(z_, C["zembL"][:, ch * 512:(ch + 1) * 512])
                bcast_row(t_, C["tKL"][:, ch * 512:(ch + 1) * 512])
                k.mm(psf[0][0:64, :], f1w, z_)
                sin_wrapped(h1, psf[0][0:64, :], b1[:, 0:1], fq[:, 0:1], a_, m_, 64, 512)
                k.mm(psf[1][0:64, :], f2w, h1)
                sin_wrapped(h2, psf[1][0:64, :], b2[:, 0:1], fq[:, 1:2], a_, m_, 64, 512)
                dr = 0 if ch * 512 < L else 1
                kb_ = kb16[ch % 2]
                for cc in range(4):
                    p_ = psf[2 + cc % 2]
                    k.mm(p_, f3w[:, dr * 512 + cc * 128:dr * 512 + (cc + 1) * 128], h2)
                    k.act(dec, t_, AF.Exp, scale=ndl[:, cc:cc + 1])
                    k.tt(kf32, p_, dec, ALU.mult)
                    k.act(sqj, kf32, AF.Square)
                    k.red(ssv[:, cc, ch:ch + 1], sqj)
                    k.cp(kb_[:, cc * 512:(cc + 1) * 512], kf32)
                k.dma(kfil.rr("(k p) n -> p k n", p=128)[:, :, ch * 512:(ch + 1) * 512],
                      kb_.rr("p (k n) -> p k n", k=4), q="pool", nowaw=True)
            sscol = k.sb("sscol", [128, 4], F32)
            k.red(sscol, ssv)
            dg = k.sb("dg", [128, 128], F32)
            rnl = k.sb("rnl", [128, 512], F32)
            rt = k.sb("rt", [128, 512], F32)
            for cc in range(4):
                k.ts(dg, identf, sscol[:, cc:cc + 1], None, ALU.mult)
                k.mm(psf[0][:, cc * 128:(cc + 1) * 128], onesf, dg)
            k.act(rt, psf[0], AF.Sqrt, bias=k.epsb, scale=1.0)
            k.recip(rnl, rt)
            k.ts(rnl, rnl, 1.0 / N, None, ALU.mult)
            k.cp(rnlk, rnl)
        with k.phase():
            F1s = k.sb("F1s", [P, 2 * P], BF16)
            TWs = k.sb("TWs", [128, 2, P], F32)
            C2s = k.sb("C2s", [128, 128], BF16)
            S2s = k.sb("S2s", [128, 128], BF16)
            nS2s = k.sb("nS2s", [128, 128], BF16)
            k.dma(F1s, C["F1"])
            k.dma(TWs, C["TW"])
            k.dma(C2s, C["C2"])
            k.dma(S2s, C["S2"])
            k.dma(nS2s, C["nS2"])
            ko = [k.sb("ko%d" % i, [128, 2 * 4 * P], F32) for i in range(2)]

            def consume_part(q, r, X):
                o = ko[q % 2]
                ov = o.rr("p (r c k) -> p r c k", r=2, c=4)
                rb = rnlk[:, q * 4:(q + 1) * 4].us(2).bc([128, 4, P])
                k.tt(ov[:, r], X.rr("p (c k) -> p c k", c=4), rb, ALU.mult)
                if r == 1:
                    k.dma(KfL.rr("r p m -> p r m")[:, :, q * 4 * P:(q + 1) * 4 * P], o.rr("p (r m) -> p r m", r=2),
                          q="pool", nowaw=True)

            def consume(q, Xr, Xi):
                consume_part(q, 0, Xr)
                consume_part(q, 1, Xi)

            if bg is None:
                fft_fwd(F1s, TWs, C2s, S2s, nS2s, kfil, P, 64, consume)
            else:
                fft_fwd(F1s, TWs, C2s, S2s, nS2s, kfil, P, 32, consume, psS1=[bg[0]], psX=[bg[1]],
                        consume_part=consume_part)
        with k.phase():
            f1w = k.sb("f1w", [33, 64], F32)
            f2w = k.sb("f2w", [64, 64], F32)
            f3w = k.sb("f3w", [64, 1024], F32)
            b1 = k.sb("b1", [64, 1], F32)
            b2 = k.sb("b2", [64, 1], F32)
            fq = k.sb("fq", [64, 2], F32)
            k.dma(f1w, W["hy_f1_w"][l])
            k.dma(f2w, W["hy_f2_w"][l])
            k.dma(f3w, W["hy_f3_w"][l])
            k.dma(b1, W["hy_f1_b"][l])
            k.dma(b2, W["hy_f2_b"][l])
            with nc.allow_non_contiguous_dma("tiny column load"):
                k.dma(fq, W["hy_sin_freq"][l].rr("t j -> j t"))
            ntk = k.sb("ntk", [128, 4], F32)
            k.dma(ntk, C["ntKC"])
            drow = k.sb("drow", [128, 512], F32)
            bcast_row(drow, C["deltarow"])
            Ccs = k.sb("Ccs", [128, 4 * 512], BF16)
            nScs = k.sb("nScs", [128, 4 * 512], BF16)
            k.dma(Ccs.rr("p (k n) -> p k n", k=4), C["Cc"])
            k.dma(nScs.rr("p (k n) -> p k n", k=4), C["nSc"])
            ze = k.sb("ze", [33, 512], F32)
            k.dma(ze, C["zembC"])
            a_ = k.sb("fa", [128, 512], F32)
            m_ = k.sb("fm", [128, 512], F32)
            h1 = k.sb("h1", [64, 512], F32)
            h2 = k.sb("h2", [64, 512], F32)
            if bg is None:
                psf = [k.ps("psf%d" % i, [128, 512], F32) for i in range(4)]
            else:
                psf = [bg[0], bg[0], bg[1], bg[1]]
            k.mm(psf[0][0:64, :], f1w, ze)
            sin_wrapped(h1, psf[0][0:64, :], b1[:, 0:1], fq[:, 0:1], a_, m_, 64, 512)
            k.mm(psf[1][0:64, :], f2w, h1)
            sin_wrapped(h2, psf[1][0:64, :], b2[:, 0:1], fq[:, 1:2], a_, m_, 64, 512)
            dec = k.sb("dec", [128, 512], F32)
            kf32 = k.sb("kf32", [128, 4 * 512], F32)
            ksq = k.sb("ksq", [128, 4 * 512], F32)
            kb16 = k.sb("kb16", [128, 4 * 512], BF16)
            for nch in range(4):
                dr = 0 if nch < 2 else 1
                p_ = psf[2 + nch % 2]
                k.mm(p_, h2[:, nch * 128:(nch + 1) * 128], f3w[:, dr * 512:(dr + 1) * 512])
                k.act(dec, drow, AF.Exp, scale=ntk[:, nch:nch + 1])
                k.tt(kf32[:, nch * 512:(nch + 1) * 512], p_, dec, ALU.mult)
            k.act(ksq, kf32, AF.Square)
            k.cp(kb16, kf32)
            for nch in range(4):
                k.mm(psf[0], onesf, ksq[:, nch * 512:(nch + 1) * 512], start=(nch == 0), stop=(nch == 3))
            rt = k.sb("rt", [128, 512], F32)
            rnc = k.sb("rnc", [128, 512], F32)
            k.act(rt, psf[0], AF.Sqrt, bias=k.epsb, scale=1.0)
            k.recip(rnc, rt)
            k.ts(rnc, rnc, 1.0 / 512, None, ALU.mult)
            ko = [k.sb("ko%d" % i, [128, 512], F32) for i in range(2)]
            n = 0
            for kc in range(4):
                for r, tab in ((0, Ccs), (1, nScs)):
                    p_ = psf[2 + n % 2]
                    o = ko[n % 2]
                    n += 1
                    for nch in range(4):
                        k.mm(p_, tab[:, nch * 512 + kc * 128:nch * 512 + (kc + 1) * 128],
                             kb16[:, nch * 512:(nch + 1) * 512], start=(nch == 0), stop=(nch == 3))
                    k.tt(o, p_, rnc, ALU.mult)
                    k.dma(KfC[r, kc * 128:(kc + 1) * 128, :], o, q="pool", nowaw=True)

    def phase_hyena(l):
        with k.phase():
            F1s = k.sb("F1s", [P, 2 * P], BF16)
            TWs = k.sb("TWs", [128, 2, P], F32)
            C2s = k.sb("C2s", [128, 128], BF16)
            S2s = k.sb("S2s", [128, 128], BF16)
            nS2s = k.sb("nS2s", [128, 128], BF16)
            RA1 = k.sb("RA1", [128, 256], BF16)
            RA2 = k.sb("RA2", [128, 256], BF16)
            TWI = k.sb("TWI", [P, 2, 128], F32)
            C1s = k.sb("C1s", [P, PH], BF16)
            nS1s = k.sb("nS1s", [P, PH], BF16)
            for t_, nm in ((F1s, "F1"), (TWs, "TW"), (C2s, "C2"), (S2s, "S2"), (nS2s, "nS2"), (RA1, "RA1"),
                           (RA2, "RA2"), (TWI, "TWI"), (C1s, "C1"), (nS1s, "nS1")):
                k.dma(t_, C[nm])
            QW = 4 * P
            kfq = [k.sb("kfq%d" % i, [128, 2 * QW], F32) for i in range(2)]
            ta = k.sb("ta", [128, QW], F32)
            tb = k.sb("tb", [128, QW], F32)
            Yri = [k.sb("Yri%d" % i, [128, 2 * QW], BF16) for i in range(2)]
            psC = [k.ps("psC%d" % i, [128, 512], F32) for i in range(2)]
            psY = k.ps("psY", [128, 512], F32)
            n1 = k.sb("n1", [128, 512], F32)
            n2 = k.sb("n2", [128, 512], F32)
            Dri = [k.sb("Dri%d" % i, [128, 2 * 512], BF16) for i in range(2)]
            yo = [k.sb("yo%d" % i, [128, 512], F32) for i in range(2)]

            def consume(q, Xr, Xi):
                kq = kfq[q % 2]
                recA = k.record()
                lA = recA.__enter__()
                k.dma(kq.rr("p (r m) -> p r m", r=2), KfL.rr("r p m -> p r m")[:, :, q * QW:(q + 1) * QW])
                Kr = kq[:, 0:QW]
                Ki = kq[:, QW:2 * QW]
                Y = Yri[q % 2]
                k.tt(ta, Xr, Kr, ALU.mult)
                k.tt(tb, Xi, Ki, ALU.mult)
                k.tt(Y[:, 0:QW], ta, tb, ALU.subtract)
                k.tt(ta, Xr, Ki, ALU.mult)
                k.tt(tb, Xi, Kr, ALU.mult)
                k.tt(Y[:, QW:2 * QW], ta, tb, ALU.add)
                recA.__exit__(None, None, None)
                recB = k.record()
                lB = recB.__enter__()
                Dt = Dri[q % 2]
                Dv = Dt.rr("p (r c t) -> p r c t", r=2, c=4)
                for half in range(2):
                    p_ = psC[half]
                    for ci in range(2):
                        cl = half * 2 + ci
                        k.mm(p_[0:P, ci * 256:(ci + 1) * 256], Y[:, cl * P:(cl + 1) * P], RA1, start=True, stop=False)
                        k.mm(p_[0:P, ci * 256:(ci + 1) * 256], Y[:, QW + cl * P:QW + (cl + 1) * P], RA2,
                             start=False, stop=True)
                    pv = p_[0:P, :].rr("p (c r t) -> p c r t", c=2, r=2)
                    n1v = n1[0:P, :].rr("p (c r t) -> p c r t", c=2, r=2)
                    n2v = n2[0:P, :].rr("p (c r t) -> p c r t", c=2, r=2)
                    k.tt(n1v, pv, TWI[:, 0, :].us(1).us(1).bc([P, 2, 2, 128]), ALU.mult)
                    k.tt(n2v, pv, TWI[:, 1, :].us(1).us(1).bc([P, 2, 2, 128]), ALU.mult)
                    k.tt(Dv[0:P, 0, half * 2:half * 2 + 2, :], n1v[:, :, 0, :], n2v[:, :, 1, :], ALU.subtract)
                    k.tt(Dv[0:P, 1, half * 2:half * 2 + 2, :], n2v[:, :, 0, :], n1v[:, :, 1, :], ALU.add)
                recB.__exit__(None, None, None)
                recC = k.record()
                lC = recC.__enter__()
                k.mm(psY[0:PH, :], C1s, Dt[0:P, 0:512], start=True, stop=False)
                k.mm(psY[0:PH, :], nS1s, Dt[0:P, 512:1024], start=False, stop=True)
                o = yo[q % 2]
                k.cp(o[0:PH, :], psY[0:PH, :], eng="act")
                k.dma(yconv[0:PH, q * 4:(q + 1) * 4, :], o[0:PH, :].rr("p (c t) -> p c t", c=4), q="pool", nowaw=True)
                recC.__exit__(None, None, None)
                return [lA, lB, lC]

            fft_fwd(F1s, TWs, C2s, S2s, nS2s, zT, PH, 64, None, staged=consume)
        with k.phase():
            Ccs = k.sb("Ccs", [128, 4 * 512], BF16)
            nScs = k.sb("nScs", [128, 4 * 512], BF16)
            k.dma(Ccs.rr("p (k n) -> p k n", k=4), C["Cc"])
            k.dma(nScs.rr("p (k n) -> p k n", k=4), C["nSc"])
            kfc = k.sb("kfc", [128, 2 * 4 * 512], F32)
            k.dma(kfc.rr("p (r k c) -> p r k c", r=2, k=4), KfC.rr("r (k p) c -> p r k c", p=128))
            zc = k.sb("zc", [128, 4 * 256], BF16)
            k.dma(zc.rr("p (k t) -> p k t", k=4), zT.rr("(k p) n -> p k n", p=128)[:, :, L:Ltot])
            psT = k.ps("psTc", [128, 8, 128], BF16)
            for t in range(2):
                for cc in range(4):
                    k.tr(psT[:, t * 4 + cc, :], zc[:, cc * 256 + t * 128:cc * 256 + (t + 1) * 128], ident)
            ztm = k.sb("ztm", [128, 2 * 512], BF16)
            k.cp(ztm.rr("p (a t) -> p a t", a=8), psT, eng="act")
            psZ = [k.ps("psZ%d" % i, [128, 512], F32) for i in range(2)]
            ta = k.sb("ta", [128, 512], F32)
            tb = k.sb("tb", [128, 512], F32)
            Yr = k.sb("Yr", [128, 4 * 512], BF16)
            Yi = k.sb("Yi", [128, 4 * 512], BF16)
            for kc in range(4):
                for r, tab in ((0, Ccs), (1, nScs)):
                    for t in range(2):
                        k.mm(psZ[r], tab[:, t * 512 + kc * 128:t * 512 + (kc + 1) * 128], ztm[:, t * 512:(t + 1) * 512],
                             start=(t == 0), stop=(t == 1))
                Kr = kfc[:, kc * 512:(kc + 1) * 512]
                Ki = kfc[:, 2048 + kc * 512:2048 + (kc + 1) * 512]
                k.tt(ta, psZ[0], Kr, ALU.mult)
                k.tt(tb, psZ[1], Ki, ALU.mult)
                k.tt(Yr[:, kc * 512:(kc + 1) * 512], ta, tb, ALU.subtract)
                k.tt(ta, psZ[0], Ki, ALU.mult)
                k.tt(tb, psZ[1], Kr, ALU.mult)
                k.tt(Yi[:, kc * 512:(kc + 1) * 512], ta, tb, ALU.add)
            yo = k.sb("yoc", [128, 4 * 256], F32)
            for cc in range(4):
                p_ = psZ[cc % 2]
                for kc in range(4):
                    k.mm(p_[:, 0:256], Yr[:, kc * 512 + cc * 128:kc * 512 + (cc + 1) * 128],
                         Ccs[:, kc * 512:kc * 512 + 256], start=(kc == 0), stop=False)
                    k.mm(p_[:, 0:256], Yi[:, kc * 512 + cc * 128:kc * 512 + (cc + 1) * 128],
                         nScs[:, kc * 512:kc * 512 + 256], start=False, stop=(kc == 3))
                k.cp(yo[:, cc * 256:(cc + 1) * 256], p_[:, 0:256], eng="act")
            for t in range(2):
                k.dma(yconv[NT + t].rr("(k p) n -> p k n", p=128),
                      yo.rr("p (k t n) -> p k t n", k=4, t=2)[:, :, t, :], q="pool", nowaw=True)

    def attn_finish(o_ps, npart_q, oS, sel_s, ps_bc, rec, yb, esk=None):
        nq = npart_q
        k.cp(oS[:, 0:nq], o_ps[:, 0:nq], eng="act")
        k.mm(ps_bc[0:64, 0:nq], sel_s, oS[:, 0:nq])
        if esk is not None:
            k.tt(rec[:, 0:nq].rr("p (h t) -> p h t", h=4), ps_bc[0:64, 0:nq].rr("p (h t) -> p h t", h=4),
                 esk.us(2).bc([64, 4, nq // 4]), ALU.add)
            k.recip(rec[:, 0:nq], rec[:, 0:nq])
        else:
            k.recip(rec[:, 0:nq], ps_bc[0:64, 0:nq])
        k.tt(yb[:, 0:nq], oS[0:64, 0:nq], rec[:, 0:nq], ALU.mult)

    def phase_gqa(l, bgjob=None):
        with k.phase():
            kTs = k.sb("kTs", [128, Ltot], BF16)
            k.dma(kTs, kT)
            vAs = k.sb("vAs", [128, NTt * 130], BF16)
            vAv = vAs.rr("p (t f) -> p t f", f=130)
            k.dma(vAv, vA.rr("(t p) f -> p t f", p=128))
            msk = k.sb("msk", [128, 2, 512], BF16)
            k.dma(msk, C["masks"])
            sel_s = k.sb("sel", [65, 64], F32)
            k.dma(sel_s, C["sel"])
            sk = k.sb("sk", [64, 8], F32)
            bcast_row_n(sk, W["gqa_sink"][l], 64)
            esk = k.sb("esk", [64, 8], F32)
            k.act(esk, sk, AF.Exp)
            qb = [k.sb("qb%d" % i, [128, 512], BF16) for i in range(2)]
            psS = [k.ps("psS%d" % i, [128, 512], F32) for i in range(3)]
            psO = [k.ps("psO%d" % i, [128, 512], F32) for i in range(2)]
            psB = k.ps("psB", [128, 512], F32)
            pT = [k.sb("pT%d" % i, [128, 512], BF16) for i in range(4)]
            oS = k.sb("oS", [65, 512], F32)
            rec = k.sb("rec", [64, 512], F32)
            yb = [k.sb("yb%d" % i, [64, 512], BF16) for i in range(2)]
            bgl = []
            if bgjob is not None:
                with k.record() as bgl:
                    bgjob()
            seqs = []
            for i in range(NTt):
                if i < NT:
                    keys = [(kt, (0 if kt == i - 1 else (1 if kt == i + 1 else None)))
                            for kt in (i - 1, i, i + 1) if 0 <= kt < NT]
                    keys += [(NT, None), (NT + 1, None)]
                else:
                    keys = [(NT, None), (NT + 1, None)]
                for g in range(2):
                    seqs.append((i, g, keys))
            items = [(si, j) for si, sq_ in enumerate(seqs) for j in range(len(sq_[2]))]
            loaded = set()

            def ensure_q(i):
                if i in loaded:
                    return
                loaded.add(i)
                cols = slice(i * 128, (i + 1) * 128)
                for g in range(2):
                    k.dma(qb[i % 2][g * 64:(g + 1) * 64, :].rr("d (h t) -> d h t", h=4),
                          qT[g * 256:(g + 1) * 256, :].rr("(h d) n -> d h n", d=64)[:, :, cols], nowaw=True)

            def S(t):
                si, j = items[t]
                i, g, keys = seqs[si]
                ensure_q(i)
                kt, mk = keys[j]
                pr = slice(g * 64, (g + 1) * 64)
                k.mm(psS[t % 3], kTs[pr, kt * 128:(kt + 1) * 128], qb[i % 2][pr, :])

            for t0 in range(min(3, len(items))):
                S(t0)
            for t, (si, j) in enumerate(items):
                i, g, keys = seqs[si]
                kt, mk = keys[j]
                p_ = pT[t % 4]
                k.act(p_, psS[t % 3], AF.Exp)
                if mk is not None:
                    k.tt(p_, p_, msk[:, mk, :], ALU.mult)
                if t + 3 < len(items):
                    S(t + 3)
                o_ = psO[si % 2]
                k.mm(o_[0:65, :], vAv[:, kt, g * 65:(g + 1) * 65], p_, start=(j == 0), stop=(j == len(keys) - 1))
                if bgl:
                    k.play(bgl, 1)
                if j == len(keys) - 1:
                    y_ = yb[si % 2]
                    cols = slice(i * 128, (i + 1) * 128)
                    attn_finish(o_[0:65, :], 512, oS, sel_s, psB, rec, y_, esk=esk[:, g * 4:(g + 1) * 4])
                    k.dma(ygT[g * 256:(g + 1) * 256, :].rr("(h d) n -> d h n", d=64)[:, :, cols],
                          y_.rr("d (h t) -> d h t", h=4), q="pool", nowaw=True)
            if bgl:
                k.play(bgl)

    def bcast_row_n(dst, src_row, n):
        k.dma(dst, src_row.bc([n, src_row.shape[1]]))

    def phase_mla(l, bgjob=None):
        with k.phase():
            sel_s = k.sb("sel", [65, 64], F32)
            k.dma(sel_s, C["sel"])
            vms = [k.sb("vms%d" % i, [128, NTt * 65], BF16) for i in range(2)]
            kms = [k.sb("kms0", [96, Ltot], BF16)]
            qms = [k.sb("qms0", [96, Ltot], BF16)]
            psS = [k.ps("psS%d" % i, [128, 512], F32) for i in range(3)]
            psO = [k.ps("psO%d" % i, [128, 512], F32) for i in range(2)]
            psB = k.ps("psB", [128, 512], F32)
            pT = [k.sb("pT%d" % i, [128, 512], BF16) for i in range(4)]
            oS = k.sb("oS", [65, 512], F32)
            rec = k.sb("rec", [64, 512], F32)
            yb = [k.sb("yb%d" % i, [64, 512], BF16) for i in range(2)]
            bgl = []
            if bgjob is not None:
                bgA = k.ps("bgA", [128, 512], F32)
                bgB = k.ps("bgB", [128, 512], F32)
                with k.record() as bgl:
                    bgjob((bgA, bgB))
            n = 0
            groups = [(g * 512, 512, list(range(NTt))) for g in range(L // 512)] + [(L, LC, [NT, NT + 1])]
            for h in range(8):
                km_ = kms[0]
                qm_ = qms[0]
                vm_ = vms[h % 2]
                vmv = vm_.rr("p (t f) -> p t f", f=65)
                k.dma(km_, kmT[h * 96:(h + 1) * 96, :])
                k.dma(qm_, qmT[h * 96:(h + 1) * 96, :])
                with nc.allow_non_contiguous_dma("per-head V gather"):
                    k.dma(vmv, vmA.rr("(t p) f -> p t f", p=128)[:, :, h * 65:(h + 1) * 65])
                for (q0, nq, keys) in groups:
                    o_ = psO[n % 2]
                    y_ = yb[n % 2]
                    n += 1
                    nk = len(keys)

                    def S(j):
                        kt = keys[j]
                        k.mm(psS[j % 3][:, 0:nq], km_[:, kt * 128:(kt + 1) * 128], qm_[:, q0:q0 + nq])

                    for j0 in range(min(2, nk)):
                        S(j0)
                    for j in range(nk):
                        kt = keys[j]
                        k.act(pT[j % 4][:, 0:nq], psS[j % 3][:, 0:nq], AF.Exp)
                        if j + 2 < nk:
                            S(j + 2)
                        k.mm(o_[0:65, 0:nq], vmv[:, kt, :], pT[j % 4][:, 0:nq], start=(j == 0),
                             stop=(j == nk - 1))
                        if bgl:
                            k.play(bgl, 1)
                    attn_finish(o_[0:65, :], nq, oS, sel_s, psB, rec, y_)
                    k.dma(ymT[h * 64:(h + 1) * 64, q0:q0 + nq], y_[:, 0:nq], q="pool", nowaw=True)
            if bgl:
                k.play(bgl)

    def phase_merge(l, xs, xd, only_latent):
        with k.phase():
            wbr = k.sb("wbr", [128, 12 * D], BF16)
            k.dma(wbr.rr("p (k n) -> p k n", k=12), W["w_branch"][l].rr("(k p) n -> p k n", p=128), q="pool")
            wo = k.sb("wo", [128, KD * D], BF16)
            k.dma(wo.rr("p (k n) -> p k n", k=KD), W["w_out"][l].rr("(k p) n -> p k n", p=128), q="pool")
            GT = [k.sb("GT%d" % s, [128, D], F32) for s in range(2)]
            for s in range(2):
                k.dma(GT[s], modv[l % 2, s, 2])
            skp = k.sb("skp", [128, 4], F32)
            with nc.allow_non_contiguous_dma("tiny column load"):
                k.dma(skp, W["hy_skip"][l].rr("o (k p) -> p (o k)", p=128))
            xt = [k.sb("xt%d" % i, [128, D], F32) for i in range(2)]
            yc = [k.sb("yc%d" % i, [128, 512], F32) for i in range(2)]
            zt = [k.sb("zt%d" % i, [128, 512], BF16) for i in range(2)]
            x0 = [k.sb("x0%d" % i, [128, 512], F32) for i in range(2)]
            yT3 = [k.sb("yT3_%d" % i, [128, 12 * 128], BF16) for i in range(2)]
            gt = [k.sb("gt%d" % i, [128, 3072], BF16) for i in range(2)]
            tmp = k.sb("tmpm", [128, 512], F32)
            mg = k.sb("mg", [128, D], F32)
            mg2 = k.sb("mg2", [128, D], F32)
            mgbs = [k.sb("mgb%d" % i, [128, D], BF16) for i in range(2)]
            mg3 = k.sb("mg3", [128, D], F32)
            mTs = k.sb("mTs", [128, KD * 128], BF16)
            psT = k.ps("psT", [128, KD, 128], BF16)
            psM = [k.ps("psM%d" % i, [128, 512], F32) for i in range(6)]
            xo = [k.sb("xo%d" % i, [128, D], F32) for i in range(2)]
            npm = [0]
            prevB = []
            ntl = NT if only_latent else NTt
            for i in range(ntl):
                s = seq_of(i)
                cols = slice(i * 128, (i + 1) * 128)
                b2 = i % 2
                mgb = mgbs[b2]
                k.dma(xt[b2], xs[i * 128:(i + 1) * 128, :])
                k.dma(yc[b2].rr("p (k t) -> p k t", k=4), yconv[i].rr("(k p) t -> p k t", p=128))
                k.dma(zt[b2].rr("p (k t) -> p k t", k=4), zT.rr("(k p) n -> p k n", p=128)[:, :, cols])
                k.dma(x0[b2].rr("p (k t) -> p k t", k=4), x0T.rr("(k p) n -> p k n", p=128)[:, :, cols])
                y3 = yT3[b2]
                y3v = y3.rr("p (k t) -> p k t", k=12)
                k.dma(y3v[:, 4:8, :], ygT.rr("(k p) n -> p k n", p=128)[:, :, cols], nowaw=True)
                k.dma(y3v[:, 8:12, :], ymT.rr("(k p) n -> p k n", p=128)[:, :, cols], nowaw=True)
                k.dma(gt[b2], gts[i * 128:(i + 1) * 128, :])
                recA = k.record()
                lA = recA.__enter__()
                for cc in range(4):
                    cs = slice(cc * 128, (cc + 1) * 128)
                    k.stt(tmp[:, cs], zt[b2][:, cs], skp[:, cc:cc + 1], yc[b2][:, cs], ALU.mult, ALU.add)
                k.tt(y3[:, 0:512], tmp, x0[b2], ALU.mult)
                for br in range(3):
                    for half in range(2):
                        p_ = psM[npm[0] % 4]
                        npm[0] += 1
                        hs = slice(half * 512, (half + 1) * 512)
                        for kk in range(4):
                            k.mm(p_, y3[:, (br * 4 + kk) * 128:(br * 4 + kk + 1) * 128],
                                 wbr[:, (br * 4 + kk) * D + half * 512:(br * 4 + kk) * D + (half + 1) * 512],
                                 start=(kk == 0), stop=(kk == 3))
                        gsl = gt[b2][:, br * D + half * 512:br * D + (half + 1) * 512]
                        if br == 0:
                            k.tt(mg[:, hs], p_, gsl, ALU.mult)
                        else:
                            k.tt(mg2[:, hs], p_, gsl, ALU.mult)
                            k.tt(mg[:, hs], mg[:, hs], mg2[:, hs], ALU.add)
                k.cp(mgb, mg, eng="act")
                recA.__exit__(None, None, None)
                k.play_interleaved([lA] + ([prevB] if prevB else []))
                recB = k.record()
                prevB = recB.__enter__()
                for kk in range(KD):
                    k.tr(psT[:, kk, :], mgb[:, kk * 128:(kk + 1) * 128], ident)
                k.cp(mTs.rr("p (k t) -> p k t", k=KD), psT, eng="act")
                for half in range(2):
                    p_ = psM[4 + half]
                    hs = slice(half * 512, (half + 1) * 512)
                    for kk in range(KD):
                        k.mm(p_, mTs[:, kk * 128:(kk + 1) * 128], wo[:, kk * D + half * 512:kk * D + (half + 1) * 512],
                             start=(kk == 0), stop=(kk == KD - 1))
                    k.tt(mg3[:, hs], p_, GT[s][:, hs], ALU.mult)
                    k.tt(xo[b2][:, hs], mg3[:, hs], xt[b2][:, hs], ALU.add)
                k.dma(xd[i * 128:(i + 1) * 128, :], xo[b2], q="pool", nowaw=True)
                recB.__exit__(None, None, None)
            if prevB:
                k.play(prevB)

    def phase_mlp(l, xs, xd, only_latent):
        with k.phase():
            w1 = k.sb("w1", [128, KD * 4096], BF16)
            w1v = w1.rr("p (k n) -> p k n", k=KD)
            for kk in range(KD):
                k.dma(w1v[:, kk, :], W["w_mlp1"][l][kk * 128:(kk + 1) * 128, :], q="pool", nowaw=True)
            w2 = k.sb("w2", [128, 32 * D], BF16)
            w2v = w2.rr("p (k n) -> p k n", k=32)
            for q4 in range(4):
                k.dma(w2v[:, q4 * 8:(q4 + 1) * 8, :],
                      W["w_mlp2"][l][q4 * 1024:(q4 + 1) * 1024, :].rr("(k p) n -> p k n", p=128), q="pool", nowaw=True)
            G2 = k.sb("G2", [128, D], F32)
            SH2 = k.sb("SH2", [128, D], F32)
            GT2 = k.sb("GT2", [128, D], F32)
            xt = [k.sb("xt%d" % i, [128, D], F32) for i in range(4)]
            junk = k.sb("junk", [128, D], F32)
            hb = k.sb("hb", [128, D], BF16)
            ss = k.sb("ss", [128, 1], F32)
            t1 = k.sb("t1", [128, 1], F32)
            rs = k.sb("rs", [128, 1], F32)
            psT = k.ps("psT", [128, KD, 128], BF16)
            hT = k.sb("hTm", [128, KD * 256], BF16)
            hTv = hT.rr("p (k t) -> p k t", k=KD)
            aT = k.sb("aT", [128, 32 * 256], BF16)
            r1 = [k.sb("r1_%d" % i, [128, 512], F32) for i in range(2)]
            psH = [k.ps("psH%d" % i, [128, 512], F32) for i in range(4)]
            psO = [k.ps("psO%d" % i, [128, 512], F32) for i in range(2)]
            cur = -1
            ntl = NT if only_latent else NTt
            for i0 in range(0, ntl, 2):
                s = seq_of(i0)
                if s != cur:
                    cur = s
                    k.dma(G2, modv[l % 2, s, 4])
                    k.dma(SH2, modv[l % 2, s, 3])
                    k.dma(GT2, modv[l % 2, s, 5])
                for t in range(2):
                    i = i0 + t
                    k.dma(xt[i % 4], xs[i * 128:(i + 1) * 128, :])
                    norm_mod_T(xt[i % 4], G2, SH2, hb, psT, hTv[:, :, t * 128:(t + 1) * 128], junk, ss, t1, rs)
                for fb in range(16):
                    p_ = psH[fb % 4]
                    for r2 in range(2):
                        fc = fb * 2 + r2
                        for kk in range(KD):
                            k.mm(p_[:, r2 * 256:(r2 + 1) * 256], w1v[:, kk, fc * 128:(fc + 1) * 128],
                                 hTv[:, kk, :], start=(kk == 0), stop=(kk == KD - 1))
                    r_ = r1[fb % 2]
                    k.act(r_, p_, AF.Relu)
                    k.tt(aT[:, fb * 512:(fb + 1) * 512], r_, r_, ALU.mult)
                for t in range(2):
                    i = i0 + t
                    x_ = xt[i % 4]
                    for half in range(2):
                        p_ = psO[half]
                        hs = slice(half * 512, (half + 1) * 512)
                        for fc in range(32):
                            k.mm(p_, aT[:, fc * 256 + t * 128:fc * 256 + (t + 1) * 128], w2v[:, fc, hs],
                                 start=(fc == 0), stop=(fc == 31))
                        k.tt(junk[:, hs], p_, GT2[:, hs], ALU.mult)
                        k.tt(x_[:, hs], junk[:, hs], x_[:, hs], ALU.add)
                    k.dma(xd[i * 128:(i + 1) * 128, :], x_, q="pool", nowaw=True)

    for l in range(DEPTH):
        last = (l == DEPTH - 1)
        if l == 0:
            phase_mod(l)
        phase_proj(l, xsA)
        if l == 0:
            phase_filt(l)
        phase_hyena(l)
        if l + 1 < DEPTH:
            phase_gqa(l, bgjob=(lambda ll=l + 1: phase_mod(ll)))
        else:
            phase_gqa(l)
        if l + 1 < DEPTH:
            phase_mla(l, bgjob=(lambda bg, ll=l + 1: phase_filt(ll, bg)))
        else:
            phase_mla(l)
        phase_merge(l, xsA, xsB, last)
        phase_mlp(l, xsB, xsA, last)
    k.dma(y_out, xsA[0:L, :])
    k.barrier()
    k.es.close()
    return nc, consts, k


def make_in_maps(inputs, consts, L, DEPTH, B):
    maps = []
    shared = {}
    for nm, shp in W_SHAPES.items():
        a = np.asarray(inputs[nm], dtype=np.float32)
        shared[nm] = np.ascontiguousarray(a.reshape([DEPTH] + list(shp)))
    for nm, arr in consts.items():
        shared["k_" + nm] = np.ascontiguousarray(arr)
    x = np.asarray(inputs["x"], dtype=np.float32)
    c = np.asarray(inputs["c"], dtype=np.float32)
    ctx = np.asarray(inputs["ctx"], dtype=np.float32)
    cc = np.asarray(inputs["c_ctx"], dtype=np.float32)
    for b in range(B):
        m = dict(shared)
        m["x"] = np.ascontiguousarray(x[b])
        m["ctx"] = np.ascontiguousarray(ctx[b])
        m["c"] = np.ascontiguousarray(c[b].reshape(D, 1))
        m["c_ctx"] = np.ascontiguousarray(cc.reshape(D, 1))
        maps.append(m)
    return maps


def run(inputs, L, DEPTH, dbg=()):
    B = inputs["x"].shape[0]
    nc, consts, kb = build(L, DEPTH, dbg)
    maps = make_in_maps(inputs, consts, L, DEPTH, B)
    res = run_bass_kernel_spmd(nc, maps, core_ids=list(range(B)))
    return res


def kernel(**inputs):
    x = inputs["x"]
    B, L, _ = x.shape
    DEPTH = inputs["w_mod"].shape[0]
    res = run(inputs, L, DEPTH)
    return np.stack([np.asarray(r["y"], dtype=np.float32) for r in res.results], 0)
```

```python
import contextlib
import math
import numpy as np
import ml_dtypes
import concourse.bass as bass
import concourse.mybir as mybir
from concourse.bass_utils import run_bass_kernel_spmd

F32 = mybir.dt.float32
BF16 = mybir.dt.bfloat16
AF = mybir.ActivationFunctionType
ALU = mybir.AluOpType
AX = mybir.AxisListType

D = 1024
KD = 8
LC = 256
EPS = 1e-6
NIN = 6048
GQA_SCALE = 64 ** -0.5
MLA_SCALE = 96 ** -0.5
PI = math.pi


class Res:
    __slots__ = ("w", "r", "dsem", "name", "scoped")

    def __init__(self, name):
        self.w = {}
        self.r = {}
        self.dsem = None
        self.name = name
        self.scoped = False


class V:
    def __init__(self, ap, res):
        self.ap = ap
        self.res = res

    def __getitem__(self, idx):
        return V(self.ap[idx], self.res)

    def rr(self, pat, **kw):
        return V(self.ap.rearrange(pat, **kw), self.res)

    def bc(self, shape):
        return V(self.ap.broadcast_to(list(shape)), self.res)

    def us(self, i):
        return V(self.ap.unsqueeze(i), self.res)

    @property
    def shape(self):
        return self.ap.shape


def _merge(d, src):
    for k, v in src.items():
        if d.get(k, 0) < v:
            d[k] = v


class KB:
    def __init__(self, nc):
        self.nc = nc
        self.es = contextlib.ExitStack()
        self.sems = []
        self.semcnt = []
        self.engs = {}
        for name, h in (("pe", nc.tensor), ("act", nc.scalar), ("dve", nc.vector),
                        ("pool", nc.gpsimd), ("sp", nc.sync)):
            si = self._newsem("e_" + name)
            self.engs[name] = dict(h=h, sem=si, waited={})
        self.rings = {}
        for q in ("sp", "pool"):
            self.rings[q] = dict(sems=[self._newsem("r%s%d" % (q, i)) for i in range(24)], nxt=0)
        self.ph = None
        self.ph_res = []
        self.recq = None
        self.uid = 0
        self.ninstr = 0

    def _newsem(self, name):
        h = self.es.enter_context(self.nc.semaphore(name))
        self.sems.append(h)
        self.semcnt.append(0)
        return len(self.sems) - 1

    def dram(self, name, shape, dtype, kind="Internal"):
        t = self.nc.dram_tensor(name, list(shape), dtype, kind=kind)
        return V(t.ap(), Res(name))

    def _stack(self):
        return self.ph if self.ph is not None else self.es

    def sb(self, name, shape, dtype):
        self.uid += 1
        t = self._stack().enter_context(self.nc.sbuf_tensor("%s_%d" % (name, self.uid), list(shape), dtype))
        r = Res(name)
        if self.ph is not None:
            r.scoped = True
            self.ph_res.append(r)
        return V(t.ap(), r)

    def ps(self, name, shape, dtype):
        self.uid += 1
        t = self._stack().enter_context(self.nc.psum_tensor("%s_%d" % (name, self.uid), list(shape), dtype))
        r = Res(name)
        if self.ph is not None:
            r.scoped = True
            self.ph_res.append(r)
        return V(t.ap(), r)

    @contextlib.contextmanager
    def phase(self):
        if self.ph is not None:
            yield
            return
        self.ph = contextlib.ExitStack()
        self.ph_res = []
        try:
            yield
            self.barrier()
        finally:
            st = self.ph
            self.ph = None
            st.close()

    def _wait(self, eng, deps):
        E = self.engs[eng]
        for sem, val in deps.items():
            if eng == "pe" and sem == E["sem"]:
                continue
            if E["waited"].get(sem, 0) >= val:
                continue
            E["h"].wait_ge(self.sems[sem], val)
            E["waited"][sem] = val
            self.ninstr += 1

    def _deps(self, reads, writes, nowaw=False):
        deps = {}
        for v in reads:
            _merge(deps, v.res.w)
        for v in writes:
            if not nowaw:
                _merge(deps, v.res.w)
            _merge(deps, v.res.r)
        return deps

    def _record(self, ev, reads, writes, nowaw=False):
        sem, val = ev
        for v in reads:
            if v.res.r.get(sem, 0) < val:
                v.res.r[sem] = val
        for v in writes:
            if nowaw:
                if v.res.w.get(sem, 0) < val:
                    v.res.w[sem] = val
            else:
                v.res.w = {sem: val}
            v.res.r = {}

    @contextlib.contextmanager
    def record(self):
        prev = self.recq
        lst = []
        self.recq = lst
        try:
            yield lst
        finally:
            self.recq = prev

    def play(self, lst, n=None):
        n = len(lst) if n is None else min(n, len(lst))
        prev = self.recq
        if prev is not None:
            for _ in range(n):
                prev.append(lst.pop(0))
            return
        self.recq = None
        for _ in range(n):
            e = lst.pop(0)
            if e[0] == "op":
                self.op(*e[1:])
            else:
                self.dma(*e[1:])
        self.recq = prev

    def play_interleaved(self, lists):
        lists = [l for l in lists if l]
        tot = [len(l) for l in lists]
        done = [0] * len(lists)
        while any(lists):
            bi, bv = -1, 2.0
            for i, l in enumerate(lists):
                if l:
                    v = done[i] / tot[i]
                    if v < bv:
                        bi, bv = i, v
            self.play(lists[bi], 1)
            done[bi] += 1

    def op(self, eng, fn, reads, writes):
        if self.recq is not None:
            self.recq.append(("op", eng, fn, reads, writes))
            return
        self._wait(eng, self._deps(reads, writes))
        ins = fn()
        E = self.engs[eng]
        self.semcnt[E["sem"]] += 1
        ins.then_inc(self.sems[E["sem"]], 1)
        self.ninstr += 1
        self._record((E["sem"], self.semcnt[E["sem"]]), reads, writes)

    def dma(self, out, in_, q="sp", nowaw=False):
        if self.recq is not None:
            self.recq.append(("dma", out, in_, q, nowaw))
            return
        ring = self.rings[q]
        ds = ring["sems"][ring["nxt"] % len(ring["sems"])]
        ring["nxt"] += 1
        deps = self._deps([in_], [out], nowaw)
        if self.semcnt[ds] > 0 and deps.get(ds, 0) < self.semcnt[ds]:
            deps[ds] = self.semcnt[ds]
        self._wait(q, deps)
        with self.nc.allow_non_contiguous_dma("layout"):
            ins = self.engs[q]["h"].dma_start(out=out.ap, in_=in_.ap)
        self.semcnt[ds] += 16
        ins.then_inc(self.sems[ds], 16)
        self.ninstr += 1
        self._record((ds, self.semcnt[ds]), [in_], [out], nowaw)

    def barrier(self):
        allev = {i: c for i, c in enumerate(self.semcnt) if c > 0}
        for eng in self.engs:
            self._wait(eng, allev)

    def mm(self, out, lhsT, rhs, start=True, stop=True):
        self.op("pe", lambda: self.nc.tensor.matmul(out.ap, lhsT.ap, rhs.ap, start=start, stop=stop),
                [lhsT, rhs], [out])

    def tr(self, out, in_, ident):
        self.op("pe", lambda: self.nc.tensor.transpose(out.ap, in_.ap, ident.ap), [in_, ident], [out])

    def act(self, out, in_, func, bias=None, scale=None):
        reads = [in_]
        kw = {}
        if bias is not None:
            if isinstance(bias, V):
                reads.append(bias)
                kw["bias"] = bias.ap
            else:
                kw["bias"] = bias
        if scale is not None:
            if isinstance(scale, V):
                reads.append(scale)
                kw["scale"] = scale.ap
            else:
                kw["scale"] = scale
        self.op("act", lambda: self.nc.scalar.activation(out.ap, in_.ap, func, **kw), reads, [out])

    def tt(self, out, a, b, op, eng="dve"):
        h = self.engs[eng]["h"]
        self.op(eng, lambda: h.tensor_tensor(out.ap, a.ap, b.ap, op), [a, b], [out])

    def ts(self, out, a, s1, s2, op0, op1=None, eng="dve"):
        h = self.engs[eng]["h"]
        reads = [a]
        x1 = s1
        x2 = s2
        if isinstance(s1, V):
            reads.append(s1)
            x1 = s1.ap
        if isinstance(s2, V):
            reads.append(s2)
            x2 = s2.ap
        if op1 is None:
            self.op(eng, lambda: h.tensor_scalar(out.ap, a.ap, x1, x2, op0), reads, [out])
        else:
            self.op(eng, lambda: h.tensor_scalar(out.ap, a.ap, x1, x2, op0, op1), reads, [out])

    def stt(self, out, a, s, b, op0, op1):
        reads = [a, b]
        x = s
        if isinstance(s, V):
            reads.append(s)
            x = s.ap
        self.op("dve", lambda: self.nc.vector.scalar_tensor_tensor(out.ap, a.ap, x, b.ap, op0, op1), reads, [out])

    def cp(self, out, in_, eng="dve"):
        if eng == "act":
            self.op("act", lambda: self.nc.scalar.copy(out.ap, in_.ap), [in_], [out])
        else:
            h = self.engs[eng]["h"]
            self.op(eng, lambda: h.tensor_copy(out.ap, in_.ap), [in_], [out])

    def red(self, out, in_, op=ALU.add):
        self.op("dve", lambda: self.nc.vector.tensor_reduce(out.ap, in_.ap, AX.X, op), [in_], [out])

    def recip(self, out, in_):
        self.op("dve", lambda: self.nc.vector.reciprocal(out.ap, in_.ap), [in_], [out])

    def memset(self, out, val, eng="dve"):
        h = self.engs[eng]["h"]
        self.op(eng, lambda: h.memset(out.ap, val), [], [out])

    def rstd(self, out, ss, scale, tmp):
        self.act(tmp, ss, AF.Sqrt, bias=self.epsb[0:ss.shape[0], :], scale=scale)
        self.recip(out, tmp)


def bf(a):
    return np.ascontiguousarray(a.astype(np.float32)).astype(ml_dtypes.bfloat16)


def make_consts(L):
    Ltot = L + LC
    N = 2 * L
    P = N // 128
    c = {}

    def axial(dim):
        nf = dim // 4
        inv = (10000.0 ** (-np.arange(nf, dtype=np.float32) / nf)).astype(np.float32)
        rows = L // 64
        r = np.repeat(np.arange(rows, dtype=np.float32), 64)
        col = np.tile(np.arange(64, dtype=np.float32), rows)
        ang = np.concatenate([r[:, None] * inv, col[:, None] * inv], -1).astype(np.float32)
        return np.cos(ang).astype(np.float32), np.sin(ang).astype(np.float32)

    rope = np.zeros((Ltot, 192), np.float32)
    cq, sq = axial(64)
    cm, sm = axial(32)
    rope[:L, 0:64] = np.concatenate([cq, cq], -1)
    rope[:L, 64:128] = np.concatenate([-sq, sq], -1)
    rope[:L, 128:160] = np.concatenate([cm, cm], -1)
    rope[:L, 160:192] = np.concatenate([-sm, sm], -1)
    rope[L:, 0:64] = 1.0
    rope[L:, 128:160] = 1.0
    c["rope"] = rope
    j = np.arange(128)[:, None]
    qi = np.arange(128)[None, :]
    lo = (j >= qi).astype(np.float32)
    hi = (j <= qi).astype(np.float32)
    c["masks"] = bf(np.stack([np.tile(lo, (1, 4)), np.tile(hi, (1, 4))], 1))
    c["ident"] = bf(np.eye(128))
    c["identf"] = np.eye(128, dtype=np.float32)
    c["onesf"] = np.ones((128, 128), np.float32)
    sel = np.zeros((65, 64), np.float32)
    sel[64, :] = 1.0
    c["sel"] = sel
    n1 = np.arange(P, dtype=np.float64)
    a1 = 2 * np.pi * np.outer(n1, n1) / P
    c["F1"] = bf(np.concatenate([np.cos(a1), -np.sin(a1)], 1))
    n2 = np.arange(128, dtype=np.float64)
    atw = 2 * np.pi * np.outer(n2, n1) / N
    c["TW"] = np.stack([np.cos(atw), -np.sin(atw)], 1).astype(np.float32)
    a2 = 2 * np.pi * np.outer(n2, n2) / 128
    C2, S2 = np.cos(a2), np.sin(a2)
    c["C2"] = bf(C2)
    c["S2"] = bf(S2)
    c["nS2"] = bf(-S2)
    c["RA1"] = bf(np.concatenate([C2, S2], 1))
    c["RA2"] = bf(np.concatenate([-S2, C2], 1))
    c["TWI"] = np.stack([np.cos(atw.T), np.sin(atw.T)], 1).astype(np.float32)
    c["C1"] = bf(np.cos(a1)[:, :P // 2])
    c["nS1"] = bf(-np.sin(a1)[:, :P // 2])

    def zemb(length, pos):
        t = (pos / (length - 1)).astype(np.float64)
        w = 2 * np.pi * pos / length
        f = np.linspace(1e-4, 15, 16)
        return np.concatenate([t[None, :], np.cos(f[:, None] * w[None, :]), -np.sin(f[:, None] * w[None, :])], 0), t

    posL = np.concatenate([np.arange(L), [0], np.arange(L - 1, 0, -1)]).astype(np.float64)
    zl, tl = zemb(L, posL)
    tl = tl.copy()
    tl[L] = 1e4
    c["zembL"] = zl.astype(np.float32)
    c["tKL"] = tl.astype(np.float32)[None, :]
    posC = np.concatenate([np.arange(LC), [0], np.arange(LC - 1, 0, -1)]).astype(np.float64)
    zc, tc = zemb(LC, posC)
    tc = tc.copy()
    tc[LC] = 1e4
    c["zembC"] = zc.astype(np.float32)
    c["ntKC"] = np.ascontiguousarray((-tc).astype(np.float32).reshape(4, 128).T)
    maxd = math.log(1e-2) / 0.3
    mind = math.log(1e-2) / 1.5
    deltas = np.abs(np.linspace(mind, maxd, 512)).astype(np.float32)
    c["ndelta"] = np.ascontiguousarray((-deltas).reshape(4, 128).T)
    c["deltarow"] = deltas[None, :]
    nn = np.arange(512, dtype=np.float64)
    ac = 2 * np.pi * np.outer(nn, nn) / 512
    c["Cc"] = bf(np.cos(ac).reshape(4, 128, 512).transpose(1, 0, 2))
    c["nSc"] = bf((-np.sin(ac)).reshape(4, 128, 512).transpose(1, 0, 2))
    return c


CONST_DT = {"masks": BF16, "ident": BF16, "F1": BF16, "C2": BF16, "S2": BF16, "nS2": BF16, "RA1": BF16,
            "RA2": BF16, "C1": BF16, "nS1": BF16, "Cc": BF16, "nSc": BF16}

W_SHAPES = {
    "w_mod": (D, 6 * D), "b_mod": (1, 6 * D), "norm_mix_g": (1, D), "norm_mlp_g": (1, D), "w_in": (D, NIN),
    "hy_short_w": (3, 1536), "hy_f1_w": (33, 64), "hy_f1_b": (64, 1), "hy_f2_w": (64, 64), "hy_f2_b": (64, 1),
    "hy_sin_freq": (2, 64), "hy_f3_w": (64, 1024), "hy_skip": (1, 512), "gqa_q_norm": (1, 64),
    "gqa_k_norm": (1, 64), "gqa_sink": (1, 8), "mla_q_a_norm": (1, 384), "mla_kv_a_norm": (1, 256),
    "w_q_b": (384, 768), "w_kv_b": (256, 1024), "mla_q_norm": (1, 96), "mla_k_norm": (1, 96),
    "w_branch": (1536, D), "w_out": (D, D), "w_mlp1": (D, 4 * D), "w_mlp2": (4 * D, D),
}


def build(L, DEPTH, dbg=()):
    Ltot = L + LC
    NT = L // 128
    NTt = NT + 2
    N = 2 * L
    P = N // 128
    PH = P // 2
    nc = bass.Bass("TRN2", target_bir_lowering=False)
    k = KB(nc)
    consts = make_consts(L)

    x_in = k.dram("x", [L, D], F32, "ExternalInput")
    ctx_in = k.dram("ctx", [LC, D], F32, "ExternalInput")
    c_in = k.dram("c", [D, 1], F32, "ExternalInput")
    cc_in = k.dram("c_ctx", [D, 1], F32, "ExternalInput")
    W = {}
    for nm, shp in W_SHAPES.items():
        W[nm] = k.dram(nm, [DEPTH] + list(shp), F32, "ExternalInput")
    C = {}
    for nm, arr in consts.items():
        C[nm] = k.dram("k_" + nm, list(arr.shape), CONST_DT.get(nm, F32), "ExternalInput")
    y_out = k.dram("y", [L, D], F32, "ExternalOutput")

    def scr(name, shape, dt):
        return k.dram(name, shape, dt, "ExternalOutput" if name in dbg else "Internal")

    xsA = scr("xsA", [Ltot, D], F32)
    xsB = scr("xsB", [Ltot, D], F32)
    modv = scr("modv", [2, 2, 6, 128, D], F32)
    zT = scr("zT", [512, Ltot], BF16)
    x0T = scr("x0T", [512, Ltot], F32)
    gts = scr("gts", [Ltot, 3072], BF16)
    qT = scr("qT", [512, Ltot], BF16)
    kT = scr("kT", [128, Ltot], BF16)
    vA = scr("vA", [Ltot, 130], BF16)
    qmT = scr("qmT", [768, Ltot], BF16)
    kmT = scr("kmT", [768, Ltot], BF16)
    vmA = scr("vmA", [Ltot, 520], BF16)
    yconv = scr("yconv", [NTt, 512, 128], F32)
    ygT = scr("ygT", [512, Ltot], BF16)
    ymT = scr("ymT", [512, Ltot], BF16)
    kfil = scr("kfil", [512, N], BF16)
    KfL = scr("KfL", [2, 128, 512 * P], F32)
    KfC = scr("KfC", [2, 512, 512], F32)

    ident = k.sb("ident", [128, 128], BF16)
    identf = k.sb("identf", [128, 128], F32)
    onesf = k.sb("onesf", [128, 128], F32)
    k.epsb = k.sb("epsb", [128, 1], F32)
    sLat = k.sb("sLat", [128, KD * 128], BF16)
    sCtx = k.sb("sCtx", [128, KD * 128], BF16)
    rnlk = k.sb("rnlk", [128, 512], F32)
    k.dma(ident, C["ident"])
    k.dma(identf, C["identf"])
    k.dma(onesf, C["onesf"])
    k.memset(k.epsb, EPS)

    k.dma(xsA[0:L, :], x_in, nowaw=True)
    k.dma(xsA[L:Ltot, :], ctx_in, nowaw=True)
    with k.phase():
        ccol = k.sb("ccol", [128, 2 * KD], F32)
        scol = k.sb("scol", [128, 2 * KD], F32)
        onesb = k.sb("onesb", [128, 128], BF16)
        with nc.allow_non_contiguous_dma("tiny column load"):
            k.dma(ccol[:, 0:KD], c_in.rr("(k p) o -> p (k o)", p=128), nowaw=True)
            k.dma(ccol[:, KD:2 * KD], cc_in.rr("(k p) o -> p (k o)", p=128), nowaw=True)
        k.act(scol, ccol, AF.Silu)
        k.memset(onesb, 1.0)
        for kk in range(KD):
            k.ts(sLat[:, kk * 128:(kk + 1) * 128], onesb, scol[:, kk:kk + 1], None, ALU.mult)
            k.ts(sCtx[:, kk * 128:(kk + 1) * 128], onesb, scol[:, KD + kk:KD + kk + 1], None, ALU.mult)

    def bcast_row(dst, src_row):
        k.dma(dst, src_row.bc([128, src_row.shape[1]]))

    def seq_of(i):
        return 0 if i < NT else 1

    def is_first(i):
        return i == 0 or i == NT

    def is_last(i):
        return i == NT - 1 or i == NTt - 1

    def phase_mod(l):
        with k.phase():
            gm = k.sb("gm", [128, D], F32)
            gl = k.sb("gl", [128, D], F32)
            bcast_row(gm, W["norm_mix_g"][l])
            bcast_row(gl, W["norm_mlp_g"][l])
            wt = [k.sb("wmod%d" % i, [128, KD * 512], BF16) for i in range(2)]
            bm = [k.sb("bm%d" % i, [128, 512], F32) for i in range(2)]
            ps = [k.ps("psmod%d" % i, [128, 512], F32) for i in range(2)]
            ot = [k.sb("omod%d" % i, [128, 512], F32) for i in range(2)]
            tmp = k.sb("tmod", [128, 512], F32)
            wm = W["w_mod"][l].rr("(k p) n -> p k n", p=128)
            n = 0
            for j in range(12):
                w_ = wt[j % 2]
                k.dma(w_.rr("p (k n) -> p k n", k=KD), wm[:, :, j * 512:(j + 1) * 512], q="pool")
                b_ = bm[j % 2]
                bcast_row(b_, W["b_mod"][l][:, j * 512:(j + 1) * 512])
                vi, half = j // 2, j % 2
                hs = slice(half * 512, (half + 1) * 512)
                for st, sv in ((0, sLat), (1, sCtx)):
                    p_ = ps[n % 2]
                    o_ = ot[n % 2]
                    n += 1
                    for kk in range(KD):
                        k.mm(p_, sv[:, kk * 128:(kk + 1) * 128], w_[:, kk * 512:(kk + 1) * 512],
                             start=(kk == 0), stop=(kk == KD - 1))
                    if vi in (1, 4):
                        g_ = gm if vi == 1 else gl
                        k.stt(tmp, p_, 1.0, b_, ALU.add, ALU.add)
                        k.tt(o_, tmp, g_[:, hs], ALU.mult)
                    else:
                        k.tt(o_, p_, b_, ALU.add)
                    k.dma(modv[l % 2, st, vi, :, hs], o_, q="pool", nowaw=True)

    def norm_mod_T(xt, G, SH, hb, psT, hT_dst, junk, ss, t1, rs):
        k.act(junk, xt, AF.Square)
        k.red(ss, junk)
        k.rstd(rs, ss, 1.0 / D, t1)
        k.stt(junk, xt, rs[:, 0:1], G, ALU.mult, ALU.mult)
        k.tt(hb, junk, SH, ALU.add)
        for kk in range(KD):
            k.tr(psT[:, kk, :], hb[:, kk * 128:(kk + 1) * 128], ident)
        k.cp(hT_dst, psT, eng="act")

    def phase_proj(l, xs):
        with k.phase():
            win = k.sb("win", [128, KD * NIN], BF16)
            winv = win.rr("p (k n) -> p k n", k=KD)
            for kk in range(KD):
                k.dma(winv[:, kk, :], W["w_in"][l][kk * 128:(kk + 1) * 128, :], q="pool", nowaw=True)
            wqb = k.sb("wqb", [128, 3 * 768], BF16)
            k.dma(wqb.rr("p (k n) -> p k n", k=3), W["w_q_b"][l].rr("(k p) n -> p k n", p=128), q="pool")
            wkvb = k.sb("wkvb", [128, 2 * 1024], BF16)
            k.dma(wkvb.rr("p (k n) -> p k n", k=2), W["w_kv_b"][l].rr("(k p) n -> p k n", p=128), q="pool")
            G1 = k.sb("G1", [128, D], F32)
            SH1 = k.sb("SH1", [128, D], F32)
            gqk = k.sb("gqk", [128, 10 * 64], F32)
            gqkv = gqk.rr("p (h d) -> p h d", d=64)
            g64 = k.sb("g64", [128, 128], F32)
            bcast_row(g64[:, 0:64], W["gqa_q_norm"][l])
            bcast_row(g64[:, 64:128], W["gqa_k_norm"][l])
            for h in range(8):
                k.ts(gqkv[:, h, :], g64[:, 0:64], GQA_SCALE, None, ALU.mult)
            for h in range(2):
                k.cp(gqkv[:, 8 + h, :], g64[:, 64:128])
            gqa = k.sb("gqa", [128, 384], F32)
            gkva = k.sb("gkva", [128, 256], F32)
            mqn = k.sb("mqn", [128, 96], F32)
            mkn = k.sb("mkn", [128, 96], F32)
            bcast_row(gqa, W["mla_q_a_norm"][l])
            bcast_row(gkva, W["mla_kv_a_norm"][l])
            bcast_row(mqn, W["mla_q_norm"][l])
            bcast_row(mkn, W["mla_k_norm"][l])
            k.ts(mqn, mqn, MLA_SCALE, None, ALU.mult)
            sw = k.sb("sw", [128, 36], F32)
            with nc.allow_non_contiguous_dma("tiny column load"):
                k.dma(sw.rr("p (t c) -> p t c", t=3), W["hy_short_w"][l].rr("t (c p) -> p t c", p=128))
            hT = [k.sb("hT%d" % i, [128, KD * 130], BF16) for i in range(3)]
            hTv = [h_.rr("p (k t) -> p k t", k=KD) for h_ in hT]
            xt = [k.sb("xt%d" % i, [128, D], F32) for i in range(2)]
            rp = [k.sb("rp%d" % i, [128, 192], F32) for i in range(2)]
            junk = k.sb("junk", [128, D], F32)
            hb = k.sb("hb", [128, D], BF16)
            ss = k.sb("ss", [128, 16], F32)
            t1 = k.sb("t1", [128, 16], F32)
            rs = k.sb("rs", [128, 16], F32)
            psT = k.ps("psT", [128, KD, 128], BF16)
            psT2 = k.ps("psT2", [128, KD, 128], BF16)
            psT3 = k.ps("psT3", [128, KD, 128], BF16)
            NPA = 4
            psC2 = k.ps("psC2", [128, 512], F32)
            psA = [k.ps("psA%d" % i, [128, 512], F32) for i in range(NPA)]
            ssA = k.sb("ssA", [128, 1], F32)
            t1A = k.sb("t1A", [128, 1], F32)
            rsA = k.sb("rsA", [128, 1], F32)
            ss2 = k.sb("ss2", [128, 16], F32)
            t12 = k.sb("t12", [128, 16], F32)
            rs2 = k.sb("rs2", [128, 16], F32)
            sq2 = k.sb("sq2", [128, 768], F32)
            ucT = k.sb("ucT", [128, 12 * 128], F32)
            ucv = ucT.rr("p (c t) -> p c t", c=12)
            tmpu = k.sb("tmpu", [128, 128], F32)
            zt = k.sb("zt", [128, 4 * 128], BF16)
            pBs = [k.sb("pB%d" % i, [128, 1440], F32) for i in range(2)]
            gsb = [k.sb("gsb%d" % i, [128, 1024], BF16) for i in range(2)]
            sq = k.sb("sq", [128, 640], F32)
            qkn = k.sb("qkn", [128, 640], F32)
            qkA = sq
            qkB = k.sb("qkB", [128, 640], F32)
            qkb = k.sb("qkb", [128, 640], BF16)
            qkT = k.sb("qkT", [128, 5 * 128], BF16)
            vaug = k.sb("vaug", [128, 130], BF16)
            k.memset(vaug, 1.0)
            cqb = k.sb("cqb", [128, 640], BF16)
            cT = k.sb("cT", [128, 5 * 128], BF16)
            qmS = k.sb("qmS", [128, 768], F32)
            kvS = k.sb("kvS", [128, 1024], F32)
            qmn = k.sb("qmn", [128, 768], F32)
            qmb = k.sb("qmb", [128, 768], BF16)
            kmb = k.sb("kmb", [128, 768], BF16)
            kmn = qmn[:, 0:512]
            rA = k.sb("rA", [128, 256], F32)
            rB = k.sb("rB", [128, 256], F32)
            krg = k.sb("krg", [128, 32], F32)
            krr = k.sb("krr", [128, 32], F32)
            vmaug = k.sb("vmaug", [128, 520], BF16)
            k.memset(vmaug, 1.0)
            mT = k.sb("mT", [128, 16 * 128], BF16)
            npa = [0]

            def nextps():
                p_ = psA[npa[0] % NPA]
                npa[0] += 1
                return p_

            def stageA(i):
                s = seq_of(i)
                if is_first(i):
                    k.dma(G1, modv[l % 2, s, 1])
                    k.dma(SH1, modv[l % 2, s, 0])
                slot = i % 3
                x_ = xt[i % 2]
                k.dma(x_, xs[i * 128:(i + 1) * 128, :])
                norm_mod_T(x_, G1, SH1, hb, psT, hTv[slot][:, :, 1:129], junk, ssA, t1A, rsA)
                if is_first(i):
                    k.memset(hTv[slot][:, :, 0:1], 0.0)
                else:
                    k.cp(hTv[(i - 1) % 3][:, :, 129:130], hTv[slot][:, :, 1:2])
                if is_last(i):
                    k.memset(hTv[slot][:, :, 129:130], 0.0)
                else:
                    k.cp(hTv[(i + 1) % 3][:, :, 0:1], hTv[slot][:, :, 128:129])

            def stageB(i):
                pB = pBs[i % 2]
                slot = i % 3
                hv = hTv[slot]
                cols = slice(i * 128, (i + 1) * 128)
                r_ = rp[i % 2]
                k.dma(r_, C["rope"][i * 128:(i + 1) * 128, :])
                for b4 in range(4):
                    p_ = nextps()
                    for r3 in range(3):
                        cc = b4 * 3 + r3
                        for kk in range(KD):
                            k.mm(p_[:, r3 * 130:(r3 + 1) * 130], winv[:, kk, cc * 128:(cc + 1) * 128],
                                 hv[:, kk, 0:130], start=(kk == 0), stop=(kk == KD - 1))
                    for r3 in range(3):
                        cc = b4 * 3 + r3
                        u_ = p_[:, r3 * 130:(r3 + 1) * 130]
                        k.ts(tmpu, u_[:, 0:128], sw[:, cc:cc + 1], None, ALU.mult)
                        k.stt(tmpu, u_[:, 1:129], sw[:, 12 + cc:13 + cc], tmpu, ALU.mult, ALU.add)
                        k.stt(ucv[:, cc, :], u_[:, 2:130], sw[:, 24 + cc:25 + cc], tmpu, ALU.mult, ALU.add)
                k.tt(zt, ucT[:, 512:1024], ucT[:, 1024:1536], ALU.mult)
                k.dma(zT.rr("(k p) n -> p k n", p=128)[:, :, cols], zt.rr("p (k t) -> p k t", k=4), q="pool",
                      nowaw=True)
                k.dma(x0T.rr("(k p) n -> p k n", p=128)[:, :, cols], ucT[:, 0:512].rr("p (k t) -> p k t", k=4),
                      q="pool", nowaw=True)
                for (o0, wd) in ((0, 512), (512, 512), (1024, 416)):
                    p_ = nextps()
                    for kk in range(KD):
                        k.mm(p_[:, 0:wd], hv[:, kk, 1:129], winv[:, kk, 1536 + o0:1536 + o0 + wd],
                             start=(kk == 0), stop=(kk == KD - 1))
                    k.cp(pB[:, o0:o0 + wd], p_[:, 0:wd], eng="act")
                for gc in range(6):
                    g_ = gsb[(gc // 2) % 2]
                    p_ = nextps()
                    for kk in range(KD):
                        k.mm(p_, hv[:, kk, 1:129], winv[:, kk, 2976 + gc * 512:2976 + (gc + 1) * 512],
                             start=(kk == 0), stop=(kk == KD - 1))
                    k.act(g_[:, (gc % 2) * 512:(gc % 2 + 1) * 512], p_, AF.Sigmoid)
                    if gc % 2 == 1:
                        k.dma(gts[i * 128:(i + 1) * 128, (gc // 2) * 1024:(gc // 2 + 1) * 1024], g_, q="pool",
                              nowaw=True)
            def chain1(i):
                pB = pBs[i % 2]
                cols = slice(i * 128, (i + 1) * 128)
                r_ = rp[i % 2]
                psT = psT2
                k.act(sq[:, 0:640], pB[:, 0:640], AF.Square)
                k.red(ss[:, 0:10], sq[:, 0:640].rr("p (h d) -> p h d", d=64))
                k.rstd(rs[:, 0:10], ss[:, 0:10], 1.0 / 64, t1[:, 0:10])
                qv = qkn.rr("p (h d) -> p h d", d=64)
                k.tt(qv, pB[:, 0:640].rr("p (h d) -> p h d", d=64), rs[:, 0:10].us(2).bc([128, 10, 64]), ALU.mult)
                k.tt(qkn, qkn, gqk, ALU.mult)
                Av = qkA.rr("p (h d) -> p h d", d=64)
                Bv = qkB.rr("p (h d) -> p h d", d=64)
                k.tt(Av, qv, r_[:, 0:64].us(1).bc([128, 10, 64]), ALU.mult)
                k.tt(Bv[:, :, 0:32], qv[:, :, 32:64], r_[:, 64:96].us(1).bc([128, 10, 32]), ALU.mult)
                k.tt(Bv[:, :, 32:64], qv[:, :, 0:32], r_[:, 96:128].us(1).bc([128, 10, 32]), ALU.mult)
                k.tt(qkb, qkA, qkB, ALU.add)
                for t in range(5):
                    k.tr(psT[:, t, :], qkb[:, t * 128:(t + 1) * 128], ident)
                k.cp(qkT.rr("p (k t) -> p k t", k=5), psT[:, 0:5, :], eng="act")
                k.dma(qT.rr("(k p) n -> p k n", p=128)[:, :, cols], qkT[:, 0:512].rr("p (k t) -> p k t", k=4),
                      q="pool", nowaw=True)
                k.dma(kT[:, cols], qkT[:, 512:640], q="pool", nowaw=True)
                k.cp(vaug.rr("p (g f) -> p g f", g=2)[:, :, 0:64], pB[:, 640:768].rr("p (g f) -> p g f", g=2))
                k.dma(vA[i * 128:(i + 1) * 128, :], vaug, q="pool", nowaw=True)
            def chain2(i):
                pB = pBs[i % 2]
                cols = slice(i * 128, (i + 1) * 128)
                r_ = rp[i % 2]
                psT = psT3
                sq, ss, rs, t1 = sq2, ss2, rs2, t12
                k.act(sq[:, 0:640], pB[:, 768:1408], AF.Square)
                k.red(ss[:, 10:11], sq[:, 0:384])
                k.red(ss[:, 11:12], sq[:, 384:640])
                k.rstd(rs[:, 10:11], ss[:, 10:11], 1.0 / 384, t1[:, 10:11])
                k.rstd(rs[:, 11:12], ss[:, 11:12], 1.0 / 256, t1[:, 11:12])
                k.stt(cqb[:, 0:384], pB[:, 768:1152], rs[:, 10:11], gqa, ALU.mult, ALU.mult)
                k.stt(cqb[:, 384:640], pB[:, 1152:1408], rs[:, 11:12], gkva, ALU.mult, ALU.mult)
                for t in range(5):
                    k.tr(psT[:, t, :], cqb[:, t * 128:(t + 1) * 128], ident)
                k.cp(cT.rr("p (k t) -> p k t", k=5), psT[:, 0:5, :], eng="act")
                for (o0, wd) in ((0, 512), (512, 256)):
                    p_ = psC2
                    for kk in range(3):
                        k.mm(p_[:, 0:wd], cT[:, kk * 128:(kk + 1) * 128], wqb[:, kk * 768 + o0:kk * 768 + o0 + wd],
                             start=(kk == 0), stop=(kk == 2))
                    k.cp(qmS[:, o0:o0 + wd], p_[:, 0:wd], eng="act")
                for o0 in (0, 512):
                    p_ = psC2
                    for kk in range(2):
                        k.mm(p_, cT[:, (3 + kk) * 128:(4 + kk) * 128], wkvb[:, kk * 1024 + o0:kk * 1024 + o0 + 512],
                             start=(kk == 0), stop=(kk == 1))
                    k.cp(kvS[:, o0:o0 + 512], p_, eng="act")
                k.act(sq[:, 0:768], qmS, AF.Square)
                k.red(ss[:, 0:8], sq[:, 0:768].rr("p (h d) -> p h d", d=96))
                k.rstd(rs[:, 0:8], ss[:, 0:8], 1.0 / 96, t1[:, 0:8])
                qmv = qmn.rr("p (h d) -> p h d", d=96)
                qbv = qmb.rr("p (h d) -> p h d", d=96)
                k.tt(qmv, qmS.rr("p (h d) -> p h d", d=96), rs[:, 0:8].us(2).bc([128, 8, 96]), ALU.mult)
                k.tt(qmv, qmv, mqn.us(1).bc([128, 8, 96]), ALU.mult)
                k.cp(qbv[:, :, 0:64], qmv[:, :, 0:64])
                rAv = rA.rr("p (h d) -> p h d", d=32)
                rBv = rB.rr("p (h d) -> p h d", d=32)
                k.tt(rAv, qmv[:, :, 64:96], r_[:, 128:160].us(1).bc([128, 8, 32]), ALU.mult)
                k.tt(rBv[:, :, 0:16], qmv[:, :, 80:96], r_[:, 160:176].us(1).bc([128, 8, 16]), ALU.mult)
                k.tt(rBv[:, :, 16:32], qmv[:, :, 64:80], r_[:, 176:192].us(1).bc([128, 8, 16]), ALU.mult)
                k.tt(qbv[:, :, 64:96], rAv, rBv, ALU.add)
                kvv = kvS.rr("p (h d) -> p h d", d=128)
                k.act(sq[:, 0:512].rr("p (h d) -> p h d", d=64), kvv[:, :, 0:64], AF.Square)
                k.red(ss[:, 0:8], sq[:, 0:512].rr("p (h d) -> p h d", d=64))
                k.act(sq[:, 512:544], pB[:, 1408:1440], AF.Square)
                k.red(ss[:, 8:9], sq[:, 512:544])
                k.ts(ss[:, 0:8], ss[:, 0:8], ss[:, 8:9], None, ALU.add)
                k.rstd(rs[:, 0:8], ss[:, 0:8], 1.0 / 96, t1[:, 0:8])
                knv = kmn.rr("p (h d) -> p h d", d=64)
                kbv = kmb.rr("p (h d) -> p h d", d=96)
                k.tt(knv, kvv[:, :, 0:64], rs[:, 0:8].us(2).bc([128, 8, 64]), ALU.mult)
                k.tt(kbv[:, :, 0:64], knv, mkn[:, 0:64].us(1).bc([128, 8, 64]), ALU.mult)
                k.tt(krg, pB[:, 1408:1440], mkn[:, 64:96], ALU.mult)
                k.tt(rA[:, 0:32], krg, r_[:, 128:160], ALU.mult)
                k.tt(rB[:, 0:16], krg[:, 16:32], r_[:, 160:176], ALU.mult)
                k.tt(rB[:, 16:32], krg[:, 0:16], r_[:, 176:192], ALU.mult)
                k.tt(krr, rA[:, 0:32], rB[:, 0:32], ALU.add)
                k.tt(kbv[:, :, 64:96], krr.us(1).bc([128, 8, 32]), rs[:, 0:8].us(2).bc([128, 8, 32]), ALU.mult)
                k.cp(vmaug.rr("p (h f) -> p h f", h=8)[:, :, 0:64], kvv[:, :, 64:128])
                k.dma(vmA[i * 128:(i + 1) * 128, :], vmaug, q="pool", nowaw=True)
                mTv = mT.rr("p (k t) -> p k t", k=16)
                for h in range(8):
                    k.tr(psT[0:96, h, :], qmb[:, h * 96:(h + 1) * 96], ident)
                k.cp(mTv[0:96, 0:8, :], psT[0:96, :, :], eng="act")
                for h in range(8):
                    k.tr(psT[0:96, h, :], kmb[:, h * 96:(h + 1) * 96], ident)
                k.cp(mTv[0:96, 8:16, :], psT[0:96, :, :], eng="act")
                k.dma(qmT.rr("(h d) n -> d h n", d=96)[:, :, cols], mTv[0:96, 0:8, :], q="pool", nowaw=True)
                k.dma(kmT.rr("(h d) n -> d h n", d=96)[:, :, cols], mTv[0:96, 8:16, :], q="pool", nowaw=True)

            stageA(0)
            prev = []
            for i in range(1, NTt + 1):
                if i < NTt:
                    stageA(i)
                with k.record() as r1:
                    stageB(i - 1)
                k.play_interleaved([r1] + prev)
                with k.record() as c1:
                    chain1(i - 1)
                with k.record() as c2:
                    chain2(i - 1)
                prev = [c1, c2]
            k.play_interleaved(prev)

    def sin_wrapped(dst, ps_in, bcol, fcol, a, m, npart, n):
        k.ts(a[0:npart, 0:n], ps_in, bcol, fcol, ALU.add, ALU.mult)
        k.ts(m[0:npart, 0:n], a[0:npart, 0:n], PI, None, ALU.is_gt)
        k.stt(dst, m[0:npart, 0:n], -2 * PI, a[0:npart, 0:n], ALU.mult, ALU.add)
        k.ts(m[0:npart, 0:n], a[0:npart, 0:n], -PI, None, ALU.is_lt)
        k.stt(dst, m[0:npart, 0:n], 2 * PI, dst, ALU.mult, ALU.add)
        k.act(dst, dst, AF.Sin)

    def fft_fwd(F1s, TWs, C2s, S2s, nS2s, src, nrows, ncb, consume, psS1=None, psX=None, consume_part=None,
                staged=None):
        QW = 4 * P
        xin = [k.sb("xin%d" % i, [128, ncb * 128], BF16) for i in range(2)]
        if psS1 is None:
            psS1 = [k.ps("psS1_%d" % i, [128, 2 * 2 * P], F32) for i in range(2)]
        if psX is None:
            psX = [k.ps("psX%d" % i, [128, QW], F32) for i in range(2)]
        m1 = k.sb("m1", [128, 2 * 2 * P], F32)
        m2 = k.sb("m2", [128, 2 * 2 * P], F32)
        Bri = [k.sb("Bri%d" % i, [128, 2 * QW], BF16) for i in range(2)]
        nq = 0
        pend = []

        def flush_iter(st1):
            lists = [st1] if st1 else []
            for sidx in range(3):
                qi = len(pend) - 1 - (sidx + 1) + (1 if st1 is None else 0)
                if 0 <= qi < len(pend) and pend[qi][sidx]:
                    lists.append(pend[qi][sidx])
            k.play_interleaved(lists)

        for cb in range(512 // ncb):
            xi = xin[cb % 2]
            xv = xi.rr("p (c t) -> p c t", c=ncb)
            with nc.allow_non_contiguous_dma("fft gather"):
                k.dma(xv[0:nrows], src[cb * ncb:(cb + 1) * ncb, 0:nrows * 128].rr("c (a t) -> a c t", t=128))
            for q4 in range(ncb // 4):
                B = Bri[nq % 2]
                Bv = B.rr("p (r c k) -> p r c k", r=2, c=4)
                if staged is not None:
                    k_rec = k.record()
                    st1 = k_rec.__enter__()
                for half in range(2):
                    p_ = psS1[half % len(psS1)][:, 0:2 * 2 * P]
                    pv = p_.rr("p (c r k) -> p c r k", c=2, r=2)
                    for ci in range(2):
                        cl = q4 * 4 + half * 2 + ci
                        k.mm(p_[:, ci * 2 * P:(ci + 1) * 2 * P], xv[0:nrows, cl, :], F1s[0:nrows, :])
                    m1v = m1.rr("p (c r k) -> p c r k", c=2, r=2)
                    m2v = m2.rr("p (c r k) -> p c r k", c=2, r=2)
                    k.tt(m1v, pv, TWs[:, 0, :].us(1).us(1).bc([128, 2, 2, P]), ALU.mult)
                    k.tt(m2v, pv, TWs[:, 1, :].us(1).us(1).bc([128, 2, 2, P]), ALU.mult)
                    k.tt(Bv[:, 0, half * 2:half * 2 + 2, :], m1v[:, :, 0, :], m2v[:, :, 1, :], ALU.subtract)
                    k.tt(Bv[:, 1, half * 2:half * 2 + 2, :], m2v[:, :, 0, :], m1v[:, :, 1, :], ALU.add)
                Br = B[:, 0:QW]
                Bi = B[:, QW:2 * QW]
                if staged is not None:
                    k_rec.__exit__(None, None, None)
                    with k.record() as st2:
                        k.mm(psX[0][:, 0:QW], C2s, Br, start=True, stop=False)
                        k.mm(psX[0][:, 0:QW], S2s, Bi, start=False, stop=True)
                        k.mm(psX[1][:, 0:QW], C2s, Bi, start=True, stop=False)
                        k.mm(psX[1][:, 0:QW], nS2s, Br, start=False, stop=True)
                    parts = staged(cb * (ncb // 4) + q4, psX[0][:, 0:QW], psX[1][:, 0:QW])
                    st2.extend(parts[0])
                    pend.append([st2, parts[1], parts[2]])
                    flush_iter(st1)
                    nq += 1
                    continue
                if len(psX) == 2:
                    k.mm(psX[0][:, 0:QW], C2s, Br, start=True, stop=False)
                    k.mm(psX[0][:, 0:QW], S2s, Bi, start=False, stop=True)
                    k.mm(psX[1][:, 0:QW], C2s, Bi, start=True, stop=False)
                    k.mm(psX[1][:, 0:QW], nS2s, Br, start=False, stop=True)
                    consume(cb * (ncb // 4) + q4, psX[0][:, 0:QW], psX[1][:, 0:QW])
                else:
                    k.mm(psX[0][:, 0:QW], C2s, Br, start=True, stop=False)
                    k.mm(psX[0][:, 0:QW], S2s, Bi, start=False, stop=True)
                    consume_part(cb * (ncb // 4) + q4, 0, psX[0][:, 0:QW])
                    k.mm(psX[0][:, 0:QW], C2s, Bi, start=True, stop=False)
                    k.mm(psX[0][:, 0:QW], nS2s, Br, start=False, stop=True)
                    consume_part(cb * (ncb // 4) + q4, 1, psX[0][:, 0:QW])
                nq += 1
        if staged is not None:
            while any(l for p_ in pend for l in p_):
                pend.append([[], [], []])
                flush_iter([])

    def phase_filt(l, bg=None):
        with k.phase():
            f1w = k.sb("f1w", [33, 64], F32)
            f2w = k.sb("f2w", [64, 64], F32)
            f3w = k.sb("f3w", [64, 1024], F32)
            b1 = k.sb("b1", [64, 1], F32)
            b2 = k.sb("b2", [64, 1], F32)
            fq = k.sb("fq", [64, 2], F32)
            k.dma(f1w, W["hy_f1_w"][l])
            k.dma(f2w, W["hy_f2_w"][l])
            k.dma(f3w, W["hy_f3_w"][l])
            k.dma(b1, W["hy_f1_b"][l])
            k.dma(b2, W["hy_f2_b"][l])
            with nc.allow_non_contiguous_dma("tiny column load"):
                k.dma(fq, W["hy_sin_freq"][l].rr("t j -> j t"))
            ndl = k.sb("ndl", [128, 4], F32)
            k.dma(ndl, C["ndelta"])
            ze = [k.sb("ze%d" % i, [33, 512], F32) for i in range(2)]
            tr_ = [k.sb("trow%d" % i, [128, 512], F32) for i in range(2)]
            a_ = k.sb("fa", [128, 512], F32)
            m_ = k.sb("fm", [128, 512], F32)
            h1 = k.sb("h1", [64, 512], F32)
            h2 = k.sb("h2", [64, 512], F32)
            dec = k.sb("dec", [128, 512], F32)
            kf32 = k.sb("kf32", [128, 512], F32)
            kb16 = [k.sb("kb16_%d" % i, [128, 4 * 512], BF16) for i in range(2)]
            sqj = k.sb("sqj", [128, 512], F32)
            NCH = N // 512
            ssall = k.sb("ssall", [128, 4 * NCH], F32)
            ssv = ssall.rr("p (c n) -> p c n", c=4)
            if bg is None:
                psf = [k.ps("psf%d" % i, [128, 512], F32) for i in range(4)]
            else:
                psf = [bg[0], bg[0], bg[1], bg[1]]
            for ch in range(NCH):
                z_ = ze[ch % 2]
                t_ = tr_[ch % 2]
                k.dma(z_, C["zembL"][:, ch * 512:(ch + 1) * 512])
                bcast_row(t_, C["tKL"][:, ch * 512:(ch + 1) * 512])
                k.mm(psf[0][0:64, :], f1w, z_)
                sin_wrapped(h1, psf[0][0:64, :], b1[:, 0:1], fq[:, 0:1], a_, m_, 64, 512)
                k.mm(psf[1][0:64, :], f2w, h1)
                sin_wrapped(h2, psf[1][0:64, :], b2[:, 0:1], fq[:, 1:2], a_, m_, 64, 512)
                dr = 0 if ch * 512 < L else 1
                kb_ = kb16[ch % 2]
                for cc in range(4):
                    p_ = psf[2 + cc % 2]
                    k.mm(p_, f3w[:, dr * 512 + cc * 128:dr * 512 + (cc + 1) * 128], h2)
                    k.act(dec, t_, AF.Exp, scale=ndl[:, cc:cc + 1])
                    k.tt(kf32, p_, dec, ALU.mult)
                    k.act(sqj, kf32, AF.Square)
                    k.red(ssv[:, cc, ch:ch + 1], sqj)
                    k.cp(kb_[:, cc * 512:(cc + 1) * 512], kf32)
                k.dma(kfil.rr("(k p) n -> p k n", p=128)[:, :, ch * 512:(ch + 1) * 512],
                      kb_.rr("p (k n) -> p k n", k=4), q="pool", nowaw=True)
            sscol = k.sb("sscol", [128, 4], F32)
            k.red(sscol, ssv)
            dg = k.sb("dg", [128, 128], F32)
            rnl = k.sb("rnl", [128, 512], F32)
            rt = k.sb("rt", [128, 512], F32)
            for cc in range(4):
                k.ts(dg, identf, sscol[:, cc:cc + 1], None, ALU.mult)
                k.mm(psf[0][:, cc * 128:(cc + 1) * 128], onesf, dg)
            k.act(rt, psf[0], AF.Sqrt, bias=k.epsb, scale=1.0)
            k.recip(rnl, rt)
            k.ts(rnl, rnl, 1.0 / N, None, ALU.mult)
            k.cp(rnlk, rnl)
        with k.phase():
            F1s = k.sb("F1s", [P, 2 * P], BF16)
            TWs = k.sb("TWs", [128, 2, P], F32)
            C2s = k.sb("C2s", [128, 128], BF16)
            S2s = k.sb("S2s", [128, 128], BF16)
            nS2s = k.sb("nS2s", [128, 128], BF16)
            k.dma(F1s, C["F1"])
            k.dma(TWs, C["TW"])
            k.dma(C2s, C["C2"])
            k.dma(S2s, C["S2"])
            k.dma(nS2s, C["nS2"])
            ko = [k.sb("ko%d" % i, [128, 2 * 4 * P], F32) for i in range(2)]

            def consume_part(q, r, X):
                o = ko[q % 2]
                ov = o.rr("p (r c k) -> p r c k", r=2, c=4)
                rb = rnlk[:, q * 4:(q + 1) * 4].us(2).bc([128, 4, P])
                k.tt(ov[:, r], X.rr("p (c k) -> p c k", c=4), rb, ALU.mult)
                if r == 1:
                    k.dma(KfL.rr("r p m -> p r m")[:, :, q * 4 * P:(q + 1) * 4 * P], o.rr("p (r m) -> p r m", r=2),
                          q="pool", nowaw=True)

            def consume(q, Xr, Xi):
                consume_part(q, 0, Xr)
                consume_part(q, 1, Xi)

            if bg is None:
                fft_fwd(F1s, TWs, C2s, S2s, nS2s, kfil, P, 64, consume)
            else:
                fft_fwd(F1s, TWs, C2s, S2s, nS2s, kfil, P, 32, consume, psS1=[bg[0]], psX=[bg[1]],
                        consume_part=consume_part)
        with k.phase():
            f1w = k.sb("f1w", [33, 64], F32)
            f2w = k.sb("f2w", [64, 64], F32)
            f3w = k.sb("f3w", [64, 1024], F32)
            b1 = k.sb("b1", [64, 1], F32)
            b2 = k.sb("b2", [64, 1], F32)
            fq = k.sb("fq", [64, 2], F32)
            k.dma(f1w, W["hy_f1_w"][l])
            k.dma(f2w, W["hy_f2_w"][l])
            k.dma(f3w, W["hy_f3_w"][l])
            k.dma(b1, W["hy_f1_b"][l])
            k.dma(b2, W["hy_f2_b"][l])
            with nc.allow_non_contiguous_dma("tiny column load"):
                k.dma(fq, W["hy_sin_freq"][l].rr("t j -> j t"))
            ntk = k.sb("ntk", [128, 4], F32)
            k.dma(ntk, C["ntKC"])
            drow = k.sb("drow", [128, 512], F32)
            bcast_row(drow, C["deltarow"])
            Ccs = k.sb("Ccs", [128, 4 * 512], BF16)
            nScs = k.sb("nScs", [128, 4 * 512], BF16)
            k.dma(Ccs.rr("p (k n) -> p k n", k=4), C["Cc"])
            k.dma(nScs.rr("p (k n) -> p k n", k=4), C["nSc"])
            ze = k.sb("ze", [33, 512], F32)
            k.dma(ze, C["zembC"])
            a_ = k.sb("fa", [128, 512], F32)
            m_ = k.sb("fm", [128, 512], F32)
            h1 = k.sb("h1", [64, 512], F32)
            h2 = k.sb("h2", [64, 512], F32)
            if bg is None:
                psf = [k.ps("psf%d" % i, [128, 512], F32) for i in range(4)]
            else:
                psf = [bg[0], bg[0], bg[1], bg[1]]
            k.mm(psf[0][0:64, :], f1w, ze)
            sin_wrapped(h1, psf[0][0:64, :], b1[:, 0:1], fq[:, 0:1], a_, m_, 64, 512)
            k.mm(psf[1][0:64, :], f2w, h1)
            sin_wrapped(h2, psf[1][0:64, :], b2[:, 0:1], fq[:, 1:2], a_, m_, 64, 512)
            dec = k.sb("dec", [128, 512], F32)
            kf32 = k.sb("kf32", [128, 4 * 512], F32)
            ksq = k.sb("ksq", [128, 4 * 512], F32)
            kb16 = k.sb("kb16", [128, 4 * 512], BF16)
            for nch in range(4):
                dr = 0 if nch < 2 else 1
                p_ = psf[2 + nch % 2]
                k.mm(p_, h2[:, nch * 128:(nch + 1) * 128], f3w[:, dr * 512:(dr + 1) * 512])
                k.act(dec, drow, AF.Exp, scale=ntk[:, nch:nch + 1])
                k.tt(kf32[:, nch * 512:(nch + 1) * 512], p_, dec, ALU.mult)
            k.act(ksq, kf32, AF.Square)
            k.cp(kb16, kf32)
            for nch in range(4):
                k.mm(psf[0], onesf, ksq[:, nch * 512:(nch + 1) * 512], start=(nch == 0), stop=(nch == 3))
            rt = k.sb("rt", [128, 512], F32)
            rnc = k.sb("rnc", [128, 512], F32)
            k.act(rt, psf[0], AF.Sqrt, bias=k.epsb, scale=1.0)
            k.recip(rnc, rt)
            k.ts(rnc, rnc, 1.0 / 512, None, ALU.mult)
            ko = [k.sb("ko%d" % i, [128, 512], F32) for i in range(2)]
            n = 0
            for kc in range(4):
                for r, tab in ((0, Ccs), (1, nScs)):
                    p_ = psf[2 + n % 2]
                    o = ko[n % 2]
                    n += 1
                    for nch in range(4):
                        k.mm(p_, tab[:, nch * 512 + kc * 128:nch * 512 + (kc + 1) * 128],
                             kb16[:, nch * 512:(nch + 1) * 512], start=(nch == 0), stop=(nch == 3))
                    k.tt(o, p_, rnc, ALU.mult)
                    k.dma(KfC[r, kc * 128:(kc + 1) * 128, :], o, q="pool", nowaw=True)

    def phase_hyena(l):
        with k.phase():
            F1s = k.sb("F1s", [P, 2 * P], BF16)
            TWs = k.sb("TWs", [128, 2, P], F32)
            C2s = k.sb("C2s", [128, 128], BF16)
            S2s = k.sb("S2s", [128, 128], BF16)
            nS2s = k.sb("nS2s", [128, 128], BF16)
            RA1 = k.sb("RA1", [128, 256], BF16)
            RA2 = k.sb("RA2", [128, 256], BF16)
            TWI = k.sb("TWI", [P, 2, 128], F32)
            C1s = k.sb("C1s", [P, PH], BF16)
            nS1s = k.sb("nS1s", [P, PH], BF16)
            for t_, nm in ((F1s, "F1"), (TWs, "TW"), (C2s, "C2"), (S2s, "S2"), (nS2s, "nS2"), (RA1, "RA1"),
                           (RA2, "RA2"), (TWI, "TWI"), (C1s, "C1"), (nS1s, "nS1")):
                k.dma(t_, C[nm])
            QW = 4 * P
            kfq = [k.sb("kfq%d" % i, [128, 2 * QW], F32) for i in range(2)]
            ta = k.sb("ta", [128, QW], F32)
            tb = k.sb("tb", [128, QW], F32)
            Yri = [k.sb("Yri%d" % i, [128, 2 * QW], BF16) for i in range(2)]
            psC = [k.ps("psC%d" % i, [128, 512], F32) for i in range(2)]
            psY = k.ps("psY", [128, 512], F32)
            n1 = k.sb("n1", [128, 512], F32)
            n2 = k.sb("n2", [128, 512], F32)
            Dri = [k.sb("Dri%d" % i, [128, 2 * 512], BF16) for i in range(2)]
            yo = [k.sb("yo%d" % i, [128, 512], F32) for i in range(2)]

            def consume(q, Xr, Xi):
                kq = kfq[q % 2]
                recA = k.record()
                lA = recA.__enter__()
                k.dma(kq.rr("p (r m) -> p r m", r=2), KfL.rr("r p m -> p r m")[:, :, q * QW:(q + 1) * QW])
                Kr = kq[:, 0:QW]
                Ki = kq[:, QW:2 * QW]
                Y = Yri[q % 2]
                k.tt(ta, Xr, Kr, ALU.mult)
                k.tt(tb, Xi, Ki, ALU.mult)
                k.tt(Y[:, 0:QW], ta, tb, ALU.subtract)
                k.tt(ta, Xr, Ki, ALU.mult)
                k.tt(tb, Xi, Kr, ALU.mult)
                k.tt(Y[:, QW:2 * QW], ta, tb, ALU.add)
                recA.__exit__(None, None, None)
                recB = k.record()
                lB = recB.__enter__()
                Dt = Dri[q % 2]
                Dv = Dt.rr("p (r c t) -> p r c t", r=2, c=4)
                for half in range(2):
                    p_ = psC[half]
                    for ci in range(2):
                        cl = half * 2 + ci
                        k.mm(p_[0:P, ci * 256:(ci + 1) * 256], Y[:, cl * P:(cl + 1) * P], RA1, start=True, stop=False)
                        k.mm(p_[0:P, ci * 256:(ci + 1) * 256], Y[:, QW + cl * P:QW + (cl + 1) * P], RA2,
                             start=False, stop=True)
                    pv = p_[0:P, :].rr("p (c r t) -> p c r t", c=2, r=2)
                    n1v = n1[0:P, :].rr("p (c r t) -> p c r t", c=2, r=2)
                    n2v = n2[0:P, :].rr("p (c r t) -> p c r t", c=2, r=2)
                    k.tt(n1v, pv, TWI[:, 0, :].us(1).us(1).bc([P, 2, 2, 128]), ALU.mult)
                    k.tt(n2v, pv, TWI[:, 1, :].us(1).us(1).bc([P, 2, 2, 128]), ALU.mult)
                    k.tt(Dv[0:P, 0, half * 2:half * 2 + 2, :], n1v[:, :, 0, :], n2v[:, :, 1, :], ALU.subtract)
                    k.tt(Dv[0:P, 1, half * 2:half * 2 + 2, :], n2v[:, :, 0, :], n1v[:, :, 1, :], ALU.add)
                recB.__exit__(None, None, None)
                recC = k.record()
                lC = recC.__enter__()
                k.mm(psY[0:PH, :], C1s, Dt[0:P, 0:512], start=True, stop=False)
                k.mm(psY[0:PH, :], nS1s, Dt[0:P, 512:1024], start=False, stop=True)
                o = yo[q % 2]
                k.cp(o[0:PH, :], psY[0:PH, :], eng="act")
                k.dma(yconv[0:PH, q * 4:(q + 1) * 4, :], o[0:PH, :].rr("p (c t) -> p c t", c=4), q="pool", nowaw=True)
                recC.__exit__(None, None, None)
                return [lA, lB, lC]

            fft_fwd(F1s, TWs, C2s, S2s, nS2s, zT, PH, 64, None, staged=consume)
        with k.phase():
            Ccs = k.sb("Ccs", [128, 4 * 512], BF16)
            nScs = k.sb("nScs", [128, 4 * 512], BF16)
            k.dma(Ccs.rr("p (k n) -> p k n", k=4), C["Cc"])
            k.dma(nScs.rr("p (k n) -> p k n", k=4), C["nSc"])
            kfc = k.sb("kfc", [128, 2 * 4 * 512], F32)
            k.dma(kfc.rr("p (r k c) -> p r k c", r=2, k=4), KfC.rr("r (k p) c -> p r k c", p=128))
            zc = k.sb("zc", [128, 4 * 256], BF16)
            k.dma(zc.rr("p (k t) -> p k t", k=4), zT.rr("(k p) n -> p k n", p=128)[:, :, L:Ltot])
            psT = k.ps("psTc", [128, 8, 128], BF16)
            for t in range(2):
                for cc in range(4):
                    k.tr(psT[:, t * 4 + cc, :], zc[:, cc * 256 + t * 128:cc * 256 + (t + 1) * 128], ident)
            ztm = k.sb("ztm", [128, 2 * 512], BF16)
            k.cp(ztm.rr("p (a t) -> p a t", a=8), psT, eng="act")
            psZ = [k.ps("psZ%d" % i, [128, 512], F32) for i in range(2)]
            ta = k.sb("ta", [128, 512], F32)
            tb = k.sb("tb", [128, 512], F32)
            Yr = k.sb("Yr", [128, 4 * 512], BF16)
            Yi = k.sb("Yi", [128, 4 * 512], BF16)
            for kc in range(4):
                for r, tab in ((0, Ccs), (1, nScs)):
                    for t in range(2):
                        k.mm(psZ[r], tab[:, t * 512 + kc * 128:t * 512 + (kc + 1) * 128], ztm[:, t * 512:(t + 1) * 512],
                             start=(t == 0), stop=(t == 1))
                Kr = kfc[:, kc * 512:(kc + 1) * 512]
                Ki = kfc[:, 2048 + kc * 512:2048 + (kc + 1) * 512]
                k.tt(ta, psZ[0], Kr, ALU.mult)
                k.tt(tb, psZ[1], Ki, ALU.mult)
                k.tt(Yr[:, kc * 512:(kc + 1) * 512], ta, tb, ALU.subtract)
                k.tt(ta, psZ[0], Ki, ALU.mult)
                k.tt(tb, psZ[1], Kr, ALU.mult)
                k.tt(Yi[:, kc * 512:(kc + 1) * 512], ta, tb, ALU.add)
            yo = k.sb("yoc", [128, 4 * 256], F32)
            for cc in range(4):
                p_ = psZ[cc % 2]
                for kc in range(4):
                    k.mm(p_[:, 0:256], Yr[:, kc * 512 + cc * 128:kc * 512 + (cc + 1) * 128],
                         Ccs[:, kc * 512:kc * 512 + 256], start=(kc == 0), stop=False)
                    k.mm(p_[:, 0:256], Yi[:, kc * 512 + cc * 128:kc * 512 + (cc + 1) * 128],
                         nScs[:, kc * 512:kc * 512 + 256], start=False, stop=(kc == 3))
                k.cp(yo[:, cc * 256:(cc + 1) * 256], p_[:, 0:256], eng="act")
            for t in range(2):
                k.dma(yconv[NT + t].rr("(k p) n -> p k n", p=128),
                      yo.rr("p (k t n) -> p k t n", k=4, t=2)[:, :, t, :], q="pool", nowaw=True)

    def attn_finish(o_ps, npart_q, oS, sel_s, ps_bc, rec, yb, esk=None):
        nq = npart_q
        k.cp(oS[:, 0:nq], o_ps[:, 0:nq], eng="act")
        k.mm(ps_bc[0:64, 0:nq], sel_s, oS[:, 0:nq])
        if esk is not None:
            k.tt(rec[:, 0:nq].rr("p (h t) -> p h t", h=4), ps_bc[0:64, 0:nq].rr("p (h t) -> p h t", h=4),
                 esk.us(2).bc([64, 4, nq // 4]), ALU.add)
            k.recip(rec[:, 0:nq], rec[:, 0:nq])
        else:
            k.recip(rec[:, 0:nq], ps_bc[0:64, 0:nq])
        k.tt(yb[:, 0:nq], oS[0:64, 0:nq], rec[:, 0:nq], ALU.mult)

    def phase_gqa(l, bgjob=None):
        with k.phase():
            kTs = k.sb("kTs", [128, Ltot], BF16)
            k.dma(kTs, kT)
            vAs = k.sb("vAs", [128, NTt * 130], BF16)
            vAv = vAs.rr("p (t f) -> p t f", f=130)
            k.dma(vAv, vA.rr("(t p) f -> p t f", p=128))
            msk = k.sb("msk", [128, 2, 512], BF16)
            k.dma(msk, C["masks"])
            sel_s = k.sb("sel", [65, 64], F32)
            k.dma(sel_s, C["sel"])
            sk = k.sb("sk", [64, 8], F32)
            bcast_row_n(sk, W["gqa_sink"][l], 64)
            esk = k.sb("esk", [64, 8], F32)
            k.act(esk, sk, AF.Exp)
            qb = [k.sb("qb%d" % i, [128, 512], BF16) for i in range(2)]
            psS = [k.ps("psS%d" % i, [128, 512], F32) for i in range(3)]
            psO = [k.ps("psO%d" % i, [128, 512], F32) for i in range(2)]
            psB = k.ps("psB", [128, 512], F32)
            pT = [k.sb("pT%d" % i, [128, 512], BF16) for i in range(4)]
            oS = k.sb("oS", [65, 512], F32)
            rec = k.sb("rec", [64, 512], F32)
            yb = [k.sb("yb%d" % i, [64, 512], BF16) for i in range(2)]
            bgl = []
            if bgjob is not None:
                with k.record() as bgl:
                    bgjob()
            seqs = []
            for i in range(NTt):
                if i < NT:
                    keys = [(kt, (0 if kt == i - 1 else (1 if kt == i + 1 else None)))
                            for kt in (i - 1, i, i + 1) if 0 <= kt < NT]
                    keys += [(NT, None), (NT + 1, None)]
                else:
                    keys = [(NT, None), (NT + 1, None)]
                for g in range(2):
                    seqs.append((i, g, keys))
            items = [(si, j) for si, sq_ in enumerate(seqs) for j in range(len(sq_[2]))]
            loaded = set()

            def ensure_q(i):
                if i in loaded:
                    return
                loaded.add(i)
                cols = slice(i * 128, (i + 1) * 128)
                for g in range(2):
                    k.dma(qb[i % 2][g * 64:(g + 1) * 64, :].rr("d (h t) -> d h t", h=4),
                          qT[g * 256:(g + 1) * 256, :].rr("(h d) n -> d h n", d=64)[:, :, cols], nowaw=True)

            def S(t):
                si, j = items[t]
                i, g, keys = seqs[si]
                ensure_q(i)
                kt, mk = keys[j]
                pr = slice(g * 64, (g + 1) * 64)
                k.mm(psS[t % 3], kTs[pr, kt * 128:(kt + 1) * 128], qb[i % 2][pr, :])

            for t0 in range(min(3, len(items))):
                S(t0)
            for t, (si, j) in enumerate(items):
                i, g, keys = seqs[si]
                kt, mk = keys[j]
                p_ = pT[t % 4]
                k.act(p_, psS[t % 3], AF.Exp)
                if mk is not None:
                    k.tt(p_, p_, msk[:, mk, :], ALU.mult)
                if t + 3 < len(items):
                    S(t + 3)
                o_ = psO[si % 2]
                k.mm(o_[0:65, :], vAv[:, kt, g * 65:(g + 1) * 65], p_, start=(j == 0), stop=(j == len(keys) - 1))
                if bgl:
                    k.play(bgl, 1)
                if j == len(keys) - 1:
                    y_ = yb[si % 2]
                    cols = slice(i * 128, (i + 1) * 128)
                    attn_finish(o_[0:65, :], 512, oS, sel_s, psB, rec, y_, esk=esk[:, g * 4:(g + 1) * 4])
                    k.dma(ygT[g * 256:(g + 1) * 256, :].rr("(h d) n -> d h n", d=64)[:, :, cols],
                          y_.rr("d (h t) -> d h t", h=4), q="pool", nowaw=True)
            if bgl:
                k.play(bgl)

    def bcast_row_n(dst, src_row, n):
        k.dma(dst, src_row.bc([n, src_row.shape[1]]))

    def phase_mla(l, bgjob=None):
        with k.phase():
            sel_s = k.sb("sel", [65, 64], F32)
            k.dma(sel_s, C["sel"])
            vms = [k.sb("vms%d" % i, [128, NTt * 65], BF16) for i in range(2)]
            kms = [k.sb("kms0", [96, Ltot], BF16)]
            qms = [k.sb("qms0", [96, Ltot], BF16)]
            psS = [k.ps("psS%d" % i, [128, 512], F32) for i in range(3)]
            psO = [k.ps("psO%d" % i, [128, 512], F32) for i in range(2)]
            psB = k.ps("psB", [128, 512], F32)
            pT = [k.sb("pT%d" % i, [128, 512], BF16) for i in range(4)]
            oS = k.sb("oS", [65, 512], F32)
            rec = k.sb("rec", [64, 512], F32)
            yb = [k.sb("yb%d" % i, [64, 512], BF16) for i in range(2)]
            bgl = []
            if bgjob is not None:
                bgA = k.ps("bgA", [128, 512], F32)
                bgB = k.ps("bgB", [128, 512], F32)
                with k.record() as bgl:
                    bgjob((bgA, bgB))
            n = 0
            groups = [(g * 512, 512, list(range(NTt))) for g in range(L // 512)] + [(L, LC, [NT, NT + 1])]
            for h in range(8):
                km_ = kms[0]
                qm_ = qms[0]
                vm_ = vms[h % 2]
                vmv = vm_.rr("p (t f) -> p t f", f=65)
                k.dma(km_, kmT[h * 96:(h + 1) * 96, :])
                k.dma(qm_, qmT[h * 96:(h + 1) * 96, :])
                with nc.allow_non_contiguous_dma("per-head V gather"):
                    k.dma(vmv, vmA.rr("(t p) f -> p t f", p=128)[:, :, h * 65:(h + 1) * 65])
                for (q0, nq, keys) in groups:
                    o_ = psO[n % 2]
                    y_ = yb[n % 2]
                    n += 1
                    nk = len(keys)

                    def S(j):
                        kt = keys[j]
                        k.mm(psS[j % 3][:, 0:nq], km_[:, kt * 128:(kt + 1) * 128], qm_[:, q0:q0 + nq])

                    for j0 in range(min(2, nk)):
                        S(j0)
                    for j in range(nk):
                        kt = keys[j]
                        k.act(pT[j % 4][:, 0:nq], psS[j % 3][:, 0:nq], AF.Exp)
                        if j + 2 < nk:
                            S(j + 2)
                        k.mm(o_[0:65, 0:nq], vmv[:, kt, :], pT[j % 4][:, 0:nq], start=(j == 0),
                             stop=(j == nk - 1))
                        if bgl:
                            k.play(bgl, 1)
                    attn_finish(o_[0:65, :], nq, oS, sel_s, psB, rec, y_)
                    k.dma(ymT[h * 64:(h + 1) * 64, q0:q0 + nq], y_[:, 0:nq], q="pool", nowaw=True)
            if bgl:
                k.play(bgl)

    def phase_merge(l, xs, xd, only_latent):
        with k.phase():
            wbr = k.sb("wbr", [128, 12 * D], BF16)
            k.dma(wbr.rr("p (k n) -> p k n", k=12), W["w_branch"][l].rr("(k p) n -> p k n", p=128), q="pool")
            wo = k.sb("wo", [128, KD * D], BF16)
            k.dma(wo.rr("p (k n) -> p k n", k=KD), W["w_out"][l].rr("(k p) n -> p k n", p=128), q="pool")
            GT = [k.sb("GT%d" % s, [128, D], F32) for s in range(2)]
            for s in range(2):
                k.dma(GT[s], modv[l % 2, s, 2])
            skp = k.sb("skp", [128, 4], F32)
            with nc.allow_non_contiguous_dma("tiny column load"):
                k.dma(skp, W["hy_skip"][l].rr("o (k p) -> p (o k)", p=128))
            xt = [k.sb("xt%d" % i, [128, D], F32) for i in range(2)]
            yc = [k.sb("yc%d" % i, [128, 512], F32) for i in range(2)]
            zt = [k.sb("zt%d" % i, [128, 512], BF16) for i in range(2)]
            x0 = [k.sb("x0%d" % i, [128, 512], F32) for i in range(2)]
            yT3 = [k.sb("yT3_%d" % i, [128, 12 * 128], BF16) for i in range(2)]
            gt = [k.sb("gt%d" % i, [128, 3072], BF16) for i in range(2)]
            tmp = k.sb("tmpm", [128, 512], F32)
            mg = k.sb("mg", [128, D], F32)
            mg2 = k.sb("mg2", [128, D], F32)
            mgbs = [k.sb("mgb%d" % i, [128, D], BF16) for i in range(2)]
            mg3 = k.sb("mg3", [128, D], F32)
            mTs = k.sb("mTs", [128, KD * 128], BF16)
            psT = k.ps("psT", [128, KD, 128], BF16)
            psM = [k.ps("psM%d" % i, [128, 512], F32) for i in range(6)]
            xo = [k.sb("xo%d" % i, [128, D], F32) for i in range(2)]
            npm = [0]
            prevB = []
            ntl = NT if only_latent else NTt
            for i in range(ntl):
                s = seq_of(i)
                cols = slice(i * 128, (i + 1) * 128)
                b2 = i % 2
                mgb = mgbs[b2]
                k.dma(xt[b2], xs[i * 128:(i + 1) * 128, :])
                k.dma(yc[b2].rr("p (k t) -> p k t", k=4), yconv[i].rr("(k p) t -> p k t", p=128))
                k.dma(zt[b2].rr("p (k t) -> p k t", k=4), zT.rr("(k p) n -> p k n", p=128)[:, :, cols])
                k.dma(x0[b2].rr("p (k t) -> p k t", k=4), x0T.rr("(k p) n -> p k n", p=128)[:, :, cols])
                y3 = yT3[b2]
                y3v = y3.rr("p (k t) -> p k t", k=12)
                k.dma(y3v[:, 4:8, :], ygT.rr("(k p) n -> p k n", p=128)[:, :, cols], nowaw=True)
                k.dma(y3v[:, 8:12, :], ymT.rr("(k p) n -> p k n", p=128)[:, :, cols], nowaw=True)
                k.dma(gt[b2], gts[i * 128:(i + 1) * 128, :])
                recA = k.record()
                lA = recA.__enter__()
                for cc in range(4):
                    cs = slice(cc * 128, (cc + 1) * 128)
                    k.stt(tmp[:, cs], zt[b2][:, cs], skp[:, cc:cc + 1], yc[b2][:, cs], ALU.mult, ALU.add)
                k.tt(y3[:, 0:512], tmp, x0[b2], ALU.mult)
                for br in range(3):
                    for half in range(2):
                        p_ = psM[npm[0] % 4]
                        npm[0] += 1
                        hs = slice(half * 512, (half + 1) * 512)
                        for kk in range(4):
                            k.mm(p_, y3[:, (br * 4 + kk) * 128:(br * 4 + kk + 1) * 128],
                                 wbr[:, (br * 4 + kk) * D + half * 512:(br * 4 + kk) * D + (half + 1) * 512],
                                 start=(kk == 0), stop=(kk == 3))
                        gsl = gt[b2][:, br * D + half * 512:br * D + (half + 1) * 512]
                        if br == 0:
                            k.tt(mg[:, hs], p_, gsl, ALU.mult)
                        else:
                            k.tt(mg2[:, hs], p_, gsl, ALU.mult)
                            k.tt(mg[:, hs], mg[:, hs], mg2[:, hs], ALU.add)
                k.cp(mgb, mg, eng="act")
                recA.__exit__(None, None, None)
                k.play_interleaved([lA] + ([prevB] if prevB else []))
                recB = k.record()
                prevB = recB.__enter__()
                for kk in range(KD):
                    k.tr(psT[:, kk, :], mgb[:, kk * 128:(kk + 1) * 128], ident)
                k.cp(mTs.rr("p (k t) -> p k t", k=KD), psT, eng="act")
                for half in range(2):
                    p_ = psM[4 + half]
                    hs = slice(half * 512, (half + 1) * 512)
                    for kk in range(KD):
                        k.mm(p_, mTs[:, kk * 128:(kk + 1) * 128], wo[:, kk * D + half * 512:kk * D + (half + 1) * 512],
                             start=(kk == 0), stop=(kk == KD - 1))
                    k.tt(mg3[:, hs], p_, GT[s][:, hs], ALU.mult)
                    k.tt(xo[b2][:, hs], mg3[:, hs], xt[b2][:, hs], ALU.add)
                k.dma(xd[i * 128:(i + 1) * 128, :], xo[b2], q="pool", nowaw=True)
                recB.__exit__(None, None, None)
            if prevB:
                k.play(prevB)

    def phase_mlp(l, xs, xd, only_latent):
        with k.phase():
            w1 = k.sb("w1", [128, KD * 4096], BF16)
            w1v = w1.rr("p (k n) -> p k n", k=KD)
            for kk in range(KD):
                k.dma(w1v[:, kk, :], W["w_mlp1"][l][kk * 128:(kk + 1) * 128, :], q="pool", nowaw=True)
            w2 = k.sb("w2", [128, 32 * D], BF16)
            w2v = w2.rr("p (k n) -> p k n", k=32)
            for q4 in range(4):
                k.dma(w2v[:, q4 * 8:(q4 + 1) * 8, :],
                      W["w_mlp2"][l][q4 * 1024:(q4 + 1) * 1024, :].rr("(k p) n -> p k n", p=128), q="pool", nowaw=True)
            G2 = k.sb("G2", [128, D], F32)
            SH2 = k.sb("SH2", [128, D], F32)
            GT2 = k.sb("GT2", [128, D], F32)
            xt = [k.sb("xt%d" % i, [128, D], F32) for i in range(4)]
            junk = k.sb("junk", [128, D], F32)
            hb = k.sb("hb", [128, D], BF16)
            ss = k.sb("ss", [128, 1], F32)
            t1 = k.sb("t1", [128, 1], F32)
            rs = k.sb("rs", [128, 1], F32)
            psT = k.ps("psT", [128, KD, 128], BF16)
            hT = k.sb("hTm", [128, KD * 256], BF16)
            hTv = hT.rr("p (k t) -> p k t", k=KD)
            aT = k.sb("aT", [128, 32 * 256], BF16)
            r1 = [k.sb("r1_%d" % i, [128, 512], F32) for i in range(2)]
            psH = [k.ps("psH%d" % i, [128, 512], F32) for i in range(4)]
            psO = [k.ps("psO%d" % i, [128, 512], F32) for i in range(2)]
            hT2 = k.sb("hTm2", [128, KD * 256], BF16)
            hTvs = [hTv, hT2.rr("p (k t) -> p k t", k=KD)]
            ntl = NT if only_latent else NTt
            curn = [-1]
            curo = [-1]

            def normstage(i0):
                s = seq_of(i0)
                if s != curn[0]:
                    curn[0] = s
                    k.dma(G2, modv[l % 2, s, 4])
                    k.dma(SH2, modv[l % 2, s, 3])
                hv = hTvs[(i0 // 2) % 2]
                for t in range(2):
                    i = i0 + t
                    k.dma(xt[i % 4], xs[i * 128:(i + 1) * 128, :])
                    norm_mod_T(xt[i % 4], G2, SH2, hb, psT, hv[:, :, t * 128:(t + 1) * 128], junk, ss, t1, rs)

            def hidden(i0):
                hv = hTvs[(i0 // 2) % 2]
                for fb in range(16):
                    p_ = psH[fb % 4]
                    for r2 in range(2):
                        fc = fb * 2 + r2
                        for kk in range(KD):
                            k.mm(p_[:, r2 * 256:(r2 + 1) * 256], w1v[:, kk, fc * 128:(fc + 1) * 128],
                                 hv[:, kk, :], start=(kk == 0), stop=(kk == KD - 1))
                    r_ = r1[fb % 2]
                    k.act(r_, p_, AF.Relu)
                    k.tt(aT[:, fb * 512:(fb + 1) * 512], r_, r_, ALU.mult)

            def output(i0):
                s = seq_of(i0)
                if s != curo[0]:
                    curo[0] = s
                    k.dma(GT2, modv[l % 2, s, 5])
                for t in range(2):
                    i = i0 + t
                    x_ = xt[i % 4]
                    for half in range(2):
                        p_ = psO[half]
                        hs = slice(half * 512, (half + 1) * 512)
                        for fc in range(32):
                            k.mm(p_, aT[:, fc * 256 + t * 128:fc * 256 + (t + 1) * 128], w2v[:, fc, hs],
                                 start=(fc == 0), stop=(fc == 31))
                        k.tt(junk2[:, hs], p_, GT2[:, hs], ALU.mult)
                        k.tt(x_[:, hs], junk2[:, hs], x_[:, hs], ALU.add)
                    k.dma(xd[i * 128:(i + 1) * 128, :], x_, q="pool", nowaw=True)

            junk2 = k.sb("junk2", [128, D], F32)
            normstage(0)
            for i0 in range(0, ntl, 2):
                with k.record() as lh:
                    hidden(i0)
                lists = [lh]
                if i0 + 2 < ntl:
                    with k.record() as ln:
                        normstage(i0 + 2)
                    lists.append(ln)
                k.play_interleaved(lists)
                output(i0)

    for l in range(DEPTH):
        last = (l == DEPTH - 1)
        if l == 0:
            phase_mod(l)
        phase_proj(l, xsA)
        if l == 0:
            phase_filt(l)
        phase_hyena(l)
        if l + 1 < DEPTH:
            phase_gqa(l, bgjob=(lambda ll=l + 1: phase_mod(ll)))
        else:
            phase_gqa(l)
        if l + 1 < DEPTH:
            phase_mla(l, bgjob=(lambda bg, ll=l + 1: phase_filt(ll, bg)))
        else:
            phase_mla(l)
        phase_merge(l, xsA, xsB, last)
        phase_mlp(l, xsB, xsA, last)
    k.dma(y_out, xsA[0:L, :])
    k.barrier()
    k.es.close()
    return nc, consts, k


def make_in_maps(inputs, consts, L, DEPTH, B):
    maps = []
    shared = {}
    for nm, shp in W_SHAPES.items():
        a = np.asarray(inputs[nm], dtype=np.float32)
        shared[nm] = np.ascontiguousarray(a.reshape([DEPTH] + list(shp)))
    for nm, arr in consts.items():
        shared["k_" + nm] = np.ascontiguousarray(arr)
    x = np.asarray(inputs["x"], dtype=np.float32)
    c = np.asarray(inputs["c"], dtype=np.float32)
    ctx = np.asarray(inputs["ctx"], dtype=np.float32)
    cc = np.asarray(inputs["c_ctx"], dtype=np.float32)
    for b in range(B):
        m = dict(shared)
        m["x"] = np.ascontiguousarray(x[b])
        m["ctx"] = np.ascontiguousarray(ctx[b])
        m["c"] = np.ascontiguousarray(c[b].reshape(D, 1))
        m["c_ctx"] = np.ascontiguousarray(cc.reshape(D, 1))
        maps.append(m)
    return maps


def run(inputs, L, DEPTH, dbg=()):
    B = inputs["x"].shape[0]
    nc, consts, kb = build(L, DEPTH, dbg)
    maps = make_in_maps(inputs, consts, L, DEPTH, B)
    res = run_bass_kernel_spmd(nc, maps, core_ids=list(range(B)))
    return res


def kernel(**inputs):
    x = inputs["x"]
    B, L, _ = x.shape
    DEPTH = inputs["w_mod"].shape[0]
    res = run(inputs, L, DEPTH)
    return np.stack([np.asarray(r["y"], dtype=np.float32) for r in res.results], 0)
```
